# Optimizing a Trainium2 kernel written in Bass

```python
import jax
import jax.numpy as jnp
from jax import lax
import numpy as np

D_MODEL = 1024
BATCH = 8
SEQ = 4096
DEPTH = 2

CHUNK = 64
Q_BLOCK = 128
HEAD_DIM = 64
NEG_INF = -1e30
LN_EPS = 1e-5
RMS_EPS = 1e-6

MLA_HEADS = 8
MLA_Q_RANK = 384
MLA_KV_RANK = 256
MLA_NOPE = 64
MLA_ROPE = 32
MLA_V = 64
ROPE_THETA = 10000.0

SWA_HEADS = 8
SWA_KV_HEADS = 2
SWA_WINDOW = 128
SWA_LEFT_CHUNKS = SWA_WINDOW // CHUNK

FOX_HEADS = 8

CK_HEADS = 8
CK_LEFT_CHUNKS = 8
REL_MAX = 256
REL_TABLE = REL_MAX + CHUNK

D_FF = -(-8 * D_MODEL // (3 * 256)) * 256

DEEPNORM_ALPHA = (2 * DEPTH) ** 0.25
DEEPNORM_BETA = (8 * DEPTH) ** -0.25

AB_SPLITS = (MLA_Q_RANK, MLA_KV_RANK, MLA_ROPE, SWA_HEADS * HEAD_DIM, SWA_KV_HEADS * HEAD_DIM, SWA_KV_HEADS * HEAD_DIM)
CD_SPLITS = (FOX_HEADS * HEAD_DIM,) * 3 + (FOX_HEADS,) + (CK_HEADS * HEAD_DIM,) * 3
AB_MIX = MLA_HEADS * MLA_V + SWA_HEADS * HEAD_DIM
CD_MIX = (FOX_HEADS + CK_HEADS) * HEAD_DIM

kernel_name = 'hybrid_chunk_causal_mla_swa_fox_relpos_deepnorm'


def _split_cols(x, widths):
    return jnp.split(x, [int(i) for i in np.cumsum(widths)[:-1]], axis=-1)


def _layer_norm(x, g, b):
    xf = x.astype(jnp.float32)
    mu = jnp.mean(xf, -1, keepdims=True)
    var = jnp.mean(jnp.square(xf - mu), -1, keepdims=True)
    y = (xf - mu) * lax.rsqrt(var + LN_EPS) * g.astype(jnp.float32) + b.astype(jnp.float32)
    return y.astype(x.dtype)


def _rms_norm(x, g):
    xf = x.astype(jnp.float32)
    y = xf * lax.rsqrt(jnp.mean(xf * xf, -1, keepdims=True) + RMS_EPS)
    return (y * g.astype(jnp.float32)).astype(x.dtype)


def _rope_tables(S):
    inv = ROPE_THETA ** (-jnp.arange(0, MLA_ROPE, 2, dtype=jnp.float32) / MLA_ROPE)
    ang = jnp.arange(S, dtype=jnp.float32)[:, None] * inv[None, :]
    return jnp.cos(ang), jnp.sin(ang)


def _apply_rope(x, cos, sin):
    x1, x2 = jnp.split(x.astype(jnp.float32), 2, axis=-1)
    return jnp.concatenate([x1 * cos - x2 * sin, x1 * sin + x2 * cos], -1).astype(x.dtype)


def _alibi_slopes(n):
    return jnp.exp2(-8.0 * jnp.arange(1, n + 1, dtype=jnp.float32) / n)


def _sweep_query_blocks(block_fn, *q_parts):
    B, S = q_parts[0].shape[:2]
    nb = S // Q_BLOCK
    blocked = tuple(jnp.moveaxis(p.reshape(B, nb, Q_BLOCK, *p.shape[2:]), 1, 0) for p in q_parts)
    out = lax.map(lambda a: block_fn(a[0], *a[1]), (jnp.arange(nb), blocked))
    return jnp.moveaxis(out, 0, 1).reshape(B, S, *out.shape[3:])


def _band(x, n_left):
    B, S = x.shape[:2]
    nc = S // CHUNK
    pad = [(0, 0), (n_left * CHUNK, 0)] + [(0, 0)] * (x.ndim - 2)
    xp = jnp.pad(x, pad).reshape(B, nc + n_left, CHUNK, *x.shape[2:])
    return jnp.concatenate([xp[:, j:j + nc] for j in range(n_left + 1)], axis=2)


def _band_geometry(nc, n_left):
    band = jnp.arange((n_left + 1) * CHUNK)
    key_off = band // CHUNK - n_left
    dist = jnp.arange(CHUNK)[:, None] - (key_off * CHUNK + band % CHUNK)[None, :]
    valid = (jnp.arange(nc)[:, None] + key_off[None, :]) >= 0
    return dist, valid


def _mla_attention(q_nope, q_rope, k_nope, k_rope, v):
    S = k_nope.shape[1]
    scale = (MLA_NOPE + MLA_ROPE) ** -0.5
    key_chunk = jnp.arange(S) // CHUNK

    def block(i, qn, qr):
        q_chunk = (i * Q_BLOCK + jnp.arange(Q_BLOCK)) // CHUNK
        logits = (jnp.einsum('bqhd,bkhd->bhqk', qn, k_nope, preferred_element_type=jnp.float32)
                  + jnp.einsum('bqhr,bkr->bhqk', qr, k_rope, preferred_element_type=jnp.float32)) * scale
        logits = jnp.where(key_chunk[None, :] <= q_chunk[:, None], logits, NEG_INF)
        p = jax.nn.softmax(logits, axis=-1).astype(v.dtype)
        return jnp.einsum('bhqk,bkhd->bqhd', p, v)

    return _sweep_query_blocks(block, q_nope, q_rope)


def _swa_sink_attention(q, k, v, sinks):
    B, S, _, d = q.shape
    nc = S // CHUNK
    G = SWA_HEADS // SWA_KV_HEADS
    qc = q.reshape(B, nc, CHUNK, SWA_KV_HEADS, G, d)
    kb, vb = _band(k, SWA_LEFT_CHUNKS), _band(v, SWA_LEFT_CHUNKS)
    dist, valid = _band_geometry(nc, SWA_LEFT_CHUNKS)
    alibi = -_alibi_slopes(SWA_HEADS).reshape(SWA_KV_HEADS, G, 1, 1) * jnp.abs(dist).astype(jnp.float32)
    logits = jnp.einsum('bcqkgd,bcskd->bckgqs', qc, kb, preferred_element_type=jnp.float32) * d ** -0.5 + alibi
    logits = jnp.where(valid[None, :, None, None, None, :], logits, NEG_INF)
    sink = jnp.broadcast_to(sinks.astype(jnp.float32).reshape(SWA_KV_HEADS, G, 1, 1), logits.shape[:-1] + (1,))
    p = jax.nn.softmax(jnp.concatenate([logits, sink], -1), axis=-1)[..., :-1].astype(v.dtype)
    return jnp.einsum('bckgqs,bcskd->bcqkgd', p, vb).reshape(B, S, SWA_HEADS, d)


def _forgetting_attention(q, k, v, f_logit, b_forget):
    S = k.shape[1]
    scale = q.shape[-1] ** -0.5
    cum_log_f = jnp.cumsum(jax.nn.log_sigmoid(f_logit.astype(jnp.float32) + b_forget.astype(jnp.float32)), axis=1)
    cum_k = jnp.swapaxes(cum_log_f, 1, 2)[:, :, None, :]
    key_pos = jnp.arange(S)

    def block(i, qb, cum_q):
        q_pos = i * Q_BLOCK + jnp.arange(Q_BLOCK)
        decay = jnp.swapaxes(cum_q, 1, 2)[..., None] - cum_k
        logits = jnp.einsum('bqhd,bkhd->bhqk', qb, k, preferred_element_type=jnp.float32) * scale + decay
        logits = jnp.where(key_pos[None, :] <= q_pos[:, None], logits, NEG_INF)
        p = jax.nn.softmax(logits, axis=-1).astype(v.dtype)
        return jnp.einsum('bhqk,bkhd->bqhd', p, v)

    return _sweep_query_blocks(block, q, cum_log_f)


def _chunk_relpos_attention(q, k, v, rel_bias):
    B, S, H, d = q.shape
    nc = S // CHUNK
    qc = q.reshape(B, nc, CHUNK, H, d)
    kb, vb = _band(k, CK_LEFT_CHUNKS), _band(v, CK_LEFT_CHUNKS)
    dist, valid = _band_geometry(nc, CK_LEFT_CHUNKS)
    idx = jnp.clip(dist, -(CHUNK - 1), REL_MAX) + (CHUNK - 1)
    bias = jnp.moveaxis(rel_bias[idx], -1, 0).astype(jnp.float32)
    logits = jnp.einsum('bcqhd,bcshd->bchqs', qc, kb, preferred_element_type=jnp.float32) * d ** -0.5 + bias
    logits = jnp.where(valid[None, :, None, None, :], logits, NEG_INF)
    p = jax.nn.softmax(logits, axis=-1).astype(v.dtype)
    return jnp.einsum('bchqs,bcshd->bcqhd', p, vb).reshape(B, S, H, d)


def _mixer_ab(h, w_in, q_norm, w_uq, kv_norm, w_ukv, sinks, w_out, cos, sin):
    B, S, _ = h.shape
    proj = jnp.einsum('bsd,df->bsf', h, w_in)
    c_q, c_kv, k_r, q_s, k_s, v_s = _split_cols(proj, AB_SPLITS)
    q = jnp.einsum('bsr,rf->bsf', _rms_norm(c_q, q_norm), w_uq).reshape(B, S, MLA_HEADS, MLA_NOPE + MLA_ROPE)
    q_nope = q[..., :MLA_NOPE]
    q_rope = _apply_rope(q[..., MLA_NOPE:], cos[None, :, None], sin[None, :, None])
    kv = jnp.einsum('bsr,rf->bsf', _rms_norm(c_kv, kv_norm), w_ukv).reshape(B, S, MLA_HEADS, MLA_NOPE + MLA_V)
    k_rope = _apply_rope(k_r, cos[None], sin[None])
    o_a = _mla_attention(q_nope, q_rope, kv[..., :MLA_NOPE], k_rope, kv[..., MLA_NOPE:])
    o_b = _swa_sink_attention(q_s.reshape(B, S, SWA_HEADS, HEAD_DIM),
                              k_s.reshape(B, S, SWA_KV_HEADS, HEAD_DIM),
                              v_s.reshape(B, S, SWA_KV_HEADS, HEAD_DIM), sinks)
    o = jnp.concatenate([o_a.reshape(B, S, -1), o_b.reshape(B, S, -1)], -1)
    return jnp.einsum('bsf,fd->bsd', o, w_out)


def _mixer_cd(h, w_in, b_forget, rel_bias, w_out):
    B, S, _ = h.shape
    proj = jnp.einsum('bsd,df->bsf', h, w_in)
    q_f, k_f, v_f, f_logit, q_c, k_c, v_c = _split_cols(proj, CD_SPLITS)
    o_c = _forgetting_attention(q_f.reshape(B, S, FOX_HEADS, HEAD_DIM), k_f.reshape(B, S, FOX_HEADS, HEAD_DIM),
                                v_f.reshape(B, S, FOX_HEADS, HEAD_DIM), f_logit, b_forget)
    o_d = _chunk_relpos_attention(q_c.reshape(B, S, CK_HEADS, HEAD_DIM), k_c.reshape(B, S, CK_HEADS, HEAD_DIM),
                                  v_c.reshape(B, S, CK_HEADS, HEAD_DIM), rel_bias)
    o = jnp.concatenate([o_c.reshape(B, S, -1), o_d.reshape(B, S, -1)], -1)
    return jnp.einsum('bsf,fd->bsd', o, w_out)


def _swiglu(x, w_gate, w_up, w_down):
    g = jnp.einsum('bsd,df->bsf', x, w_gate)
    u = jnp.einsum('bsd,df->bsf', x, w_up)
    return jnp.einsum('bsf,fd->bsd', jax.nn.silu(g) * u, w_down)


def setup_inputs(seed: int = 0) -> dict:
    key = jax.random.key(seed)
    ks = jax.random.split(key, 20)
    ne, no = (DEPTH + 1) // 2, DEPTH // 2
    f32 = jnp.float32

    def normal(k, shape, scale):
        return jax.random.normal(k, shape, f32) * scale

    def gain(k, shape):
        return 1.0 + normal(k, shape, 0.02)

    return {
        'x': normal(ks[0], (BATCH, SEQ, D_MODEL), 1.0),
        'ab_w_in': normal(ks[1], (ne, D_MODEL, sum(AB_SPLITS)), D_MODEL ** -0.5),
        'ab_q_norm': gain(ks[2], (ne, MLA_Q_RANK)),
        'ab_w_uq': normal(ks[3], (ne, MLA_Q_RANK, MLA_HEADS * (MLA_NOPE + MLA_ROPE)), MLA_Q_RANK ** -0.5),
        'ab_kv_norm': gain(ks[4], (ne, MLA_KV_RANK)),
        'ab_w_ukv': normal(ks[5], (ne, MLA_KV_RANK, MLA_HEADS * (MLA_NOPE + MLA_V)), MLA_KV_RANK ** -0.5),
        'ab_sinks': normal(ks[6], (ne, SWA_HEADS), 0.5),
        'ab_w_out': normal(ks[7], (ne, AB_MIX, D_MODEL), DEEPNORM_BETA * AB_MIX ** -0.5),
        'cd_w_in': normal(ks[8], (no, D_MODEL, sum(CD_SPLITS)), D_MODEL ** -0.5),
        'cd_b_forget': jax.random.uniform(ks[9], (no, FOX_HEADS), dtype=f32, minval=1.0, maxval=5.0),
        'cd_rel_bias': normal(ks[10], (no, REL_TABLE, CK_HEADS), 0.2),
        'cd_w_out': normal(ks[11], (no, CD_MIX, D_MODEL), DEEPNORM_BETA * CD_MIX ** -0.5),
        'ln1_g': gain(ks[12], (DEPTH, D_MODEL)),
        'ln1_b': normal(ks[13], (DEPTH, D_MODEL), 0.02),
        'ffn_w_gate': normal(ks[14], (DEPTH, D_MODEL, D_FF), D_MODEL ** -0.5),
        'ffn_w_up': normal(ks[15], (DEPTH, D_MODEL, D_FF), D_MODEL ** -0.5),
        'ffn_w_down': normal(ks[16], (DEPTH, D_FF, D_MODEL), DEEPNORM_BETA * D_FF ** -0.5),
        'ln2_g': gain(ks[17], (DEPTH, D_MODEL)),
        'ln2_b': normal(ks[18], (DEPTH, D_MODEL), 0.02),
    }


def reference(x, ab_w_in, ab_q_norm, ab_w_uq, ab_kv_norm, ab_w_ukv, ab_sinks, ab_w_out,
              cd_w_in, cd_b_forget, cd_rel_bias, cd_w_out,
              ln1_g, ln1_b, ffn_w_gate, ffn_w_up, ffn_w_down, ln2_g, ln2_b):
    cos, sin = _rope_tables(x.shape[1])
    for layer in range(DEPTH):
        i = layer // 2
        if layer % 2 == 0:
            mix = _mixer_ab(x, ab_w_in[i], ab_q_norm[i], ab_w_uq[i], ab_kv_norm[i], ab_w_ukv[i],
                            ab_sinks[i], ab_w_out[i], cos, sin)
        else:
            mix = _mixer_cd(x, cd_w_in[i], cd_b_forget[i], cd_rel_bias[i], cd_w_out[i])
        x = _layer_norm(DEEPNORM_ALPHA * x + mix, ln1_g[layer], ln1_b[layer])
        ffn = _swiglu(x, ffn_w_gate[layer], ffn_w_up[layer], ffn_w_down[layer])
        x = _layer_norm(DEEPNORM_ALPHA * x + ffn, ln2_g[layer], ln2_b[layer])
    return x
```

```python
import numpy as np
import concourse.bass as bass
import concourse.mybir as mybir
from concourse.bass_utils import run_bass_kernel_spmd
from contextlib import ExitStack

F32 = mybir.dt.float32
BF16 = mybir.dt.bfloat16
AF = mybir.ActivationFunctionType
ALU = mybir.AluOpType
AX = mybir.AxisListType

COMPUTE = ("pe", "act", "dve", "pool")
QUEUES = ("sp", "poolq")
NDMASEM = 12


class _Rec:
    def __init__(self, name):
        self.name = name

    def __call__(self, *a, **k):
        name = self.name
        return lambda e: getattr(e, name)(*a, **k)


class _Recorder:
    def __getattr__(self, name):
        return _Rec(name)


X = _Recorder()


class Buf:
    __slots__ = ("name", "w", "r", "rd", "excl")

    def __init__(self, name="", excl=False):
        self.name = name
        self.excl = excl
        self.w = None
        self.r = {}
        self.rd = []


class Op:
    __slots__ = ("eng", "fn", "deps", "sig", "sem", "val", "ndma", "isdma", "prev")


class Prog:
    def __init__(self, nc):
        self.nc = nc
        self.streams = {"pe": [], "act": [], "dve": [], "pool": [], "sp": []}
        self.barrier_deps = []
        self.since_barrier_dma = []
        self.last = {}
        self.wgroups = {}

    def _stream_of(self, eng):
        return "pool" if eng == "poolq" else eng

    def op(self, eng, fn, reads=(), writes=(), ndma=0):
        o = Op()
        o.eng = eng
        o.fn = fn
        o.ndma = ndma
        o.isdma = ndma > 0
        o.sig = False
        o.sem = None
        o.val = 0
        st = self._stream_of(eng)
        deps = []
        xr = [b for b in reads if b.excl]
        if xr:
            reads = [b for b in reads if not b.excl]
            writes = list(writes) + xr
        for b in reads:
            if b.w is not None:
                deps.append(b.w)
            deps.extend(self.wgroups.get(id(b), ()))
        for b in writes:
            if b.w is not None:
                deps.append(b.w)
            if id(b) in self.wgroups:
                deps.extend(self.wgroups.pop(id(b)))
            deps.extend(b.r.values())
            deps.extend(b.rd)
        deps.extend(self.barrier_deps)
        if eng == "pe":
            deps = [d for d in deps if d.isdma or d.eng != "pe"]
        seen = set()
        dd = []
        for d in deps:
            if id(d) not in seen and d is not o:
                seen.add(id(d))
                dd.append(d)
                d.sig = True
        o.deps = dd
        for b in reads:
            if o.isdma:
                b.rd.append(o)
            else:
                b.r[eng] = o
        for b in writes:
            b.w = o
            b.r = {}
            b.rd = []
        self.streams[st].append(o)
        if o.isdma:
            self.since_barrier_dma.append(o)
            o.sig = True
        else:
            self.last[eng] = o
        return o

    def barrier(self):
        deps = list(self.last.values()) + list(self.since_barrier_dma)
        for d in deps:
            d.sig = True
        self.barrier_deps = deps
        self.since_barrier_dma = []

    def emit(self, final_wait_eng="sp"):
        nc = self.nc
        for d in list(self.last.values()):
            d.sig = True
        with ExitStack() as es:
            sems = {}
            for e in COMPUTE:
                sems[e] = es.enter_context(nc.semaphore("s_" + e))
            dsems = {}
            for q in QUEUES:
                dsems[q] = [es.enter_context(nc.semaphore("d_%s%d" % (q, i))) for i in range(NDMASEM)]
            for st, ops in self.streams.items():
                cnt = 0
                qcnt = {q: 0 for q in QUEUES}
                qval = {q: [0] * NDMASEM for q in QUEUES}
                for o in ops:
                    if o.isdma:
                        k = qcnt[o.eng] % NDMASEM
                        qcnt[o.eng] += 1
                        o.sem = dsems[o.eng][k]
                        o.prev = qval[o.eng][k]
                        qval[o.eng][k] += 16 * o.ndma
                        o.val = qval[o.eng][k]
                    elif o.sig:
                        cnt += 1
                        o.sem = sems[o.eng]
                        o.val = cnt
            tail = list(self.barrier_deps) + list(self.last.values()) + list(self.since_barrier_dma)
            for d in tail:
                assert d.sig or d.isdma
            self.maxval = 0
            es.enter_context(nc.allow_non_contiguous_dma(reason='small strided param loads'))
            block = es.enter_context(nc.Block())
            handles = {"pe": "tensor", "act": "scalar", "dve": "vector", "pool": "gpsimd", "sp": "sync"}

            def run_stream(st, eng):
                waited = {}

                def wait(sem, val):
                    if val <= 0:
                        return
                    key = id(sem)
                    if waited.get(key, 0) >= val:
                        return
                    waited[key] = val
                    self.maxval = max(self.maxval, val)
                    eng.wait_ge(sem, val)

                for o in self.streams[st]:
                    for d in o.deps:
                        wait(d.sem, d.val)
                    if o.isdma:
                        wait(o.sem, o.prev)
                        ins = o.fn(eng)
                        assert len(ins) == o.ndma, (len(ins), o.ndma)
                        for i in ins:
                            i.then_inc(o.sem, 16)
                    else:
                        i = o.fn(eng)
                        if o.sig:
                            i.then_inc(o.sem, 1)
                if st == final_wait_eng:
                    for d in tail:
                        wait(d.sem, d.val)

            for st in self.streams:
                getattr(block, handles[st])(lambda eng, st=st: run_stream(st, eng))


class Arena:
    def __init__(self, nc, es, nbytes, name="arena"):
        self.t = es.enter_context(nc.sbuf_tensor(name, [128, nbytes // 4], F32))
        self.nbytes = nbytes
        self.off = 0
        self.marks = []

    def alloc(self, free_shape, dtype, parts=128, pbase=0):
        esz = 2 if dtype == BF16 else 4
        n = int(np.prod(free_shape))
        nb = (n * esz + 31) // 32 * 32
        assert self.off + nb <= self.nbytes, "SBUF arena overflow: %d + %d > %d" % (self.off, nb, self.nbytes)
        w0 = self.off // 4
        ap = self.t[:, w0:w0 + nb // 4]
        if dtype != F32:
            ap = ap.bitcast(dtype)
        ap = ap[:, 0:n]
        self.off += nb
        if len(free_shape) == 2:
            ap = ap.rearrange("p (a b) -> p a b", b=free_shape[1])
        elif len(free_shape) == 3:
            ap = ap.rearrange("p (a b c) -> p a b c", b=free_shape[1], c=free_shape[2])
        return ap

    def mark(self):
        self.marks.append(self.off)

    def release(self):
        self.off = self.marks.pop()


S = 4096
D = 1024
NT = 32
NB = 8
DFF = 2816
NF = 22
ALPHA = 4.0 ** 0.25
LN_EPS = 1e-5
RMS_EPS = 1e-6
NEG = -30000.0
import os as _os
DBGSTOP = int(_os.environ.get('DBGSTOP', '0'))


def host_constants():
    c = {}
    inv = (10000.0 ** (-np.arange(0, 32, 2, dtype=np.float32) / 32)).astype(np.float32)
    ang = np.arange(S, dtype=np.float32)[:, None] * inv[None, :]
    cos, sin = np.cos(ang).astype(np.float32), np.sin(ang).astype(np.float32)
    cs = np.zeros((2, 32, S), np.float32)
    cs[0, 0:16] = cos.T
    cs[0, 16:32] = cos.T
    cs[1, 0:16] = sin.T
    cs[1, 16:32] = sin.T
    c["rope_cs"] = cs
    k = np.arange(128)[:, None]
    q = np.arange(128)[None, :]
    c["mla_mask"] = np.where((k >= 64) & (q < 64), NEG, 0.0).astype(np.float32)
    c["causal_mask"] = np.where(k > q, NEG, 0.0).astype(np.float32)

    def band_mask(nof, L):
        m = np.zeros((128, nof, 128), np.float32)
        for o in range(nof):
            dc = 2 * o + (q >= 64).astype(np.int32) - (k >= 64).astype(np.int32)
            m[:, o, :] = np.where((dc >= 0) & (dc <= L), 0.0, NEG)
        return m
    slopes = np.exp2(-8.0 * np.arange(1, 9, dtype=np.float32) / 8).astype(np.float32)
    al = np.zeros((128, 8, 2, 128), np.float32)
    bm = band_mask(2, 2)
    for h in range(8):
        for o in range(2):
            dist = 128 * o + q - k
            al[:, h, o, :] = -slopes[h] * np.abs(dist).astype(np.float32) + bm[:, o, :]
    c["alibi"] = al
    c["ck_mask"] = band_mask(5, 8)
    idx = np.zeros((128, 5, 128), np.int64)
    for o in range(5):
        idx[:, o, :] = np.clip(128 * o + q - k, -63, 256) + 63
    c["_ck_idx"] = idx
    return c


def build_nc(dbg=False, nphase=99):
    nc = bass.Bass("TRN2", target_bir_lowering=False)

    def din(name, shape, dt=F32):
        return nc.dram_tensor(name, list(shape), dt, kind="ExternalInput").ap()
    skind = "ExternalOutput" if dbg else "Internal"

    def dsc(name, shape, dt):
        return nc.dram_tensor(name, list(shape), dt, kind=skind).ap()

    x_in = din("x", [S, D])
    ab_w_in = din("ab_w_in", [D, 1440]); ab_q_norm = din("ab_q_norm", [384]); ab_w_uq = din("ab_w_uq", [384, 768])
    ab_kv_norm = din("ab_kv_norm", [256]); ab_w_ukv = din("ab_w_ukv", [256, 1024]); ab_sinks = din("ab_sinks", [8])
    ab_w_out = din("ab_w_out", [D, D]); cd_w_in = din("cd_w_in", [D, 3080]); cd_b_forget = din("cd_b_forget", [8])
    cd_w_out = din("cd_w_out", [D, D])
    ln1_g = din("ln1_g", [2, D]); ln1_b = din("ln1_b", [2, D]); ln2_g = din("ln2_g", [2, D]); ln2_b = din("ln2_b", [2, D])
    w_gate = din("ffn_w_gate", [2, D, DFF]); w_up = din("ffn_w_up", [2, D, DFF]); w_down = din("ffn_w_down", [2, DFF, D])
    rope_cs = din("rope_cs", [2, 32, S]); mla_mask_d = din("mla_mask", [128, 128]); causal_mask_d = din("causal_mask", [128, 128])
    alibi_d = din("alibi", [128, 8, 2, 128]); ck_mask_d = din("ck_mask", [128, 5, 128]); ck_bias_d = din("ck_bias", [128, 8, 5, 128])
    y_out = nc.dram_tensor("y", [S, D], F32, kind="ExternalOutput").ap()

    QTa = dsc("QTa", [8, 96, S], BF16); KTa = dsc("KTa", [8, 96, S], BF16); Va = dsc("Va", [S, 8, 64], BF16)
    QTs = dsc("QTs", [8, 64, S], BF16); KTs = dsc("KTs", [2, 64, S], BF16); Vs = dsc("Vs", [S, 2, 64], BF16)
    QTf = dsc("QTf", [8, 70, S], BF16); KTf = dsc("KTf", [8, 70, S], BF16); Vf = dsc("Vf", [S, 8, 64], BF16)
    QTc = dsc("QTc", [8, 64, S], BF16); KTc = dsc("KTc", [8, 64, S], BF16); Vc = dsc("Vc", [S, 8, 64], BF16)
    OT = dsc("OT", [D, S], BF16)
    HT = dsc("HT", [DFF, S], BF16)
    X1 = dsc("X1", [S, D], F32); X2 = dsc("X2", [S, D], F32); X3 = dsc("X3", [S, D], F32)

    P = Prog(nc)
    es = ExitStack()
    A = Arena(nc, es, 176 * 1024)
    PP = [es.enter_context(nc.psum_tensor("pp%d" % i, [128, 1024], F32))[:, :] for i in range(4)]
    PS = []
    for i in range(4):
        PS += [PP[i][:, 0:512], PP[i][:, 512:1024]]

    def dma(q, out, in_, reads=(), writes=()):
        return P.op(q, lambda e: [e.dma_start(out=out, in_=in_)], reads, writes, ndma=1)

    def evac(eng, out, in_, reads, writes, scale=None):
        if eng == "act":
            if scale is None:
                return P.op("act", X.copy(out=out, in_=in_), reads, writes)
            return P.op("act", X.mul(out=out, in_=in_, mul=float(scale)), reads, writes)
        if scale is None:
            return P.op(eng, X.tensor_copy(out=out, in_=in_), reads, writes)
        return P.op(eng, X.tensor_scalar(out=out, in0=in_, scalar1=float(scale), scalar2=None, op0=ALU.mult), reads, writes)

    ident = A.alloc([128], F32); b_ident = Buf()
    P.op("pool", X.memset(ident, 1.0), writes=[b_ident])
    P.op("pool", X.affine_select(out=ident, in_=ident, pattern=[[-1, 128]], compare_op=ALU.is_equal, fill=0.0, base=0, channel_multiplier=1), reads=[b_ident], writes=[b_ident])
    ident_bf = A.alloc([128], BF16)
    P.op("dve", X.tensor_copy(out=ident_bf, in_=ident), reads=[b_ident], writes=[b_ident])
    ones_bf = A.alloc([128], BF16); b_ones = Buf()
    P.op("pool", X.memset(ones_bf, 1.0), writes=[b_ones])
    lng = A.alloc([4, D], F32); b_lng = Buf()
    eps_ln = A.alloc([1], F32); eps_rms = A.alloc([1], F32); b_eps = Buf()
    P.op("pool", X.memset(eps_ln, LN_EPS), writes=[b_eps])
    P.op("pool", X.memset(eps_rms, RMS_EPS), writes=[b_eps])

    def psbufs(n):
        return [Buf() for _ in range(n)]

    def pbufs(n):
        return [Buf(excl=True) for _ in range(n)]

    def load_x(blk, Xsrc, xt, b_xt):
        dma("sp", xt, Xsrc[blk * 512:(blk + 1) * 512, :].rearrange("(t p) d -> p t d", p=128), writes=[b_xt])

    def make_xT(xt, b_xt, xT, b_xT, pst, b_pst, cnt):
        for c in range(8):
            j = cnt[0] % len(pst); cnt[0] += 1
            for t in range(4):
                P.op("pe", X.transpose(out=pst[j][:, t * 128:(t + 1) * 128], in_=xt[:, t, c * 128:(c + 1) * 128], identity=ident),
                     reads=[b_xt, b_ident], writes=[b_pst[j]])
            evac("act" if c % 2 == 0 else "dve", xT[:, c, :], pst[j], [b_pst[j]], [b_xT])

    def ln_stage1(z, b_z, gi, tmp):
        st, mv, sc, b_st = tmp
        for hh in range(2):
            P.op("dve", X.bn_stats(out=st[:, hh, :], in_=z[:, hh * 512:(hh + 1) * 512]), reads=[b_z], writes=[b_st])
        P.op("dve", X.bn_aggr(out=mv, in_=st), reads=[b_st], writes=[b_st])
        P.op("act", X.activation(out=sc[:, 0:1], in_=mv[:, 1:2], func=AF.Ln, bias=eps_ln[:, 0:1]), reads=[b_st, b_eps], writes=[b_st])
        P.op("act", X.activation(out=sc[:, 0:1], in_=sc[:, 0:1], func=AF.Exp, scale=-0.5), reads=[b_st], writes=[b_st])

    def ln_stage2(z, b_z, gi, tmp):
        st, mv, sc, b_st = tmp
        P.op("dve", X.scalar_tensor_tensor(out=sc[:, 1:2], in0=mv[:, 0:1], scalar=-1.0, in1=sc[:, 0:1], op0=ALU.mult, op1=ALU.mult), reads=[b_st], writes=[b_st])
        P.op("act", X.activation(out=z, in_=z, func=AF.Identity, scale=sc[:, 0:1], bias=sc[:, 1:2]), reads=[b_z, b_st], writes=[b_z])

    def ln_stage3(z, b_z, gi, Xdst, rows):
        P.op("dve", X.tensor_tensor(out=z[:, 0:512], in0=z[:, 0:512], in1=lng[:, gi, 0:512], op=ALU.mult), reads=[b_z, b_lng], writes=[b_z])
        P.op("pool", X.tensor_tensor(out=z[:, 512:1024], in0=z[:, 512:1024], in1=lng[:, gi, 512:1024], op=ALU.mult), reads=[b_z, b_lng], writes=[b_z])
        P.op("pool", X.tensor_tensor(out=z, in0=z, in1=lng[:, gi + 1, :], op=ALU.add), reads=[b_z, b_lng], writes=[b_z])
        dma("sp", Xdst[rows, :], z, reads=[b_z])

    def load_ln_params(g_d, b_d, layer, gi):
        dma("sp", lng[:, gi, :], g_d[layer:layer + 1, :].partition_broadcast(128) if False else g_d[layer:layer + 1, :].broadcast_to([128, D]), writes=[b_lng])
        dma("sp", lng[:, gi + 1, :], b_d[layer:layer + 1, :].broadcast_to([128, D]), writes=[b_lng])

    def load_w_cast(dst, src_rows_by_cols, ncols, b):
        C = dst.shape[1]
        step = 1024
        pairs = []
        for c in range(C):
            for c0 in range(0, ncols, step):
                c1 = min(ncols, c0 + step)
                pairs.append((dst[:, c, c0:c1], src_rows_by_cols[c * 128:(c + 1) * 128, c0:c1]))
        GR = 8
        for g0 in range(0, len(pairs), GR):
            grp = pairs[g0:g0 + GR]
            P.op("poolq", (lambda grp: (lambda e: [e.dma_start(out=o, in_=i) for (o, i) in grp]))(grp), writes=[Buf()], ndma=len(grp))
        P.wgroups.setdefault(id(b), []).extend(P.streams["pool"][-((len(pairs) + GR - 1) // GR):])
        b.w = None

    def load_w_colblocks(dsts_srcs, ncols, cb=512):
        out = [[] for _ in dsts_srcs]
        for c0 in range(0, ncols, cb):
            c1 = min(ncols, c0 + cb)
            for wi, (dst, src) in enumerate(dsts_srcs):
                C = dst.shape[1]
                grp = [(dst[:, c, c0:c1], src[c * 128:(c + 1) * 128, c0:c1]) for c in range(C)]
                b = Buf()
                P.op("poolq", (lambda grp: (lambda e: [e.dma_start(out=o, in_=i) for (o, i) in grp]))(grp), writes=[b], ndma=len(grp))
                out[wi].append((c1, b))
        return out

    def colbuf(blist, c_hi):
        for (c1, b) in blist:
            if c_hi <= c1:
                return b
        raise AssertionError

    def attention_full(QT_d, KT_d, V_d, R, mask_d, orow0):
        A.mark()
        mask32 = A.alloc([128], F32); mask = A.alloc([128], BF16); b_mask = Buf()
        dma("sp", mask32, mask_d, writes=[b_mask])
        P.op("dve", X.tensor_copy(out=mask, in_=mask32), reads=[b_mask], writes=[b_mask])
        sets = []
        for i in range(2):
            qt = A.alloc([S], BF16); kt_ = A.alloc([S], BF16); va = A.alloc([NT, 128], BF16)
            bq, bk, bv = Buf(), Buf(), Buf()
            P.op("pool", X.memset(va[:, :, 64:128], 1.0), writes=[bv])
            sets.append((qt, kt_, va, bq, bk, bv))
        NPB = 3
        PTs = [A.alloc([1024], BF16) for _ in range(NPB)]; b_PT = psbufs(NPB)
        Rc = [A.alloc([512], F32) for _ in range(2)]; b_Rc = psbufs(2)
        On = [A.alloc([512], BF16) for _ in range(2)]; b_On = psbufs(2)
        b_S = pbufs(NPB); b_O = pbufs(2)
        Sp = PP[0:3]; Ob = [PP[3][:, 0:512], PP[3][:, 512:1024]]

        def load_head(h):
            qt, kt_, va, bq, bk, bv = sets[h % 2]
            dma("sp", qt[0:R, :], QT_d[h], writes=[bq])
            dma("sp", kt_[0:R, :], KT_d[h], writes=[bk])
            dma("sp", va[:, :, 0:64], V_d[:, h, :].rearrange("(k p) e -> p k e", p=128), writes=[bv])
        load_head(0)
        gi = [0]
        for h in range(8):
            if h + 1 < 8:
                load_head(h + 1)
            qt, kt_, va, bq, bk, bv = sets[h % 2]
            pairs = [(Q, kp) for Q in range(NB) for kp in range(2 * Q + 2)]
            base = gi[0]

            def col0(Q, kt):
                return 0 if kt < 4 * Q else 128 * (kt - 4 * Q)

            def qk(i):
                Q, kp = pairs[i]; j = (base + i) % NPB
                for u in range(2):
                    kt = 2 * kp + u; n0 = col0(Q, kt); diag = kt >= 4 * Q
                    Sv = Sp[j][:, u * 512:(u + 1) * 512]
                    if not diag:
                        P.op("pe", X.matmul(Sv[:, 0:512], lhsT=kt_[0:R, kt * 128:(kt + 1) * 128], rhs=qt[0:R, Q * 512:(Q + 1) * 512], start=True, stop=True),
                             reads=[bq, bk], writes=[b_S[j]])
                    else:
                        if n0 + 128 < 512:
                            P.op("pe", X.matmul(Sv[:, n0 + 128:512], lhsT=kt_[0:R, kt * 128:(kt + 1) * 128], rhs=qt[0:R, Q * 512 + n0 + 128:(Q + 1) * 512], start=True, stop=True),
                                 reads=[bq, bk], writes=[b_S[j]])
                        P.op("pe", X.matmul(Sv[:, n0:n0 + 128], lhsT=kt_[0:R, kt * 128:(kt + 1) * 128], rhs=qt[0:R, Q * 512 + n0:Q * 512 + n0 + 128], start=True, stop=False),
                             reads=[bq, bk], writes=[b_S[j]])
                        P.op("pe", X.matmul(Sv[:, n0:n0 + 128], lhsT=ident_bf, rhs=mask, start=False, stop=True),
                             reads=[b_ident, b_mask], writes=[b_S[j]])

            def ex(i):
                Q, kp = pairs[i]; j = (base + i) % NPB
                if 2 * kp + 1 < 4 * Q:
                    P.op("act", X.activation(out=PTs[j], in_=Sp[j], func=AF.Exp), reads=[b_S[j]], writes=[b_PT[j]])
                else:
                    for u in range(2):
                        n0 = col0(Q, 2 * kp + u)
                        P.op("act", X.activation(out=PTs[j][:, u * 512 + n0:(u + 1) * 512], in_=Sp[j][:, u * 512 + n0:(u + 1) * 512], func=AF.Exp), reads=[b_S[j]], writes=[b_PT[j]])

            def pv(i):
                Q, kp = pairs[i]; j = (base + i) % NPB; ob = Q % 2
                for u in range(2):
                    kt = 2 * kp + u; n0 = col0(Q, kt)
                    P.op("pe", X.matmul(Ob[ob][:, n0:512], lhsT=va[:, kt, :], rhs=PTs[j][:, u * 512 + n0:(u + 1) * 512], start=(kt == 0), stop=(kt == 4 * Q + 3)),
                         reads=[bv, b_PT[j]], writes=[b_O[ob]])

            def norm(Q):
                ob = Q % 2
                P.op("dve", X.reciprocal(out=Rc[ob][0:64, :], in_=Ob[ob][64:128, :]), reads=[b_O[ob]], writes=[b_Rc[ob]])
                P.op("dve", X.tensor_tensor(out=On[ob][0:64, :], in0=Ob[ob][0:64, :], in1=Rc[ob][0:64, :], op=ALU.mult), reads=[b_O[ob], b_Rc[ob]], writes=[b_On[ob]])
                dma("sp", OT[orow0 + h * 64:orow0 + (h + 1) * 64, Q * 512:(Q + 1) * 512], On[ob][0:64, :], reads=[b_On[ob]])
            n = len(pairs)
            qk(0)
            if n > 1:
                qk(1)
            for i in range(n):
                if i + 2 < n:
                    qk(i + 2)
                ex(i)
                pv(i)
                Q, kp = pairs[i]
                if kp == 2 * Q + 1:
                    norm(Q)
            gi[0] += n
        A.release()
        P.barrier()

    def attention_band(QT_d, KT_d, V_d, nkv, omax, bhi, b_bias, orow0, sink=None, b_sink=None):
        A.mark()
        G = 8 // nkv
        sets = []
        for i in range(2):
            qt = A.alloc([S], BF16); bq = Buf()
            sets.append((qt, bq))
        ksets = []
        for i in range(2):
            kt_ = A.alloc([S], BF16); va = A.alloc([NT, 128], BF16); bk, bv = Buf(), Buf()
            P.op("pool", X.memset(va[:, :, 64:128], 1.0), writes=[bv])
            ksets.append((kt_, va, bk, bv))
        NSB = 6; LA = 4
        PTs = [A.alloc([512], BF16) for _ in range(NSB)]; b_PT = psbufs(NSB)
        Rc = [A.alloc([512], F32) for _ in range(2)]; b_Rc = psbufs(2)
        On = [A.alloc([512], BF16) for _ in range(2)]; b_On = psbufs(2)
        b_S = pbufs(NSB); b_O = pbufs(2)
        Sb = PS[0:NSB]; Ob = PS[6:8]

        def load_q(h):
            qt, bq = sets[h % 2]
            dma("sp", qt[0:64, :], QT_d[h], writes=[bq])

        def load_kv(g):
            kt_, va, bk, bv = ksets[g % 2]
            dma("sp", kt_[0:64, :], KT_d[g], writes=[bk])
            dma("sp", va[:, :, 0:64], V_d[:, g, :].rearrange("(k p) e -> p k e", p=128), writes=[bv])
        load_q(0); load_kv(0)
        gi = [0]
        for h in range(8):
            g = h // G
            if h + 1 < 8:
                load_q(h + 1)
                if (h + 1) // G != g:
                    load_kv((h + 1) // G)
            qt, bq = sets[h % 2]
            kt_, va, bk, bv = ksets[g % 2]
            steps = []
            for Q in range(NB):
                kts = list(range(max(0, 4 * Q - omax), 4 * Q + 4))
                for kt in kts:
                    m0 = max(kt, 4 * Q); m1 = min(kt + omax, 4 * Q + 3)
                    steps.append((Q, kt, m0, m1, kt == kts[0], kt == kts[-1]))
            base = gi[0]

            def qk(i):
                Q, kt, m0, m1, first, last = steps[i]; j = (base + i) % NSB
                c0 = (m0 - 4 * Q) * 128; c1 = (m1 - 4 * Q + 1) * 128
                o0 = m0 - kt; o1 = m1 - kt
                P.op("pe", X.matmul(Sb[j][:, c0:c1], lhsT=kt_[0:64, kt * 128:(kt + 1) * 128], rhs=qt[0:64, Q * 512 + c0:Q * 512 + c1], start=True, stop=False),
                     reads=[bq, bk], writes=[b_S[j]])
                P.op("pe", X.matmul(Sb[j][:, c0:c1].rearrange("p (o q) -> p o q", q=128), lhsT=ident_bf, rhs=bhi[:, h, o0:o1 + 1, :], start=False, stop=True),
                     reads=[b_ident, b_bias], writes=[b_S[j]])

            def ex(i):
                Q, kt, m0, m1, first, last = steps[i]; j = (base + i) % NSB
                c0 = (m0 - 4 * Q) * 128; c1 = (m1 - 4 * Q + 1) * 128
                P.op("act", X.activation(out=PTs[j][:, c0:c1], in_=Sb[j][:, c0:c1], func=AF.Exp), reads=[b_S[j]], writes=[b_PT[j]])

            def pv(i):
                Q, kt, m0, m1, first, last = steps[i]; j = (base + i) % NSB; ob = Q % 2
                c0 = (m0 - 4 * Q) * 128; c1 = (m1 - 4 * Q + 1) * 128
                P.op("pe", X.matmul(Ob[ob][:, c0:c1], lhsT=va[:, kt, :], rhs=PTs[j][:, c0:c1], start=first, stop=last, skip_group_check=True),
                     reads=[bv, b_PT[j]], writes=[b_O[ob]])

            def norm(Q):
                ob = Q % 2
                if sink is not None:
                    P.op("dve", X.tensor_scalar(out=Rc[ob][0:64, :], in0=Ob[ob][64:128, :], scalar1=sink[64:128, h:h + 1], scalar2=None, op0=ALU.add), reads=[b_O[ob], b_sink], writes=[b_Rc[ob]])
                    P.op("act", X.activation(out=Rc[ob][0:64, :], in_=Rc[ob][0:64, :], func=AF.Ln), reads=[b_Rc[ob]], writes=[b_Rc[ob]])
                    P.op("act", X.activation(out=Rc[ob][0:64, :], in_=Rc[ob][0:64, :], func=AF.Exp, scale=-1.0), reads=[b_Rc[ob]], writes=[b_Rc[ob]])
                else:
                    P.op("dve", X.tensor_copy(out=Rc[ob][0:64, :], in_=Ob[ob][64:128, :]), reads=[b_O[ob]], writes=[b_Rc[ob]])
                    P.op("act", X.activation(out=Rc[ob][0:64, :], in_=Rc[ob][0:64, :], func=AF.Ln), reads=[b_Rc[ob]], writes=[b_Rc[ob]])
                    P.op("act", X.activation(out=Rc[ob][0:64, :], in_=Rc[ob][0:64, :], func=AF.Exp, scale=-1.0), reads=[b_Rc[ob]], writes=[b_Rc[ob]])
                P.op("dve", X.tensor_tensor(out=On[ob][0:64, :], in0=Ob[ob][0:64, :], in1=Rc[ob][0:64, :], op=ALU.mult), reads=[b_O[ob], b_Rc[ob]], writes=[b_On[ob]])
                dma("sp", OT[orow0 + h * 64:orow0 + (h + 1) * 64, Q * 512:(Q + 1) * 512], On[ob][0:64, :], reads=[b_On[ob]])
            for i in range(min(LA, len(steps))):
                qk(i)
            for i in range(len(steps)):
                if i + LA < len(steps):
                    qk(i + LA)
                ex(i)
                pv(i)
                if steps[i][5]:
                    norm(steps[i][0])
            gi[0] += len(steps)
        A.release()
        P.barrier()

    def outproj_ln(Wo, b_Wo, Xsrc, Xdst, layer):
        A.mark()
        load_ln_params(ln1_g, ln1_b, layer, 0)
        NZ = 5
        tmps = [(A.alloc([2, 6], F32), A.alloc([2], F32), A.alloc([2], F32), Buf()) for _ in range(NZ)]
        xts = [A.alloc([4, D], F32) for _ in range(3)]; b_xt = psbufs(3)
        ots = [A.alloc([8, 512], BF16) for _ in range(3)]; b_ot = psbufs(3)
        zs = [A.alloc([D], F32) for _ in range(NZ)]; b_z = psbufs(NZ)
        b_ps = pbufs(4); pi = [0]

        def loads(blk):
            dma("sp", xts[blk % 3], Xsrc[blk * 512:(blk + 1) * 512, :].rearrange("(t p) d -> p t d", p=128), writes=[b_xt[blk % 3]])
            dma("sp", ots[blk % 3], OT[:, blk * 512:(blk + 1) * 512].rearrange("(c p) s -> p c s", p=128), writes=[b_ot[blk % 3]])

        def s1(i):
            blk, t = divmod(i, 4)
            if t == 0 and blk + 2 < NB:
                loads(blk + 2)
            xt, bx = xts[blk % 3], b_xt[blk % 3]; ot, bo = ots[blk % 3], b_ot[blk % 3]
            z, bz = zs[i % NZ], b_z[i % NZ]
            for hh in range(2):
                pj = pi[0] % 4; pi[0] += 1
                for c in range(8):
                    P.op("pe", X.matmul(PS[pj], lhsT=ot[:, c, t * 128:(t + 1) * 128], rhs=Wo[:, c, hh * 512:(hh + 1) * 512], start=(c == 0), stop=(c == 7)),
                         reads=[bo, b_Wo], writes=[b_ps[pj]])
                P.op("dve", X.scalar_tensor_tensor(out=z[:, hh * 512:(hh + 1) * 512], in0=xt[:, t, hh * 512:(hh + 1) * 512], scalar=ALPHA, in1=PS[pj], op0=ALU.mult, op1=ALU.add),
                     reads=[bx, b_ps[pj]], writes=[bz])
            ln_stage1(z, bz, 0, tmps[i % NZ])

        def s2(i):
            ln_stage2(zs[i % NZ], b_z[i % NZ], 0, tmps[i % NZ])

        def s3(i):
            blk, t = divmod(i, 4)
            ln_stage3(zs[i % NZ], b_z[i % NZ], 0, Xdst, slice(blk * 512 + t * 128, blk * 512 + (t + 1) * 128))
        loads(0); loads(1)
        s1(0); s1(1); s2(0)
        for i in range(NT):
            if i + 2 < NT:
                s1(i + 2)
            if i + 1 < NT:
                s2(i + 1)
            s3(i)
        A.release()
        P.barrier()

    def ffn(layer, Xsrc, Xdst):
        A.mark()
        NFA = 14
        Wd_a = A.alloc([NFA, D], BF16)
        A.mark()
        Wg = A.alloc([8, DFF], BF16); Wu = A.alloc([8, DFF], BF16)
        bl_Wg, bl_Wu = load_w_colblocks([(Wg, w_gate[layer]), (Wu, w_up[layer])], DFF, cb=256)
        b_Wd = []
        for f0 in range(0, NFA, 2):
            grp = [(Wd_a[:, f, :], w_down[layer][f * 128:(f + 1) * 128, :]) for f in (f0, f0 + 1)]
            b = Buf()
            P.op("poolq", (lambda grp: (lambda e: [e.dma_start(out=o, in_=i) for (o, i) in grp]))(grp), writes=[b], ndma=2)
            b_Wd += [b, b]
        xts = [A.alloc([4, D], F32)] * 2; b_xt = [Buf()] * 2
        xTs = [A.alloc([8, 512], BF16) for _ in range(2)]; b_xT = psbufs(2)
        sgs = [A.alloc([512], F32) for _ in range(2)]; b_sg = psbufs(2)
        hts = [A.alloc([512], BF16) for _ in range(4)]; b_ht = psbufs(4)
        pst = PS[0:2]; b_pst = pbufs(2); cnt = [0]
        b_g = pbufs(2); b_u = pbufs(2); fi = 0
        load_x(0, Xsrc, xts[0], b_xt[0])
        for blk in range(NB):
            xt, bx = xts[blk % 2], b_xt[blk % 2]; xT, bxT = xTs[blk % 2], b_xT[blk % 2]
            make_xT(xt, bx, xT, bxT, pst, b_pst, cnt)
            if blk + 1 < NB:
                load_x(blk + 1, Xsrc, xts[(blk + 1) % 2], b_xt[(blk + 1) % 2])
            for f in range(NF):
                j = fi % 2; hj = fi % 4; fi += 1
                pg, pu = PS[2 + j], PS[4 + j]
                for c in range(8):
                    P.op("pe", X.matmul(pg, lhsT=Wg[:, c, f * 128:(f + 1) * 128], rhs=xT[:, c, :], start=(c == 0), stop=(c == 7)), reads=[colbuf(bl_Wg, (f + 1) * 128), bxT], writes=[b_g[j]])
                for c in range(8):
                    P.op("pe", X.matmul(pu, lhsT=Wu[:, c, f * 128:(f + 1) * 128], rhs=xT[:, c, :], start=(c == 0), stop=(c == 7)), reads=[colbuf(bl_Wu, (f + 1) * 128), bxT], writes=[b_u[j]])
                P.op("act", X.activation(out=sgs[j], in_=pg, func=AF.Silu), reads=[b_g[j]], writes=[b_sg[j]])
                P.op("dve", X.tensor_tensor(out=hts[hj], in0=pu, in1=sgs[j], op=ALU.mult), reads=[b_u[j], b_sg[j]], writes=[b_ht[hj]])
                dma("sp", HT[f * 128:(f + 1) * 128, blk * 512:(blk + 1) * 512], hts[hj], reads=[b_ht[hj]])
        A.release()
        P.barrier()
        A.mark()
        Wd_b = A.alloc([NF - NFA, D], BF16)
        for f0 in range(NFA, NF, 2):
            grp = [(Wd_b[:, f - NFA, :], w_down[layer][f * 128:(f + 1) * 128, :]) for f in (f0, f0 + 1)]
            b = Buf()
            P.op("poolq", (lambda grp: (lambda e: [e.dma_start(out=o, in_=i) for (o, i) in grp]))(grp), writes=[b], ndma=2)
            b_Wd += [b, b]

        def Wd_of(f):
            return Wd_a[:, f] if f < NFA else Wd_b[:, f - NFA]
        load_ln_params(ln2_g, ln2_b, layer, 2)
        NZ = 5
        tmps = [(A.alloc([2, 6], F32), A.alloc([2], F32), A.alloc([2], F32), Buf()) for _ in range(NZ)]
        xts = [A.alloc([4, D], F32) for _ in range(2)]; b_xt = psbufs(2)
        hbs = [A.alloc([NF, 512], BF16) for _ in range(2)]; b_hb = psbufs(2)
        zs = [A.alloc([D], F32) for _ in range(NZ)]; b_z = psbufs(NZ)
        b_ps = pbufs(4); pi = [0]

        def loads(blk):
            dma("sp", xts[blk % 2], Xsrc[blk * 512:(blk + 1) * 512, :].rearrange("(t p) d -> p t d", p=128), writes=[b_xt[blk % 2]])
            dma("sp", hbs[blk % 2], HT[:, blk * 512:(blk + 1) * 512].rearrange("(f p) s -> p f s", p=128), writes=[b_hb[blk % 2]])

        def s1(i):
            blk, t = divmod(i, 4)
            if t == 0 and blk + 1 < NB:
                loads(blk + 1)
            xt, bx = xts[blk % 2], b_xt[blk % 2]; hb, bh = hbs[blk % 2], b_hb[blk % 2]
            z, bz = zs[i % NZ], b_z[i % NZ]
            for hh in range(2):
                pj = pi[0] % 4; pi[0] += 1
                for f in range(NF):
                    P.op("pe", X.matmul(PS[pj], lhsT=hb[:, f, t * 128:(t + 1) * 128], rhs=Wd_of(f)[:, hh * 512:(hh + 1) * 512], start=(f == 0), stop=(f == NF - 1)),
                         reads=[bh, b_Wd[f]], writes=[b_ps[pj]])
                P.op("dve", X.scalar_tensor_tensor(out=z[:, hh * 512:(hh + 1) * 512], in0=xt[:, t, hh * 512:(hh + 1) * 512], scalar=ALPHA, in1=PS[pj], op0=ALU.mult, op1=ALU.add),
                     reads=[bx, b_ps[pj]], writes=[bz])
            ln_stage1(z, bz, 2, tmps[i % NZ])

        def s2(i):
            ln_stage2(zs[i % NZ], b_z[i % NZ], 2, tmps[i % NZ])

        def s3(i):
            blk, t = divmod(i, 4)
            ln_stage3(zs[i % NZ], b_z[i % NZ], 2, Xdst, slice(blk * 512 + t * 128, blk * 512 + (t + 1) * 128))
        loads(0)
        s1(0); s1(1); s2(0)
        for i in range(NT):
            if i + 2 < NT:
                s1(i + 2)
            if i + 1 < NT:
                s2(i + 1)
            s3(i)
        A.release()
        P.barrier()
        A.release()

    def inproj_ab(Xsrc):
        A.mark()
        Win = A.alloc([8, 1440], BF16); b_Win = Buf()
        load_w_cast(Win, ab_w_in, 1440, b_Win)
        Wuq = A.alloc([3, 768], BF16); b_Wuq = Buf()
        load_w_cast(Wuq, ab_w_uq, 768, b_Wuq)
        Wukv = A.alloc([2, 1024], BF16); b_Wukv = Buf()
        load_w_cast(Wukv, ab_w_ukv, 1024, b_Wukv)
        Wkr_rot = A.alloc([8, 32], BF16); Wuq_rot = A.alloc([3, 8, 32], BF16); b_rot = Buf()
        P.op("act", X.mul(out=Wkr_rot[:, :, 0:16], in_=Win[:, :, 656:672], mul=-1.0), reads=[b_Win], writes=[b_rot])
        P.op("act", X.copy(out=Wkr_rot[:, :, 16:32], in_=Win[:, :, 640:656]), reads=[b_Win], writes=[b_rot])
        Wuq4 = Wuq.rearrange("p c (h e) -> p c h e", e=96)
        for c in range(3):
            P.op("act", X.mul(out=Wuq_rot[:, c, :, 0:16], in_=Wuq4[:, c, :, 80:96], mul=-1.0), reads=[b_Wuq], writes=[b_rot])
            P.op("act", X.copy(out=Wuq_rot[:, c, :, 16:32], in_=Wuq4[:, c, :, 64:80]), reads=[b_Wuq], writes=[b_rot])
        Wq_nope = A.alloc([3, 512], BF16); Wq_rope = A.alloc([3, 256], BF16); Wk_nope = A.alloc([2, 512], BF16); b_wc = Buf()
        Wukv4 = Wukv.rearrange("p c (h e) -> p c h e", e=128)
        for c in range(3):
            P.op("dve", X.tensor_copy(out=Wq_nope[:, c, :].rearrange("p (h e) -> p h e", e=64), in_=Wuq4[:, c, :, 0:64]), reads=[b_Wuq], writes=[b_wc])
            P.op("pool", X.tensor_copy(out=Wq_rope[:, c, :].rearrange("p (h e) -> p h e", e=32), in_=Wuq4[:, c, :, 64:96]), reads=[b_Wuq], writes=[b_wc])
        for c in range(2):
            P.op("dve", X.tensor_copy(out=Wk_nope[:, c, :].rearrange("p (h e) -> p h e", e=64), in_=Wukv4[:, c, :, 0:64]), reads=[b_Wukv], writes=[b_wc])
        Wq_rot = Wuq_rot.rearrange("p c h e -> p c (h e)")
        gn = A.alloc([5], F32); b_gn = Buf()
        dma("sp", gn[:, 0:3], ab_q_norm.rearrange("(c p) -> p c", p=128), writes=[b_gn])
        dma("sp", gn[:, 3:5], ab_kv_norm.rearrange("(c p) -> p c", p=128), writes=[b_gn])
        xts = [A.alloc([4, D], F32) for _ in range(2)]; b_xt = psbufs(2)
        xTs = [A.alloc([8, 512], BF16) for _ in range(2)]; b_xT = psbufs(2)
        cs = [A.alloc([2, 512], F32) for _ in range(2)]; b_cs = psbufs(2)
        c32 = A.alloc([5, 512], F32); b_c32 = Buf()
        sq = A.alloc([5, 512], BF16); b_sq = Buf()
        rbc = A.alloc([2, 512], F32); b_rbc = Buf()
        cn = A.alloc([5, 512], BF16); b_cn = Buf()
        NST = 6
        stg = [A.alloc([512], BF16) for _ in range(NST)]; b_stg = psbufs(NST)
        t1s = [A.alloc([512], F32) for _ in range(2)]; t2s = [A.alloc([512], F32) for _ in range(2)]; b_t = psbufs(2)
        vst = [A.alloc([4, 512], BF16) for _ in range(2)]; b_vst = psbufs(2)
        vss = [A.alloc([4, 128], BF16) for _ in range(2)]; b_vss = psbufs(2)
        pst = PS[0:2]; b_pst = pbufs(2); cnt = [0]
        b_ps = pbufs(6); pi = [0]; si = [0]; ti = [0]
        SC_MLA = 96.0 ** -0.5
        SC_SWA = 64.0 ** -0.5

        def nps():
            j = pi[0] % 6; pi[0] += 1
            return PS[2 + j], b_ps[j]

        def nstg():
            j = si[0] % NST; si[0] += 1
            return stg[j], b_stg[j]

        def fm_proj(ps, bps, M, wsl, W_b, rhs_of_c, nchunks, rhs_b):
            for c in range(nchunks):
                P.op("pe", X.matmul(ps[0:M, :], lhsT=wsl(c), rhs=rhs_of_c(c), start=(c == 0), stop=(c == nchunks - 1)), reads=[W_b, rhs_b], writes=[bps])

        def rope_combine(psA, bA, psB, bB, cst, bcs, scale, dst_dram, np_=32):
            j = ti[0] % 2; ti[0] += 1
            t1, t2, bt = t1s[j], t2s[j], b_t[j]
            P.op("dve", X.scalar_tensor_tensor(out=t1[0:np_, :], in0=psA[0:np_, :], scalar=float(scale), in1=cst[0:np_, 0, :], op0=ALU.mult, op1=ALU.mult), reads=[bA, bcs], writes=[bt])
            P.op("dve", X.scalar_tensor_tensor(out=t2[0:np_, :], in0=psB[0:np_, :], scalar=float(scale), in1=cst[0:np_, 1, :], op0=ALU.mult, op1=ALU.mult), reads=[bB, bcs], writes=[bt])
            s, bs = nstg()
            P.op("pool", X.tensor_tensor(out=s[0:np_, :], in0=t1[0:np_, :], in1=t2[0:np_, :], op=ALU.add), reads=[bt], writes=[bs])
            return s, bs

        for blk in range(NB):
            cols = slice(blk * 512, (blk + 1) * 512)
            xt, bx = xts[blk % 2], b_xt[blk % 2]; xT, bxT = xTs[blk % 2], b_xT[blk % 2]
            cst, bcs = cs[blk % 2], b_cs[blk % 2]
            if blk == 0:
                load_x(0, Xsrc, xts[0], b_xt[0])
            if blk + 1 < NB:
                load_x(blk + 1, Xsrc, xts[(blk + 1) % 2], b_xt[(blk + 1) % 2])
            make_xT(xt, bx, xT, bxT, pst, b_pst, cnt)
            P.op("sp", (lambda cst, cols: (lambda e: [e.dma_start(out=cst[32 * r:32 * r + 32], in_=rope_cs[:, :, cols].rearrange("a p s -> p a s")) for r in range(4)]))(cst, cols), writes=[bcs], ndma=4)
            if dbg and blk == 0:
                dxt = dsc("DBG_xT", [128, 8, 512], BF16); dwin = dsc("DBG_Win", [128, 8, 1440], BF16)
                dma("sp", dxt, xT, reads=[bxT]); dma("sp", dwin, Win, reads=[b_Win])
            if DBGSTOP == 2: continue
            for i in range(5):
                ps, bps = nps()
                fm_proj(ps, bps, 128, lambda c, i=i: Win[:, c, i * 128:(i + 1) * 128], b_Win, lambda c: xT[:, c, :], 8, bxT)
                P.op("dve", X.tensor_copy(out=c32[:, i, :], in_=ps), reads=[bps], writes=[b_c32])
                P.op("act", X.activation(out=sq[:, i, :], in_=ps, func=AF.Square), reads=[bps], writes=[b_sq])
            for (k, i0, n, R_) in ((0, 0, 3, 384.0), (1, 3, 2, 256.0)):
                ps, bps = nps()
                for i in range(n):
                    P.op("pe", X.matmul(ps, lhsT=ones_bf, rhs=sq[:, i0 + i, :], start=(i == 0), stop=(i == n - 1)), reads=[b_ones, b_sq], writes=[bps])
                P.op("act", X.activation(out=rbc[:, k, :], in_=ps, func=AF.Ln, scale=1.0 / R_, bias=eps_rms[:, 0:1]), reads=[bps, b_eps], writes=[b_rbc])
                P.op("act", X.activation(out=rbc[:, k, :], in_=rbc[:, k, :], func=AF.Exp, scale=-0.5), reads=[b_rbc], writes=[b_rbc])
                for i in range(n):
                    P.op("dve", X.scalar_tensor_tensor(out=cn[:, i0 + i, :], in0=c32[:, i0 + i, :], scalar=gn[:, i0 + i:i0 + i + 1], in1=rbc[:, k, :], op0=ALU.mult, op1=ALU.mult),
                         reads=[b_c32, b_gn, b_rbc], writes=[b_cn])
            if DBGSTOP == 3: continue
            psA, bA = nps(); psB, bB = nps()
            fm_proj(psA, bA, 32, lambda c: Win[:, c, 640:672], b_Win, lambda c: xT[:, c, :], 8, bxT)
            fm_proj(psB, bB, 32, lambda c: Wkr_rot[:, c, :], b_rot, lambda c: xT[:, c, :], 8, bxT)
            s, bs = rope_combine(psA, bA, psB, bB, cst, bcs, 1.0, None)
            for h in range(8):
                dma("sp", KTa[h, 0:32, cols], s[0:32, :], reads=[bs])
            if DBGSTOP == 4: continue
            for i in range(4):
                ps, bps = nps()
                fm_proj(ps, bps, 128, lambda c, i=i: Win[:, c, 672 + i * 128:672 + (i + 1) * 128], b_Win, lambda c: xT[:, c, :], 8, bxT)
                s, bs = nstg()
                evac("act" if i % 2 == 0 else "dve", s, ps, [bps], [bs], scale=SC_SWA)
                dma("sp", QTs[2 * i:2 * i + 2, :, cols].rearrange("h p s -> (h p) s"), s, reads=[bs])
            ps, bps = nps()
            fm_proj(ps, bps, 128, lambda c: Win[:, c, 1184:1312], b_Win, lambda c: xT[:, c, :], 8, bxT)
            s, bs = nstg()
            evac("act", s, ps, [bps], [bs])
            dma("sp", KTs[:, :, cols].rearrange("h p s -> (h p) s"), s, reads=[bs])
            ps, bps = nps()
            for t in range(4):
                for c in range(8):
                    P.op("pe", X.matmul(ps[:, t * 128:(t + 1) * 128], lhsT=xT[:, c, t * 128:(t + 1) * 128], rhs=Win[:, c, 1312:1440], start=(c == 0), stop=(c == 7)), reads=[bxT, b_Win], writes=[bps])
            vs_, bvs = vss[blk % 2], b_vss[blk % 2]
            evac("dve", vs_, ps.rearrange("p (t e) -> p t e", e=128), [bps], [bvs])
            dma("sp", Vs[cols].rearrange("(t p) g e -> p t (g e)", p=128), vs_, reads=[bvs])
            if DBGSTOP == 5: continue
            Wuq4_ = Wuq.rearrange("p c (h e) -> p c h e", e=96)
            for g in range(4):
                ps, bps = nps()
                fm_proj(ps, bps, 128, lambda c, g=g: Wq_nope[:, c, 128 * g:128 * g + 128], b_wc, lambda c: cn[:, c, :], 3, b_cn)
                s, bs = nstg()
                evac("act", s, ps, [bps], [bs], scale=SC_MLA)
                dma("sp", QTa[2 * g, 32:96, cols], s[0:64, :], reads=[bs])
                dma("sp", QTa[2 * g + 1, 32:96, cols], s[64:128, :], reads=[bs])
            for g in range(2):
                psA, bA = nps(); psB, bB = nps()
                fm_proj(psA, bA, 128, lambda c, g=g: Wq_rope[:, c, 128 * g:128 * g + 128], b_wc, lambda c: cn[:, c, :], 3, b_cn)
                fm_proj(psB, bB, 128, lambda c, g=g: Wq_rot[:, c, 128 * g:128 * g + 128], b_rot, lambda c: cn[:, c, :], 3, b_cn)
                s, bs = rope_combine(psA, bA, psB, bB, cst, bcs, SC_MLA, None, np_=128)
                for r in range(4):
                    dma("sp", QTa[4 * g + r, 0:32, cols], s[32 * r:32 * r + 32, :], reads=[bs])
            if DBGSTOP == 6: continue
            Wukv4_ = Wukv.rearrange("p c (h e) -> p c h e", e=128)
            for g in range(4):
                ps, bps = nps()
                fm_proj(ps, bps, 128, lambda c, g=g: Wk_nope[:, c, 128 * g:128 * g + 128], b_wc, lambda c: cn[:, 3 + c, :], 2, b_cn)
                s, bs = nstg()
                evac("dve" if g % 2 == 0 else "act", s, ps, [bps], [bs])
                dma("sp", KTa[2 * g, 32:96, cols], s[0:64, :], reads=[bs])
                dma("sp", KTa[2 * g + 1, 32:96, cols], s[64:128, :], reads=[bs])
            if DBGSTOP == 7: continue
            vt, bvt = vst[blk % 2], b_vst[blk % 2]
            for t in range(4):
                ps, bps = nps()
                for c in range(2):
                    P.op("pe", X.matmul(ps.rearrange("p (h e) -> p h e", e=64), lhsT=cn[:, 3 + c, t * 128:(t + 1) * 128], rhs=Wukv[:, c, :].rearrange("p (h e) -> p h e", e=128)[:, :, 64:128], start=(c == 0), stop=(c == 1)),
                         reads=[b_cn, b_Wukv], writes=[bps])
                evac("act" if t % 2 == 0 else "dve", vt[:, t, :], ps, [bps], [bvt])
            dma("sp", Va[cols].rearrange("(t p) h e -> p t (h e)", p=128), vt, reads=[bvt])
        A.release()
        P.barrier()

    def inproj_cd(Xsrc):
        A.mark()
        Win = A.alloc([8, 3080], BF16)
        bl_Win = load_w_colblocks([(Win, cd_w_in)], 3080, cb=440)[0]

        def b_Win_of(c_lo, c_hi):
            bs = []
            for (c1, b) in bl_Win:
                if c1 > c_lo and c1 - 440 < c_hi:
                    bs.append(b)
            return bs
        nb = A.alloc([1], F32); b_nb = Buf()
        dma("sp", nb[0:8, :], cd_b_forget.rearrange("(h o) -> h o", o=1), writes=[b_nb])
        P.op("dve", X.tensor_scalar(out=nb[0:8, :], in0=nb[0:8, :], scalar1=-1.0, scalar2=None, op0=ALU.mult), reads=[b_nb], writes=[b_nb])
        ones32 = A.alloc([512], F32); b_o32 = Buf()
        P.op("pool", X.memset(ones32, 1.0), writes=[b_o32])
        onesS = A.alloc([3, 512], BF16); b_oS = Buf()
        P.op("pool", X.memset(onesS[0:8], 1.0), writes=[b_oS])
        cum = A.alloc([S], F32); b_cum = Buf()
        xts = [A.alloc([4, D], F32)] * 2; b_xt = [Buf()] * 2
        xTs = [A.alloc([8, 512], BF16) for _ in range(2)]; b_xT = psbufs(2)
        NST = 6
        stg = [A.alloc([512], BF16) for _ in range(NST)]; b_stg = psbufs(NST)
        vst = [A.alloc([4, 512], BF16) for _ in range(2)] * 2; b_vst = psbufs(2) * 2
        ls = A.alloc([512], F32); b_ls = Buf()
        r1 = A.alloc([512], F32); r2 = A.alloc([512], F32); b_r = Buf()
        augk = [A.alloc([3, 512], BF16) for _ in range(2)]; augq = [A.alloc([3, 512], BF16) for _ in range(2)]; b_aug = psbufs(2)
        pst = PS[0:2]; b_pst = pbufs(2); cnt = [0]
        b_ps = pbufs(6); pi = [0]; si = [0]; vi = [0]
        SC = 64.0 ** -0.5

        def nps():
            j = pi[0] % 6; pi[0] += 1
            return PS[2 + j], b_ps[j]

        def nstg():
            j = si[0] % NST; si[0] += 1
            return stg[j], b_stg[j]

        for blk in range(NB):
            cols = slice(blk * 512, (blk + 1) * 512)
            xt, bx = xts[blk % 2], b_xt[blk % 2]; xT, bxT = xTs[blk % 2], b_xT[blk % 2]
            if blk == 0:
                load_x(0, Xsrc, xt, bx)
            make_xT(xt, bx, xT, bxT, pst, b_pst, cnt)
            if blk + 1 < NB:
                load_x(blk + 1, Xsrc, xt, bx)
            dma("sp", QTf[:, 67:70, blk * 512:(blk + 1) * 512], onesS[0:8], reads=[b_oS])
            dma("sp", KTf[:, 64:67, blk * 512:(blk + 1) * 512], onesS[0:8], reads=[b_oS])
            for (c0, dst, scale) in ((0, QTf, SC), (512, KTf, None), (1544, QTc, SC), (2056, KTc, None)):
                for i in range(4):
                    ps, bps = nps()
                    for c in range(8):
                        P.op("pe", X.matmul(ps, lhsT=Win[:, c, c0 + i * 128:c0 + (i + 1) * 128], rhs=xT[:, c, :], start=(c == 0), stop=(c == 7)), reads=b_Win_of(c0 + i * 128, c0 + (i + 1) * 128) + [bxT], writes=[bps])
                    s, bs = nstg()
                    evac("act" if i % 2 == 0 else "dve", s, ps, [bps], [bs], scale=scale)
                    dma("sp", dst[2 * i, 0:64, cols], s[0:64, :], reads=[bs])
                    dma("sp", dst[2 * i + 1, 0:64, cols], s[64:128, :], reads=[bs])
            for (c0, dst) in ((1024, Vf), (2568, Vc)):
                vt, bvt = vst[vi[0] % 4], b_vst[vi[0] % 4]; vi[0] += 1
                for t in range(4):
                    ps, bps = nps()
                    for c in range(8):
                        P.op("pe", X.matmul(ps, lhsT=xT[:, c, t * 128:(t + 1) * 128], rhs=Win[:, c, c0:c0 + 512], start=(c == 0), stop=(c == 7)), reads=[bxT] + b_Win_of(c0, c0 + 512), writes=[bps])
                    evac("act" if t % 2 == 0 else "dve", vt[:, t, :], ps, [bps], [bvt])
                dma("sp", dst[cols].rearrange("(t p) h e -> p t (h e)", p=128), vt, reads=[bvt])
            ps, bps = nps()
            for c in range(8):
                P.op("pe", X.matmul(ps[0:8, :], lhsT=Win[:, c, 1536:1544], rhs=xT[:, c, :], start=(c == 0), stop=(c == 7)), reads=b_Win_of(1536, 1544) + [bxT], writes=[bps])
            P.op("act", X.activation(out=ls[0:8, :], in_=ps[0:8, :], func=AF.Exp, scale=-1.0, bias=nb[0:8, 0:1]), reads=[bps, b_nb], writes=[b_ls])
            P.op("act", X.activation(out=ls[0:8, :], in_=ls[0:8, :], func=AF.Ln, bias=1.0), reads=[b_ls], writes=[b_ls])
            init = 0.0 if blk == 0 else cum[0:8, blk * 512 - 1:blk * 512]
            P.op("dve", X.tensor_tensor_scan(out=cum[0:8, cols], data0=ones32[0:8, :], data1=ls[0:8, :], initial=init, op0=ALU.mult, op1=ALU.add), reads=[b_ls, b_o32, b_cum], writes=[b_cum])
            ak, aq, ba = augk[blk % 2], augq[blk % 2], b_aug[blk % 2]
            P.op("dve", X.tensor_copy(out=ak[0:8, 0, :], in_=cum[0:8, cols]), reads=[b_cum], writes=[ba])
            P.op("dve", X.tensor_tensor(out=r1[0:8, :], in0=cum[0:8, cols], in1=ak[0:8, 0, :], op=ALU.subtract), reads=[b_cum, ba], writes=[b_r])
            P.op("dve", X.tensor_copy(out=ak[0:8, 1, :], in_=r1[0:8, :]), reads=[b_r], writes=[ba])
            P.op("dve", X.tensor_tensor(out=r2[0:8, :], in0=r1[0:8, :], in1=ak[0:8, 1, :], op=ALU.subtract), reads=[b_r, ba], writes=[b_r])
            P.op("dve", X.tensor_copy(out=ak[0:8, 2, :], in_=r2[0:8, :]), reads=[b_r], writes=[ba])
            P.op("dve", X.tensor_scalar(out=aq[0:8], in0=ak[0:8], scalar1=-1.0, scalar2=None, op0=ALU.mult), reads=[ba], writes=[ba])
            dma("sp", KTf[:, 67:70, cols], ak[0:8], reads=[ba])
            dma("sp", QTf[:, 64:67, cols], aq[0:8], reads=[ba])
        A.release()
        P.barrier()

    def prep_bias(shape, load_fn):
        bf = A.alloc(shape, BF16); b = Buf()
        f = A.alloc(shape, F32)
        load_fn(f, b)
        P.op("dve", X.tensor_copy(out=bf, in_=f), reads=[b], writes=[b])
        return bf, b

    inproj_ab(x_in)
    A.mark()
    Wo = A.alloc([8, D], BF16); b_Wo = Buf()
    abf, b_al = prep_bias([8, 2, 128], lambda f, b: dma("poolq", f, alibi_d, writes=[b]))
    sk = A.alloc([8], F32); b_sk = Buf()
    dma("poolq", sk, ab_sinks.rearrange("(o h) -> o h", o=1).broadcast_to([128, 8]), writes=[b_sk])
    P.op("act", X.activation(out=sk, in_=sk, func=AF.Exp), reads=[b_sk], writes=[b_sk])
    load_w_cast(Wo, ab_w_out, D, b_Wo)
    attention_full(QTa, KTa, Va, 96, mla_mask_d, 0)
    attention_band(QTs, KTs, Vs, 2, 1, abf, b_al, 512, sink=sk, b_sink=b_sk)
    outproj_ln(Wo, b_Wo, x_in, X1, 0)
    A.release()
    ffn(0, X1, X2)
    inproj_cd(X2)
    A.mark()
    Wo = A.alloc([8, D], BF16); b_Wo = Buf()

    def load_ck(f, b):
        cm = A.alloc([5, 128], F32)
        dma("poolq", f, ck_bias_d, writes=[b])
        dma("poolq", cm, ck_mask_d, writes=[b])
        for h in range(8):
            P.op("pool", X.tensor_tensor(out=f[:, h], in0=f[:, h], in1=cm, op=ALU.add), reads=[b], writes=[b])
    cbf, b_cb = prep_bias([8, 5, 128], load_ck)
    load_w_cast(Wo, cd_w_out, D, b_Wo)
    attention_full(QTf, KTf, Vf, 70, causal_mask_d, 0)
    attention_band(QTc, KTc, Vc, 8, 4, cbf, b_cb, 512)
    outproj_ln(Wo, b_Wo, X2, X3, 1)
    A.release()
    ffn(1, X3, y_out)
    P.emit()
    es.close()
    return nc


_CACHE = {}


def kernel(**inputs):
    import ml_dtypes
    hc = host_constants()
    idx = hc.pop("_ck_idx")
    rel = np.asarray(inputs["cd_rel_bias"], np.float32)[0]
    hc["ck_bias"] = np.ascontiguousarray(np.transpose(rel[idx], (0, 3, 1, 2)))
    if "nc" not in _CACHE:
        _CACHE["nc"] = build_nc()
    nc = _CACHE["nc"]
    x = np.asarray(inputs["x"], np.float32)
    shared = {
        "ab_w_in": inputs["ab_w_in"][0], "ab_q_norm": inputs["ab_q_norm"][0], "ab_w_uq": inputs["ab_w_uq"][0],
        "ab_kv_norm": inputs["ab_kv_norm"][0], "ab_w_ukv": inputs["ab_w_ukv"][0], "ab_sinks": inputs["ab_sinks"][0],
        "ab_w_out": inputs["ab_w_out"][0], "cd_w_in": inputs["cd_w_in"][0], "cd_b_forget": inputs["cd_b_forget"][0],
        "cd_w_out": inputs["cd_w_out"][0],
        "ln1_g": inputs["ln1_g"], "ln1_b": inputs["ln1_b"], "ln2_g": inputs["ln2_g"], "ln2_b": inputs["ln2_b"],
        "ffn_w_gate": inputs["ffn_w_gate"], "ffn_w_up": inputs["ffn_w_up"], "ffn_w_down": inputs["ffn_w_down"],
    }
    shared = {k: np.ascontiguousarray(np.asarray(v, np.float32)) for k, v in shared.items()}
    shared.update(hc)
    in_maps = [dict(shared, x=np.ascontiguousarray(x[b])) for b in range(8)]
    res = run_bass_kernel_spmd(nc, in_maps, core_ids=list(range(8)))
    return np.stack([np.asarray(r["y"], np.float32) for r in res.results], axis=0)
```

```python
import numpy as np
import concourse.bass as bass
import concourse.mybir as mybir
from concourse.bass_utils import run_bass_kernel_spmd
from contextlib import ExitStack

F32 = mybir.dt.float32
BF16 = mybir.dt.bfloat16
AF = mybir.ActivationFunctionType
ALU = mybir.AluOpType
AX = mybir.AxisListType

COMPUTE = ("pe", "act", "dve", "pool")
QUEUES = ("sp", "poolq")
NDMASEM = 12


class _Rec:
    def __init__(self, name):
        self.name = name

    def __call__(self, *a, **k):
        name = self.name
        return lambda e: getattr(e, name)(*a, **k)


class _Recorder:
    def __getattr__(self, name):
        return _Rec(name)


X = _Recorder()


class Buf:
    __slots__ = ("name", "w", "r", "rd", "excl")

    def __init__(self, name="", excl=False):
        self.name = name
        self.excl = excl
        self.w = None
        self.r = {}
        self.rd = []


class Op:
    __slots__ = ("eng", "fn", "deps", "sig", "sem", "val", "ndma", "isdma", "prev")


class Prog:
    def __init__(self, nc):
        self.nc = nc
        self.streams = {"pe": [], "act": [], "dve": [], "pool": [], "sp": []}
        self.barrier_deps = []
        self.since_barrier_dma = []
        self.last = {}
        self.wgroups = {}

    def _stream_of(self, eng):
        return "pool" if eng == "poolq" else eng

    def op(self, eng, fn, reads=(), writes=(), ndma=0):
        o = Op()
        o.eng = eng
        o.fn = fn
        o.ndma = ndma
        o.isdma = ndma > 0
        o.sig = False
        o.sem = None
        o.val = 0
        st = self._stream_of(eng)
        deps = []
        xr = [b for b in reads if b.excl]
        if xr:
            reads = [b for b in reads if not b.excl]
            writes = list(writes) + xr
        for b in reads:
            if b.w is not None:
                deps.append(b.w)
            deps.extend(self.wgroups.get(id(b), ()))
        for b in writes:
            if b.w is not None:
                deps.append(b.w)
            if id(b) in self.wgroups:
                deps.extend(self.wgroups.pop(id(b)))
            deps.extend(b.r.values())
            deps.extend(b.rd)
        deps.extend(self.barrier_deps)
        if eng == "pe":
            deps = [d for d in deps if d.isdma or d.eng != "pe"]
        seen = set()
        dd = []
        for d in deps:
            if id(d) not in seen and d is not o:
                seen.add(id(d))
                dd.append(d)
                d.sig = True
        o.deps = dd
        for b in reads:
            if o.isdma:
                b.rd.append(o)
            else:
                b.r[eng] = o
        for b in writes:
            b.w = o
            b.r = {}
            b.rd = []
        self.streams[st].append(o)
        if o.isdma:
            self.since_barrier_dma.append(o)
            o.sig = True
        else:
            self.last[eng] = o
        return o

    def barrier(self):
        deps = list(self.last.values()) + list(self.since_barrier_dma)
        for d in deps:
            d.sig = True
        self.barrier_deps = deps
        self.since_barrier_dma = []

    def emit(self, final_wait_eng="sp"):
        nc = self.nc
        for d in list(self.last.values()):
            d.sig = True
        with ExitStack() as es:
            sems = {}
            for e in COMPUTE:
                sems[e] = es.enter_context(nc.semaphore("s_" + e))
            dsems = {}
            for q in QUEUES:
                dsems[q] = [es.enter_context(nc.semaphore("d_%s%d" % (q, i))) for i in range(NDMASEM)]
            for st, ops in self.streams.items():
                cnt = 0
                qcnt = {q: 0 for q in QUEUES}
                qval = {q: [0] * NDMASEM for q in QUEUES}
                for o in ops:
                    if o.isdma:
                        k = qcnt[o.eng] % NDMASEM
                        qcnt[o.eng] += 1
                        o.sem = dsems[o.eng][k]
                        o.prev = qval[o.eng][k]
                        qval[o.eng][k] += 16 * o.ndma
                        o.val = qval[o.eng][k]
                    elif o.sig:
                        cnt += 1
                        o.sem = sems[o.eng]
                        o.val = cnt
            tail = list(self.barrier_deps) + list(self.last.values()) + list(self.since_barrier_dma)
            for d in tail:
                assert d.sig or d.isdma
            self.maxval = 0
            es.enter_context(nc.allow_non_contiguous_dma(reason='small strided param loads'))
            block = es.enter_context(nc.Block())
            handles = {"pe": "tensor", "act": "scalar", "dve": "vector", "pool": "gpsimd", "sp": "sync"}

            def run_stream(st, eng):
                waited = {}

                def wait(sem, val):
                    if val <= 0:
                        return
                    key = id(sem)
                    if waited.get(key, 0) >= val:
                        return
                    waited[key] = val
                    self.maxval = max(self.maxval, val)
                    eng.wait_ge(sem, val)

                for o in self.streams[st]:
                    for d in o.deps:
                        wait(d.sem, d.val)
                    if o.isdma:
                        wait(o.sem, o.prev)
                        ins = o.fn(eng)
                        assert len(ins) == o.ndma, (len(ins), o.ndma)
                        for i in ins:
                            i.then_inc(o.sem, 16)
                    else:
                        i = o.fn(eng)
                        if o.sig:
                            i.then_inc(o.sem, 1)
                if st == final_wait_eng:
                    for d in tail:
                        wait(d.sem, d.val)

            for st in self.streams:
                getattr(block, handles[st])(lambda eng, st=st: run_stream(st, eng))


class Arena:
    def __init__(self, nc, es, nbytes, name="arena"):
        self.t = es.enter_context(nc.sbuf_tensor(name, [128, nbytes // 4], F32))
        self.nbytes = nbytes
        self.off = 0
        self.marks = []

    def alloc(self, free_shape, dtype, parts=128, pbase=0):
        esz = 2 if dtype == BF16 else 4
        n = int(np.prod(free_shape))
        nb = (n * esz + 31) // 32 * 32
        assert self.off + nb <= self.nbytes, "SBUF arena overflow: %d + %d > %d" % (self.off, nb, self.nbytes)
        w0 = self.off // 4
        ap = self.t[:, w0:w0 + nb // 4]
        if dtype != F32:
            ap = ap.bitcast(dtype)
        ap = ap[:, 0:n]
        self.off += nb
        if len(free_shape) == 2:
            ap = ap.rearrange("p (a b) -> p a b", b=free_shape[1])
        elif len(free_shape) == 3:
            ap = ap.rearrange("p (a b c) -> p a b c", b=free_shape[1], c=free_shape[2])
        return ap

    def mark(self):
        self.marks.append(self.off)

    def release(self):
        self.off = self.marks.pop()


S = 4096
D = 1024
NT = 32
NB = 8
DFF = 2816
NF = 22
ALPHA = 4.0 ** 0.25
LN_EPS = 1e-5
RMS_EPS = 1e-6
NEG = -30000.0
import os as _os
DBGSTOP = int(_os.environ.get('DBGSTOP', '0'))


def host_constants():
    c = {}
    inv = (10000.0 ** (-np.arange(0, 32, 2, dtype=np.float32) / 32)).astype(np.float32)
    ang = np.arange(S, dtype=np.float32)[:, None] * inv[None, :]
    cos, sin = np.cos(ang).astype(np.float32), np.sin(ang).astype(np.float32)
    cs = np.zeros((2, 32, S), np.float32)
    cs[0, 0:16] = cos.T
    cs[0, 16:32] = cos.T
    cs[1, 0:16] = sin.T
    cs[1, 16:32] = sin.T
    c["rope_cs"] = cs
    k = np.arange(128)[:, None]
    q = np.arange(128)[None, :]
    c["mla_mask"] = np.where((k >= 64) & (q < 64), NEG, 0.0).astype(np.float32)
    c["causal_mask"] = np.where(k > q, NEG, 0.0).astype(np.float32)

    def band_mask(nof, L):
        m = np.zeros((128, nof, 128), np.float32)
        for o in range(nof):
            dc = 2 * o + (q >= 64).astype(np.int32) - (k >= 64).astype(np.int32)
            m[:, o, :] = np.where((dc >= 0) & (dc <= L), 0.0, NEG)
        return m
    slopes = np.exp2(-8.0 * np.arange(1, 9, dtype=np.float32) / 8).astype(np.float32)
    al = np.zeros((128, 8, 2, 128), np.float32)
    bm = band_mask(2, 2)
    for h in range(8):
        for o in range(2):
            dist = 128 * o + q - k
            al[:, h, o, :] = -slopes[h] * np.abs(dist).astype(np.float32) + bm[:, o, :]
    c["alibi"] = al
    c["ck_mask"] = band_mask(5, 8)
    idx = np.zeros((128, 5, 128), np.int64)
    for o in range(5):
        idx[:, o, :] = np.clip(128 * o + q - k, -63, 256) + 63
    c["_ck_idx"] = idx
    return c


def build_nc(dbg=False, nphase=99):
    nc = bass.Bass("TRN2", target_bir_lowering=False)

    def din(name, shape, dt=F32):
        return nc.dram_tensor(name, list(shape), dt, kind="ExternalInput").ap()
    skind = "ExternalOutput" if dbg else "Internal"

    def dsc(name, shape, dt):
        return nc.dram_tensor(name, list(shape), dt, kind=skind).ap()

    x_in = din("x", [S, D])
    ab_w_in = din("ab_w_in", [D, 1440]); ab_q_norm = din("ab_q_norm", [384]); ab_w_uq = din("ab_w_uq", [384, 768])
    ab_kv_norm = din("ab_kv_norm", [256]); ab_w_ukv = din("ab_w_ukv", [256, 1024]); ab_sinks = din("ab_sinks", [8])
    ab_w_out = din("ab_w_out", [D, D]); cd_w_in = din("cd_w_in", [D, 3080]); cd_b_forget = din("cd_b_forget", [8])
    cd_w_out = din("cd_w_out", [D, D])
    ln1_g = din("ln1_g", [2, D]); ln1_b = din("ln1_b", [2, D]); ln2_g = din("ln2_g", [2, D]); ln2_b = din("ln2_b", [2, D])
    w_gate = din("ffn_w_gate", [2, D, DFF]); w_up = din("ffn_w_up", [2, D, DFF]); w_down = din("ffn_w_down", [2, DFF, D])
    rope_cs = din("rope_cs", [2, 32, S]); mla_mask_d = din("mla_mask", [128, 128]); causal_mask_d = din("causal_mask", [128, 128])
    alibi_d = din("alibi", [128, 8, 2, 128]); ck_mask_d = din("ck_mask", [128, 5, 128]); ck_bias_d = din("ck_bias", [128, 8, 5, 128])
    y_out = nc.dram_tensor("y", [S, D], F32, kind="ExternalOutput").ap()

    QTa = dsc("QTa", [8, 96, S], BF16); KTa = dsc("KTa", [8, 96, S], BF16); Va = dsc("Va", [S, 8, 64], BF16)
    QTs = dsc("QTs", [8, 64, S], BF16); KTs = dsc("KTs", [2, 64, S], BF16); Vs = dsc("Vs", [S, 2, 64], BF16)
    QTf = dsc("QTf", [8, 70, S], BF16); KTf = dsc("KTf", [8, 70, S], BF16); Vf = dsc("Vf", [S, 8, 64], BF16)
    QTc = dsc("QTc", [8, 64, S], BF16); KTc = dsc("KTc", [8, 64, S], BF16); Vc = dsc("Vc", [S, 8, 64], BF16)
    OT = dsc("OT", [D, S], BF16)
    HT = dsc("HT", [DFF, S], BF16)
    X1 = dsc("X1", [S, D], F32); X2 = dsc("X2", [S, D], F32); X3 = dsc("X3", [S, D], F32)

    P = Prog(nc)
    es = ExitStack()
    A = Arena(nc, es, 176 * 1024)
    PP = [es.enter_context(nc.psum_tensor("pp%d" % i, [128, 1024], F32))[:, :] for i in range(4)]
    PS = []
    for i in range(4):
        PS += [PP[i][:, 0:512], PP[i][:, 512:1024]]

    def dma(q, out, in_, reads=(), writes=()):
        return P.op(q, lambda e: [e.dma_start(out=out, in_=in_)], reads, writes, ndma=1)

    def evac(eng, out, in_, reads, writes, scale=None):
        if eng == "act":
            if scale is None:
                return P.op("act", X.copy(out=out, in_=in_), reads, writes)
            return P.op("act", X.mul(out=out, in_=in_, mul=float(scale)), reads, writes)
        if scale is None:
            return P.op(eng, X.tensor_copy(out=out, in_=in_), reads, writes)
        return P.op(eng, X.tensor_scalar(out=out, in0=in_, scalar1=float(scale), scalar2=None, op0=ALU.mult), reads, writes)

    ident = A.alloc([128], F32); b_ident = Buf()
    P.op("pool", X.memset(ident, 1.0), writes=[b_ident])
    P.op("pool", X.affine_select(out=ident, in_=ident, pattern=[[-1, 128]], compare_op=ALU.is_equal, fill=0.0, base=0, channel_multiplier=1), reads=[b_ident], writes=[b_ident])
    ident_bf = A.alloc([128], BF16)
    P.op("dve", X.tensor_copy(out=ident_bf, in_=ident), reads=[b_ident], writes=[b_ident])
    ones_bf = A.alloc([128], BF16); b_ones = Buf()
    P.op("pool", X.memset(ones_bf, 1.0), writes=[b_ones])
    lng = A.alloc([4, D], F32); b_lng = Buf()
    eps_ln = A.alloc([1], F32); eps_rms = A.alloc([1], F32); b_eps = Buf()
    P.op("pool", X.memset(eps_ln, LN_EPS), writes=[b_eps])
    P.op("pool", X.memset(eps_rms, RMS_EPS), writes=[b_eps])

    def psbufs(n):
        return [Buf() for _ in range(n)]

    def pbufs(n):
        return [Buf(excl=True) for _ in range(n)]

    def load_x(blk, Xsrc, xt, b_xt):
        dma("sp", xt, Xsrc[blk * 512:(blk + 1) * 512, :].rearrange("(t p) d -> p t d", p=128), writes=[b_xt])

    def make_xT(xt, b_xt, xT, b_xT, pst, b_pst, cnt):
        for c in range(8):
            j = cnt[0] % len(pst); cnt[0] += 1
            for t in range(4):
                P.op("pe", X.transpose(out=pst[j][:, t * 128:(t + 1) * 128], in_=xt[:, t, c * 128:(c + 1) * 128], identity=ident),
                     reads=[b_xt, b_ident], writes=[b_pst[j]])
            evac("act" if c % 2 == 0 else "dve", xT[:, c, :], pst[j], [b_pst[j]], [b_xT])

    def ln_stage1(z, b_z, gi, tmp):
        st, mv, sc, b_st = tmp
        for hh in range(2):
            P.op("dve", X.bn_stats(out=st[:, hh, :], in_=z[:, hh * 512:(hh + 1) * 512]), reads=[b_z], writes=[b_st])
        P.op("dve", X.bn_aggr(out=mv, in_=st), reads=[b_st], writes=[b_st])
        P.op("act", X.activation(out=sc[:, 0:1], in_=mv[:, 1:2], func=AF.Ln, bias=eps_ln[:, 0:1]), reads=[b_st, b_eps], writes=[b_st])
        P.op("act", X.activation(out=sc[:, 0:1], in_=sc[:, 0:1], func=AF.Exp, scale=-0.5), reads=[b_st], writes=[b_st])

    def ln_stage2(z, b_z, gi, tmp):
        st, mv, sc, b_st = tmp
        P.op("dve", X.scalar_tensor_tensor(out=sc[:, 1:2], in0=mv[:, 0:1], scalar=-1.0, in1=sc[:, 0:1], op0=ALU.mult, op1=ALU.mult), reads=[b_st], writes=[b_st])
        P.op("act", X.activation(out=z, in_=z, func=AF.Identity, scale=sc[:, 0:1], bias=sc[:, 1:2]), reads=[b_z, b_st], writes=[b_z])

    def ln_stage3(z, b_z, gi, Xdst, rows):
        P.op("dve", X.tensor_tensor(out=z[:, 0:512], in0=z[:, 0:512], in1=lng[:, gi, 0:512], op=ALU.mult), reads=[b_z, b_lng], writes=[b_z])
        P.op("pool", X.tensor_tensor(out=z[:, 512:1024], in0=z[:, 512:1024], in1=lng[:, gi, 512:1024], op=ALU.mult), reads=[b_z, b_lng], writes=[b_z])
        P.op("pool", X.tensor_tensor(out=z, in0=z, in1=lng[:, gi + 1, :], op=ALU.add), reads=[b_z, b_lng], writes=[b_z])
        dma("sp", Xdst[rows, :], z, reads=[b_z])

    def load_ln_params(g_d, b_d, layer, gi):
        dma("sp", lng[:, gi, :], g_d[layer:layer + 1, :].partition_broadcast(128) if False else g_d[layer:layer + 1, :].broadcast_to([128, D]), writes=[b_lng])
        dma("sp", lng[:, gi + 1, :], b_d[layer:layer + 1, :].broadcast_to([128, D]), writes=[b_lng])

    def load_w_cast(dst, src_rows_by_cols, ncols, b):
        C = dst.shape[1]
        step = 1024
        pairs = []
        for c in range(C):
            for c0 in range(0, ncols, step):
                c1 = min(ncols, c0 + step)
                pairs.append((dst[:, c, c0:c1], src_rows_by_cols[c * 128:(c + 1) * 128, c0:c1]))
        GR = 8
        for g0 in range(0, len(pairs), GR):
            grp = pairs[g0:g0 + GR]
            P.op("poolq", (lambda grp: (lambda e: [e.dma_start(out=o, in_=i) for (o, i) in grp]))(grp), writes=[Buf()], ndma=len(grp))
        P.wgroups.setdefault(id(b), []).extend(P.streams["pool"][-((len(pairs) + GR - 1) // GR):])
        b.w = None

    def load_w_colblocks(dsts_srcs, ncols, cb=512):
        out = [[] for _ in dsts_srcs]
        for c0 in range(0, ncols, cb):
            c1 = min(ncols, c0 + cb)
            for wi, (dst, src) in enumerate(dsts_srcs):
                C = dst.shape[1]
                grp = [(dst[:, c, c0:c1], src[c * 128:(c + 1) * 128, c0:c1]) for c in range(C)]
                b = Buf()
                P.op("poolq", (lambda grp: (lambda e: [e.dma_start(out=o, in_=i) for (o, i) in grp]))(grp), writes=[b], ndma=len(grp))
                out[wi].append((c1, b))
        return out

    def colbuf(blist, c_hi):
        for (c1, b) in blist:
            if c_hi <= c1:
                return b
        raise AssertionError

    def attention_full(QT_d, KT_d, V_d, R, mask_d, orow0):
        A.mark()
        mask32 = A.alloc([128], F32); mask = A.alloc([128], BF16); b_mask = Buf()
        dma("sp", mask32, mask_d, writes=[b_mask])
        P.op("dve", X.tensor_copy(out=mask, in_=mask32), reads=[b_mask], writes=[b_mask])
        sets = []
        for i in range(2):
            qt = A.alloc([S], BF16); kt_ = A.alloc([S], BF16); va = A.alloc([NT, 128], BF16)
            bq, bk, bv = Buf(), Buf(), Buf()
            P.op("pool", X.memset(va[:, :, 64:128], 1.0), writes=[bv])
            sets.append((qt, kt_, va, bq, bk, bv))
        NPB = 3
        PTs = [A.alloc([1024], BF16) for _ in range(NPB)]; b_PT = psbufs(NPB)
        Rc = [A.alloc([512], F32) for _ in range(2)]; b_Rc = psbufs(2)
        On = [A.alloc([512], BF16) for _ in range(2)]; b_On = psbufs(2)
        b_S = pbufs(NPB); b_O = pbufs(2)
        Sp = PP[0:3]; Ob = [PP[3][:, 0:512], PP[3][:, 512:1024]]

        def load_head(h):
            qt, kt_, va, bq, bk, bv = sets[h % 2]
            dma("sp", qt[0:R, :], QT_d[h], writes=[bq])
            dma("sp", kt_[0:R, :], KT_d[h], writes=[bk])
            dma("sp", va[:, :, 0:64], V_d[:, h, :].rearrange("(k p) e -> p k e", p=128), writes=[bv])
        load_head(0)
        gi = [0]
        for h in range(8):
            if h + 1 < 8:
                load_head(h + 1)
            qt, kt_, va, bq, bk, bv = sets[h % 2]
            pairs = [(Q, kp) for Q in range(NB) for kp in range(2 * Q + 2)]
            base = gi[0]

            def col0(Q, kt):
                return 0 if kt < 4 * Q else 128 * (kt - 4 * Q)

            def qk(i):
                Q, kp = pairs[i]; j = (base + i) % NPB
                for u in range(2):
                    kt = 2 * kp + u; n0 = col0(Q, kt); diag = kt >= 4 * Q
                    Sv = Sp[j][:, u * 512:(u + 1) * 512]
                    if not diag:
                        P.op("pe", X.matmul(Sv[:, 0:512], lhsT=kt_[0:R, kt * 128:(kt + 1) * 128], rhs=qt[0:R, Q * 512:(Q + 1) * 512], start=True, stop=True),
                             reads=[bq, bk], writes=[b_S[j]])
                    else:
                        if n0 + 128 < 512:
                            P.op("pe", X.matmul(Sv[:, n0 + 128:512], lhsT=kt_[0:R, kt * 128:(kt + 1) * 128], rhs=qt[0:R, Q * 512 + n0 + 128:(Q + 1) * 512], start=True, stop=True),
                                 reads=[bq, bk], writes=[b_S[j]])
                        P.op("pe", X.matmul(Sv[:, n0:n0 + 128], lhsT=kt_[0:R, kt * 128:(kt + 1) * 128], rhs=qt[0:R, Q * 512 + n0:Q * 512 + n0 + 128], start=True, stop=False),
                             reads=[bq, bk], writes=[b_S[j]])
                        P.op("pe", X.matmul(Sv[:, n0:n0 + 128], lhsT=ident_bf, rhs=mask, start=False, stop=True),
                             reads=[b_ident, b_mask], writes=[b_S[j]])

            def ex(i):
                Q, kp = pairs[i]; j = (base + i) % NPB
                if 2 * kp + 1 < 4 * Q:
                    P.op("act", X.activation(out=PTs[j], in_=Sp[j], func=AF.Exp), reads=[b_S[j]], writes=[b_PT[j]])
                else:
                    for u in range(2):
                        n0 = col0(Q, 2 * kp + u)
                        P.op("act", X.activation(out=PTs[j][:, u * 512 + n0:(u + 1) * 512], in_=Sp[j][:, u * 512 + n0:(u + 1) * 512], func=AF.Exp), reads=[b_S[j]], writes=[b_PT[j]])

            def pv(i):
                Q, kp = pairs[i]; j = (base + i) % NPB; ob = Q % 2
                for u in range(2):
                    kt = 2 * kp + u; n0 = col0(Q, kt)
                    P.op("pe", X.matmul(Ob[ob][:, n0:512], lhsT=va[:, kt, :], rhs=PTs[j][:, u * 512 + n0:(u + 1) * 512], start=(kt == 0), stop=(kt == 4 * Q + 3)),
                         reads=[bv, b_PT[j]], writes=[b_O[ob]])

            def norm(Q):
                ob = Q % 2
                P.op("dve", X.reciprocal(out=Rc[ob][0:64, :], in_=Ob[ob][64:128, :]), reads=[b_O[ob]], writes=[b_Rc[ob]])
                P.op("dve", X.tensor_tensor(out=On[ob][0:64, :], in0=Ob[ob][0:64, :], in1=Rc[ob][0:64, :], op=ALU.mult), reads=[b_O[ob], b_Rc[ob]], writes=[b_On[ob]])
                dma("sp", OT[orow0 + h * 64:orow0 + (h + 1) * 64, Q * 512:(Q + 1) * 512], On[ob][0:64, :], reads=[b_On[ob]])
            n = len(pairs)
            qk(0)
            if n > 1:
                qk(1)
            for i in range(n):
                if i + 2 < n:
                    qk(i + 2)
                ex(i)
                pv(i)
                Q, kp = pairs[i]
                if kp == 2 * Q + 1:
                    norm(Q)
            gi[0] += n
        A.release()
        P.barrier()

    def attention_band(QT_d, KT_d, V_d, nkv, omax, bhi, b_bias, orow0, sink=None, b_sink=None):
        A.mark()
        G = 8 // nkv
        sets = []
        for i in range(2):
            qt = A.alloc([S], BF16); bq = Buf()
            sets.append((qt, bq))
        ksets = []
        for i in range(2):
            kt_ = A.alloc([S], BF16); va = A.alloc([NT, 128], BF16); bk, bv = Buf(), Buf()
            P.op("pool", X.memset(va[:, :, 64:128], 1.0), writes=[bv])
            ksets.append((kt_, va, bk, bv))
        NSB = 6; LA = 4
        PTs = [A.alloc([512], BF16) for _ in range(NSB)]; b_PT = psbufs(NSB)
        Rc = [A.alloc([512], F32) for _ in range(2)]; b_Rc = psbufs(2)
        On = [A.alloc([512], BF16) for _ in range(2)]; b_On = psbufs(2)
        b_S = pbufs(NSB); b_O = pbufs(2)
        Sb = PS[0:NSB]; Ob = PS[6:8]

        def load_q(h):
            qt, bq = sets[h % 2]
            dma("sp", qt[0:64, :], QT_d[h], writes=[bq])

        def load_kv(g):
            kt_, va, bk, bv = ksets[g % 2]
            dma("sp", kt_[0:64, :], KT_d[g], writes=[bk])
            dma("sp", va[:, :, 0:64], V_d[:, g, :].rearrange("(k p) e -> p k e", p=128), writes=[bv])
        load_q(0); load_kv(0)
        gi = [0]
        for h in range(8):
            g = h // G
            if h + 1 < 8:
                load_q(h + 1)
                if (h + 1) // G != g:
                    load_kv((h + 1) // G)
            qt, bq = sets[h % 2]
            kt_, va, bk, bv = ksets[g % 2]
            steps = []
            for Q in range(NB):
                kts = list(range(max(0, 4 * Q - omax), 4 * Q + 4))
                for kt in kts:
                    m0 = max(kt, 4 * Q); m1 = min(kt + omax, 4 * Q + 3)
                    steps.append((Q, kt, m0, m1, kt == kts[0], kt == kts[-1]))
            base = gi[0]

            def qk(i):
                Q, kt, m0, m1, first, last = steps[i]; j = (base + i) % NSB
                c0 = (m0 - 4 * Q) * 128; c1 = (m1 - 4 * Q + 1) * 128
                o0 = m0 - kt; o1 = m1 - kt
                P.op("pe", X.matmul(Sb[j][:, c0:c1], lhsT=kt_[0:64, kt * 128:(kt + 1) * 128], rhs=qt[0:64, Q * 512 + c0:Q * 512 + c1], start=True, stop=False),
                     reads=[bq, bk], writes=[b_S[j]])
                P.op("pe", X.matmul(Sb[j][:, c0:c1].rearrange("p (o q) -> p o q", q=128), lhsT=ident_bf, rhs=bhi[:, h, o0:o1 + 1, :], start=False, stop=True),
                     reads=[b_ident, b_bias], writes=[b_S[j]])

            def ex(i):
                Q, kt, m0, m1, first, last = steps[i]; j = (base + i) % NSB
                c0 = (m0 - 4 * Q) * 128; c1 = (m1 - 4 * Q + 1) * 128
                P.op("act", X.activation(out=PTs[j][:, c0:c1], in_=Sb[j][:, c0:c1], func=AF.Exp), reads=[b_S[j]], writes=[b_PT[j]])

            def pv(i):
                Q, kt, m0, m1, first, last = steps[i]; j = (base + i) % NSB; ob = Q % 2
                c0 = (m0 - 4 * Q) * 128; c1 = (m1 - 4 * Q + 1) * 128
                P.op("pe", X.matmul(Ob[ob][:, c0:c1], lhsT=va[:, kt, :], rhs=PTs[j][:, c0:c1], start=first, stop=last, skip_group_check=True),
                     reads=[bv, b_PT[j]], writes=[b_O[ob]])

            def norm(Q):
                ob = Q % 2
                if sink is not None:
                    P.op("dve", X.tensor_scalar(out=Rc[ob][0:64, :], in0=Ob[ob][64:128, :], scalar1=sink[64:128, h:h + 1], scalar2=None, op0=ALU.add), reads=[b_O[ob], b_sink], writes=[b_Rc[ob]])
                    P.op("act", X.activation(out=Rc[ob][0:64, :], in_=Rc[ob][0:64, :], func=AF.Ln), reads=[b_Rc[ob]], writes=[b_Rc[ob]])
                    P.op("act", X.activation(out=Rc[ob][0:64, :], in_=Rc[ob][0:64, :], func=AF.Exp, scale=-1.0), reads=[b_Rc[ob]], writes=[b_Rc[ob]])
                else:
                    P.op("dve", X.tensor_copy(out=Rc[ob][0:64, :], in_=Ob[ob][64:128, :]), reads=[b_O[ob]], writes=[b_Rc[ob]])
                    P.op("act", X.activation(out=Rc[ob][0:64, :], in_=Rc[ob][0:64, :], func=AF.Ln), reads=[b_Rc[ob]], writes=[b_Rc[ob]])
                    P.op("act", X.activation(out=Rc[ob][0:64, :], in_=Rc[ob][0:64, :], func=AF.Exp, scale=-1.0), reads=[b_Rc[ob]], writes=[b_Rc[ob]])
                P.op("dve", X.tensor_tensor(out=On[ob][0:64, :], in0=Ob[ob][0:64, :], in1=Rc[ob][0:64, :], op=ALU.mult), reads=[b_O[ob], b_Rc[ob]], writes=[b_On[ob]])
                dma("sp", OT[orow0 + h * 64:orow0 + (h + 1) * 64, Q * 512:(Q + 1) * 512], On[ob][0:64, :], reads=[b_On[ob]])
            for i in range(min(LA, len(steps))):
                qk(i)
            pending = []
            for i in range(len(steps)):
                if i + LA < len(steps):
                    qk(i + LA)
                ex(i)
                pv(i)
                while pending and pending[0][0] <= i:
                    norm(pending.pop(0)[1])
                if steps[i][5]:
                    pending.append((i + 3, steps[i][0]))
            for _, Qp in pending:
                norm(Qp)
            gi[0] += len(steps)
        A.release()
        P.barrier()

    def outproj_ln(Wo, b_Wo, Xsrc, Xdst, layer):
        A.mark()
        load_ln_params(ln1_g, ln1_b, layer, 0)
        NZ = 5
        tmps = [(A.alloc([2, 6], F32), A.alloc([2], F32), A.alloc([2], F32), Buf()) for _ in range(NZ)]
        xts = [A.alloc([4, D], F32) for _ in range(3)]; b_xt = psbufs(3)
        ots = [A.alloc([8, 512], BF16) for _ in range(3)]; b_ot = psbufs(3)
        zs = [A.alloc([D], F32) for _ in range(NZ)]; b_z = psbufs(NZ)
        b_ps = pbufs(4); pi = [0]

        def loads(blk):
            dma("sp", xts[blk % 3], Xsrc[blk * 512:(blk + 1) * 512, :].rearrange("(t p) d -> p t d", p=128), writes=[b_xt[blk % 3]])
            dma("sp", ots[blk % 3], OT[:, blk * 512:(blk + 1) * 512].rearrange("(c p) s -> p c s", p=128), writes=[b_ot[blk % 3]])

        def s1(i):
            blk, t = divmod(i, 4)
            if t == 0 and blk + 2 < NB:
                loads(blk + 2)
            xt, bx = xts[blk % 3], b_xt[blk % 3]; ot, bo = ots[blk % 3], b_ot[blk % 3]
            z, bz = zs[i % NZ], b_z[i % NZ]
            for hh in range(2):
                pj = pi[0] % 4; pi[0] += 1
                for c in range(8):
                    P.op("pe", X.matmul(PS[pj], lhsT=ot[:, c, t * 128:(t + 1) * 128], rhs=Wo[:, c, hh * 512:(hh + 1) * 512], start=(c == 0), stop=(c == 7)),
                         reads=[bo, b_Wo], writes=[b_ps[pj]])
                P.op("dve", X.scalar_tensor_tensor(out=z[:, hh * 512:(hh + 1) * 512], in0=xt[:, t, hh * 512:(hh + 1) * 512], scalar=ALPHA, in1=PS[pj], op0=ALU.mult, op1=ALU.add),
                     reads=[bx, b_ps[pj]], writes=[bz])
            ln_stage1(z, bz, 0, tmps[i % NZ])

        def s2(i):
            ln_stage2(zs[i % NZ], b_z[i % NZ], 0, tmps[i % NZ])

        def s3(i):
            blk, t = divmod(i, 4)
            ln_stage3(zs[i % NZ], b_z[i % NZ], 0, Xdst, slice(blk * 512 + t * 128, blk * 512 + (t + 1) * 128))
        loads(0); loads(1)
        s1(0); s1(1); s2(0)
        for i in range(NT):
            if i + 2 < NT:
                s1(i + 2)
            if i + 1 < NT:
                s2(i + 1)
            s3(i)
        A.release()
        P.barrier()

    def ffn(layer, Xsrc, Xdst):
        A.mark()
        NFA = 14
        Wd_a = A.alloc([NFA, D], BF16)
        A.mark()
        Wg = A.alloc([8, DFF], BF16); Wu = A.alloc([8, DFF], BF16)
        bl_Wg, bl_Wu = load_w_colblocks([(Wg, w_gate[layer]), (Wu, w_up[layer])], DFF, cb=256)
        b_Wd = []
        for f0 in range(0, NFA, 2):
            grp = [(Wd_a[:, f, :], w_down[layer][f * 128:(f + 1) * 128, :]) for f in (f0, f0 + 1)]
            b = Buf()
            P.op("poolq", (lambda grp: (lambda e: [e.dma_start(out=o, in_=i) for (o, i) in grp]))(grp), writes=[b], ndma=2)
            b_Wd += [b, b]
        xts = [A.alloc([4, D], F32)] * 2; b_xt = [Buf()] * 2
        xTs = [A.alloc([8, 512], BF16) for _ in range(2)]; b_xT = psbufs(2)
        sgs = [A.alloc([512], F32) for _ in range(2)]; b_sg = psbufs(2)
        hts = [A.alloc([512], BF16) for _ in range(4)]; b_ht = psbufs(4)
        pst = PS[0:2]; b_pst = pbufs(2); cnt = [0]
        b_g = pbufs(2); b_u = pbufs(2); fi = 0
        load_x(0, Xsrc, xts[0], b_xt[0])
        for blk in range(NB):
            xt, bx = xts[blk % 2], b_xt[blk % 2]; xT, bxT = xTs[blk % 2], b_xT[blk % 2]
            make_xT(xt, bx, xT, bxT, pst, b_pst, cnt)
            if blk + 1 < NB:
                load_x(blk + 1, Xsrc, xts[(blk + 1) % 2], b_xt[(blk + 1) % 2])
            for f in range(NF):
                j = fi % 2; hj = fi % 4; fi += 1
                pg, pu = PS[2 + j], PS[4 + j]
                for c in range(8):
                    P.op("pe", X.matmul(pg, lhsT=Wg[:, c, f * 128:(f + 1) * 128], rhs=xT[:, c, :], start=(c == 0), stop=(c == 7)), reads=[colbuf(bl_Wg, (f + 1) * 128), bxT], writes=[b_g[j]])
                for c in range(8):
                    P.op("pe", X.matmul(pu, lhsT=Wu[:, c, f * 128:(f + 1) * 128], rhs=xT[:, c, :], start=(c == 0), stop=(c == 7)), reads=[colbuf(bl_Wu, (f + 1) * 128), bxT], writes=[b_u[j]])
                P.op("act", X.activation(out=sgs[j], in_=pg, func=AF.Silu), reads=[b_g[j]], writes=[b_sg[j]])
                P.op("dve", X.tensor_tensor(out=hts[hj], in0=pu, in1=sgs[j], op=ALU.mult), reads=[b_u[j], b_sg[j]], writes=[b_ht[hj]])
                dma("sp", HT[f * 128:(f + 1) * 128, blk * 512:(blk + 1) * 512], hts[hj], reads=[b_ht[hj]])
        A.release()
        P.barrier()
        A.mark()
        Wd_b = A.alloc([NF - NFA, D], BF16)
        for f0 in range(NFA, NF, 2):
            grp = [(Wd_b[:, f - NFA, :], w_down[layer][f * 128:(f + 1) * 128, :]) for f in (f0, f0 + 1)]
            b = Buf()
            P.op("poolq", (lambda grp: (lambda e: [e.dma_start(out=o, in_=i) for (o, i) in grp]))(grp), writes=[b], ndma=2)
            b_Wd += [b, b]

        def Wd_of(f):
            return Wd_a[:, f] if f < NFA else Wd_b[:, f - NFA]
        load_ln_params(ln2_g, ln2_b, layer, 2)
        NZ = 5
        tmps = [(A.alloc([2, 6], F32), A.alloc([2], F32), A.alloc([2], F32), Buf()) for _ in range(NZ)]
        xts = [A.alloc([4, D], F32) for _ in range(2)]; b_xt = psbufs(2)
        hbs = [A.alloc([NF, 512], BF16) for _ in range(2)]; b_hb = psbufs(2)
        zs = [A.alloc([D], F32) for _ in range(NZ)]; b_z = psbufs(NZ)
        b_ps = pbufs(4); pi = [0]

        def loads(blk):
            dma("sp", xts[blk % 2], Xsrc[blk * 512:(blk + 1) * 512, :].rearrange("(t p) d -> p t d", p=128), writes=[b_xt[blk % 2]])
            dma("sp", hbs[blk % 2], HT[:, blk * 512:(blk + 1) * 512].rearrange("(f p) s -> p f s", p=128), writes=[b_hb[blk % 2]])

        def s1(i):
            blk, t = divmod(i, 4)
            if t == 0 and blk + 1 < NB:
                loads(blk + 1)
            xt, bx = xts[blk % 2], b_xt[blk % 2]; hb, bh = hbs[blk % 2], b_hb[blk % 2]
            z, bz = zs[i % NZ], b_z[i % NZ]
            for hh in range(2):
                pj = pi[0] % 4; pi[0] += 1
                for f in range(NF):
                    P.op("pe", X.matmul(PS[pj], lhsT=hb[:, f, t * 128:(t + 1) * 128], rhs=Wd_of(f)[:, hh * 512:(hh + 1) * 512], start=(f == 0), stop=(f == NF - 1)),
                         reads=[bh, b_Wd[f]], writes=[b_ps[pj]])
                P.op("dve", X.scalar_tensor_tensor(out=z[:, hh * 512:(hh + 1) * 512], in0=xt[:, t, hh * 512:(hh + 1) * 512], scalar=ALPHA, in1=PS[pj], op0=ALU.mult, op1=ALU.add),
                     reads=[bx, b_ps[pj]], writes=[bz])
            ln_stage1(z, bz, 2, tmps[i % NZ])

        def s2(i):
            ln_stage2(zs[i % NZ], b_z[i % NZ], 2, tmps[i % NZ])

        def s3(i):
            blk, t = divmod(i, 4)
            ln_stage3(zs[i % NZ], b_z[i % NZ], 2, Xdst, slice(blk * 512 + t * 128, blk * 512 + (t + 1) * 128))
        loads(0)
        s1(0); s1(1); s2(0)
        for i in range(NT):
            if i + 2 < NT:
                s1(i + 2)
            if i + 1 < NT:
                s2(i + 1)
            s3(i)
        A.release()
        P.barrier()
        A.release()

    def inproj_ab(Xsrc):
        A.mark()
        Win = A.alloc([8, 1440], BF16); b_Win = Buf()
        load_w_cast(Win, ab_w_in, 1440, b_Win)
        Wuq = A.alloc([3, 768], BF16); b_Wuq = Buf()
        load_w_cast(Wuq, ab_w_uq, 768, b_Wuq)
        Wukv = A.alloc([2, 1024], BF16); b_Wukv = Buf()
        load_w_cast(Wukv, ab_w_ukv, 1024, b_Wukv)
        Wkr_rot = A.alloc([8, 32], BF16); Wuq_rot = A.alloc([3, 8, 32], BF16); b_rot = Buf()
        P.op("act", X.mul(out=Wkr_rot[:, :, 0:16], in_=Win[:, :, 656:672], mul=-1.0), reads=[b_Win], writes=[b_rot])
        P.op("act", X.copy(out=Wkr_rot[:, :, 16:32], in_=Win[:, :, 640:656]), reads=[b_Win], writes=[b_rot])
        Wuq4 = Wuq.rearrange("p c (h e) -> p c h e", e=96)
        for c in range(3):
            P.op("act", X.mul(out=Wuq_rot[:, c, :, 0:16], in_=Wuq4[:, c, :, 80:96], mul=-1.0), reads=[b_Wuq], writes=[b_rot])
            P.op("act", X.copy(out=Wuq_rot[:, c, :, 16:32], in_=Wuq4[:, c, :, 64:80]), reads=[b_Wuq], writes=[b_rot])
        Wq_nope = A.alloc([3, 512], BF16); Wq_rope = A.alloc([3, 256], BF16); Wk_nope = A.alloc([2, 512], BF16); b_wc = Buf()
        Wukv4 = Wukv.rearrange("p c (h e) -> p c h e", e=128)
        for c in range(3):
            P.op("dve", X.tensor_copy(out=Wq_nope[:, c, :].rearrange("p (h e) -> p h e", e=64), in_=Wuq4[:, c, :, 0:64]), reads=[b_Wuq], writes=[b_wc])
            P.op("pool", X.tensor_copy(out=Wq_rope[:, c, :].rearrange("p (h e) -> p h e", e=32), in_=Wuq4[:, c, :, 64:96]), reads=[b_Wuq], writes=[b_wc])
        for c in range(2):
            P.op("dve", X.tensor_copy(out=Wk_nope[:, c, :].rearrange("p (h e) -> p h e", e=64), in_=Wukv4[:, c, :, 0:64]), reads=[b_Wukv], writes=[b_wc])
        Wq_rot = Wuq_rot.rearrange("p c h e -> p c (h e)")
        gn = A.alloc([5], F32); b_gn = Buf()
        dma("sp", gn[:, 0:3], ab_q_norm.rearrange("(c p) -> p c", p=128), writes=[b_gn])
        dma("sp", gn[:, 3:5], ab_kv_norm.rearrange("(c p) -> p c", p=128), writes=[b_gn])
        xts = [A.alloc([4, D], F32) for _ in range(2)]; b_xt = psbufs(2)
        xTs = [A.alloc([8, 512], BF16) for _ in range(2)]; b_xT = psbufs(2)
        cs = [A.alloc([2, 512], F32) for _ in range(2)]; b_cs = psbufs(2)
        c32 = A.alloc([5, 512], F32); b_c32 = Buf()
        sq = A.alloc([5, 512], BF16); b_sq = Buf()
        rbc = A.alloc([2, 512], F32); b_rbc = Buf()
        cn = A.alloc([5, 512], BF16); b_cn = Buf()
        NST = 6
        stg = [A.alloc([512], BF16) for _ in range(NST)]; b_stg = psbufs(NST)
        t1s = [A.alloc([512], F32) for _ in range(2)]; t2s = [A.alloc([512], F32) for _ in range(2)]; b_t = psbufs(2)
        vst = [A.alloc([4, 512], BF16) for _ in range(2)]; b_vst = psbufs(2)
        vss = [A.alloc([4, 128], BF16) for _ in range(2)]; b_vss = psbufs(2)
        pst = PS[0:2]; b_pst = pbufs(2); cnt = [0]
        b_ps = pbufs(6); pi = [0]; si = [0]; ti = [0]
        SC_MLA = 96.0 ** -0.5
        SC_SWA = 64.0 ** -0.5

        def nps():
            j = pi[0] % 6; pi[0] += 1
            return PS[2 + j], b_ps[j]

        def nstg():
            j = si[0] % NST; si[0] += 1
            return stg[j], b_stg[j]

        def fm_proj(ps, bps, M, wsl, W_b, rhs_of_c, nchunks, rhs_b):
            for c in range(nchunks):
                P.op("pe", X.matmul(ps[0:M, :], lhsT=wsl(c), rhs=rhs_of_c(c), start=(c == 0), stop=(c == nchunks - 1)), reads=[W_b, rhs_b], writes=[bps])

        def rope_combine(psA, bA, psB, bB, cst, bcs, scale, dst_dram, np_=32):
            j = ti[0] % 2; ti[0] += 1
            t1, t2, bt = t1s[j], t2s[j], b_t[j]
            P.op("dve", X.scalar_tensor_tensor(out=t1[0:np_, :], in0=psA[0:np_, :], scalar=float(scale), in1=cst[0:np_, 0, :], op0=ALU.mult, op1=ALU.mult), reads=[bA, bcs], writes=[bt])
            P.op("dve", X.scalar_tensor_tensor(out=t2[0:np_, :], in0=psB[0:np_, :], scalar=float(scale), in1=cst[0:np_, 1, :], op0=ALU.mult, op1=ALU.mult), reads=[bB, bcs], writes=[bt])
            s, bs = nstg()
            P.op("pool", X.tensor_tensor(out=s[0:np_, :], in0=t1[0:np_, :], in1=t2[0:np_, :], op=ALU.add), reads=[bt], writes=[bs])
            return s, bs

        for blk in range(NB):
            cols = slice(blk * 512, (blk + 1) * 512)
            xt, bx = xts[blk % 2], b_xt[blk % 2]; xT, bxT = xTs[blk % 2], b_xT[blk % 2]
            cst, bcs = cs[blk % 2], b_cs[blk % 2]
            if blk == 0:
                load_x(0, Xsrc, xts[0], b_xt[0])
            if blk + 1 < NB:
                load_x(blk + 1, Xsrc, xts[(blk + 1) % 2], b_xt[(blk + 1) % 2])
            make_xT(xt, bx, xT, bxT, pst, b_pst, cnt)
            P.op("sp", (lambda cst, cols: (lambda e: [e.dma_start(out=cst[32 * r:32 * r + 32], in_=rope_cs[:, :, cols].rearrange("a p s -> p a s")) for r in range(4)]))(cst, cols), writes=[bcs], ndma=4)
            if dbg and blk == 0:
                dxt = dsc("DBG_xT", [128, 8, 512], BF16); dwin = dsc("DBG_Win", [128, 8, 1440], BF16)
                dma("sp", dxt, xT, reads=[bxT]); dma("sp", dwin, Win, reads=[b_Win])
            if DBGSTOP == 2: continue
            for i in range(5):
                ps, bps = nps()
                fm_proj(ps, bps, 128, lambda c, i=i: Win[:, c, i * 128:(i + 1) * 128], b_Win, lambda c: xT[:, c, :], 8, bxT)
                P.op("dve", X.tensor_copy(out=c32[:, i, :], in_=ps), reads=[bps], writes=[b_c32])
                P.op("act", X.activation(out=sq[:, i, :], in_=ps, func=AF.Square), reads=[bps], writes=[b_sq])
            for (k, i0, n, R_) in ((0, 0, 3, 384.0), (1, 3, 2, 256.0)):
                ps, bps = nps()
                for i in range(n):
                    P.op("pe", X.matmul(ps, lhsT=ones_bf, rhs=sq[:, i0 + i, :], start=(i == 0), stop=(i == n - 1)), reads=[b_ones, b_sq], writes=[bps])
                P.op("act", X.activation(out=rbc[:, k, :], in_=ps, func=AF.Ln, scale=1.0 / R_, bias=eps_rms[:, 0:1]), reads=[bps, b_eps], writes=[b_rbc])
                P.op("act", X.activation(out=rbc[:, k, :], in_=rbc[:, k, :], func=AF.Exp, scale=-0.5), reads=[b_rbc], writes=[b_rbc])
                for i in range(n):
                    P.op("dve", X.scalar_tensor_tensor(out=cn[:, i0 + i, :], in0=c32[:, i0 + i, :], scalar=gn[:, i0 + i:i0 + i + 1], in1=rbc[:, k, :], op0=ALU.mult, op1=ALU.mult),
                         reads=[b_c32, b_gn, b_rbc], writes=[b_cn])
            if DBGSTOP == 3: continue
            psA, bA = nps(); psB, bB = nps()
            fm_proj(psA, bA, 32, lambda c: Win[:, c, 640:672], b_Win, lambda c: xT[:, c, :], 8, bxT)
            fm_proj(psB, bB, 32, lambda c: Wkr_rot[:, c, :], b_rot, lambda c: xT[:, c, :], 8, bxT)
            s, bs = rope_combine(psA, bA, psB, bB, cst, bcs, 1.0, None)
            for h in range(8):
                dma("sp", KTa[h, 0:32, cols], s[0:32, :], reads=[bs])
            if DBGSTOP == 4: continue
            for i in range(4):
                ps, bps = nps()
                fm_proj(ps, bps, 128, lambda c, i=i: Win[:, c, 672 + i * 128:672 + (i + 1) * 128], b_Win, lambda c: xT[:, c, :], 8, bxT)
                s, bs = nstg()
                evac("act" if i % 2 == 0 else "dve", s, ps, [bps], [bs], scale=SC_SWA)
                dma("sp", QTs[2 * i:2 * i + 2, :, cols].rearrange("h p s -> (h p) s"), s, reads=[bs])
            ps, bps = nps()
            fm_proj(ps, bps, 128, lambda c: Win[:, c, 1184:1312], b_Win, lambda c: xT[:, c, :], 8, bxT)
            s, bs = nstg()
            evac("act", s, ps, [bps], [bs])
            dma("sp", KTs[:, :, cols].rearrange("h p s -> (h p) s"), s, reads=[bs])
            ps, bps = nps()
            for t in range(4):
                for c in range(8):
                    P.op("pe", X.matmul(ps[:, t * 128:(t + 1) * 128], lhsT=xT[:, c, t * 128:(t + 1) * 128], rhs=Win[:, c, 1312:1440], start=(c == 0), stop=(c == 7)), reads=[bxT, b_Win], writes=[bps])
            vs_, bvs = vss[blk % 2], b_vss[blk % 2]
            evac("dve", vs_, ps.rearrange("p (t e) -> p t e", e=128), [bps], [bvs])
            dma("sp", Vs[cols].rearrange("(t p) g e -> p t (g e)", p=128), vs_, reads=[bvs])
            if DBGSTOP == 5: continue
            Wuq4_ = Wuq.rearrange("p c (h e) -> p c h e", e=96)
            for g in range(4):
                ps, bps = nps()
                fm_proj(ps, bps, 128, lambda c, g=g: Wq_nope[:, c, 128 * g:128 * g + 128], b_wc, lambda c: cn[:, c, :], 3, b_cn)
                s, bs = nstg()
                evac("act", s, ps, [bps], [bs], scale=SC_MLA)
                dma("sp", QTa[2 * g, 32:96, cols], s[0:64, :], reads=[bs])
                dma("sp", QTa[2 * g + 1, 32:96, cols], s[64:128, :], reads=[bs])
            for g in range(2):
                psA, bA = nps(); psB, bB = nps()
                fm_proj(psA, bA, 128, lambda c, g=g: Wq_rope[:, c, 128 * g:128 * g + 128], b_wc, lambda c: cn[:, c, :], 3, b_cn)
                fm_proj(psB, bB, 128, lambda c, g=g: Wq_rot[:, c, 128 * g:128 * g + 128], b_rot, lambda c: cn[:, c, :], 3, b_cn)
                s, bs = rope_combine(psA, bA, psB, bB, cst, bcs, SC_MLA, None, np_=128)
                for r in range(4):
                    dma("sp", QTa[4 * g + r, 0:32, cols], s[32 * r:32 * r + 32, :], reads=[bs])
            if DBGSTOP == 6: continue
            Wukv4_ = Wukv.rearrange("p c (h e) -> p c h e", e=128)
            for g in range(4):
                ps, bps = nps()
                fm_proj(ps, bps, 128, lambda c, g=g: Wk_nope[:, c, 128 * g:128 * g + 128], b_wc, lambda c: cn[:, 3 + c, :], 2, b_cn)
                s, bs = nstg()
                evac("dve" if g % 2 == 0 else "act", s, ps, [bps], [bs])
                dma("sp", KTa[2 * g, 32:96, cols], s[0:64, :], reads=[bs])
                dma("sp", KTa[2 * g + 1, 32:96, cols], s[64:128, :], reads=[bs])
            if DBGSTOP == 7: continue
            vt, bvt = vst[blk % 2], b_vst[blk % 2]
            for t in range(4):
                ps, bps = nps()
                for c in range(2):
                    P.op("pe", X.matmul(ps.rearrange("p (h e) -> p h e", e=64), lhsT=cn[:, 3 + c, t * 128:(t + 1) * 128], rhs=Wukv[:, c, :].rearrange("p (h e) -> p h e", e=128)[:, :, 64:128], start=(c == 0), stop=(c == 1)),
                         reads=[b_cn, b_Wukv], writes=[bps])
                evac("act" if t % 2 == 0 else "dve", vt[:, t, :], ps, [bps], [bvt])
            dma("sp", Va[cols].rearrange("(t p) h e -> p t (h e)", p=128), vt, reads=[bvt])
        A.release()
        P.barrier()

    def inproj_cd(Xsrc):
        A.mark()
        Win = A.alloc([8, 3080], BF16)
        bl_Win = load_w_colblocks([(Win, cd_w_in)], 3080, cb=440)[0]

        def b_Win_of(c_lo, c_hi):
            bs = []
            for (c1, b) in bl_Win:
                if c1 > c_lo and c1 - 440 < c_hi:
                    bs.append(b)
            return bs
        nb = A.alloc([1], F32); b_nb = Buf()
        dma("sp", nb[0:8, :], cd_b_forget.rearrange("(h o) -> h o", o=1), writes=[b_nb])
        P.op("dve", X.tensor_scalar(out=nb[0:8, :], in0=nb[0:8, :], scalar1=-1.0, scalar2=None, op0=ALU.mult), reads=[b_nb], writes=[b_nb])
        ones32 = A.alloc([512], F32); b_o32 = Buf()
        P.op("pool", X.memset(ones32, 1.0), writes=[b_o32])
        onesS = A.alloc([3, 512], BF16); b_oS = Buf()
        P.op("pool", X.memset(onesS[0:8], 1.0), writes=[b_oS])
        cum = A.alloc([S], F32); b_cum = Buf()
        xts = [A.alloc([4, D], F32)] * 2; b_xt = [Buf()] * 2
        xTs = [A.alloc([8, 512], BF16) for _ in range(2)]; b_xT = psbufs(2)
        NST = 6
        stg = [A.alloc([512], BF16) for _ in range(NST)]; b_stg = psbufs(NST)
        vst = [A.alloc([4, 512], BF16) for _ in range(2)] * 2; b_vst = psbufs(2) * 2
        ls = A.alloc([512], F32); b_ls = Buf()
        r1 = A.alloc([512], F32); r2 = A.alloc([512], F32); b_r = Buf()
        augk = [A.alloc([3, 512], BF16) for _ in range(2)]; augq = [A.alloc([3, 512], BF16) for _ in range(2)]; b_aug = psbufs(2)
        pst = PS[0:2]; b_pst = pbufs(2); cnt = [0]
        b_ps = pbufs(6); pi = [0]; si = [0]; vi = [0]
        SC = 64.0 ** -0.5

        def nps():
            j = pi[0] % 6; pi[0] += 1
            return PS[2 + j], b_ps[j]

        def nstg():
            j = si[0] % NST; si[0] += 1
            return stg[j], b_stg[j]

        for blk in range(NB):
            cols = slice(blk * 512, (blk + 1) * 512)
            xt, bx = xts[blk % 2], b_xt[blk % 2]; xT, bxT = xTs[blk % 2], b_xT[blk % 2]
            if blk == 0:
                load_x(0, Xsrc, xt, bx)
            make_xT(xt, bx, xT, bxT, pst, b_pst, cnt)
            if blk + 1 < NB:
                load_x(blk + 1, Xsrc, xt, bx)
            dma("sp", QTf[:, 67:70, blk * 512:(blk + 1) * 512], onesS[0:8], reads=[b_oS])
            dma("sp", KTf[:, 64:67, blk * 512:(blk + 1) * 512], onesS[0:8], reads=[b_oS])
            for (c0, dst, scale) in ((0, QTf, SC), (512, KTf, None), (1544, QTc, SC), (2056, KTc, None)):
                for i in range(4):
                    ps, bps = nps()
                    for c in range(8):
                        P.op("pe", X.matmul(ps, lhsT=Win[:, c, c0 + i * 128:c0 + (i + 1) * 128], rhs=xT[:, c, :], start=(c == 0), stop=(c == 7)), reads=b_Win_of(c0 + i * 128, c0 + (i + 1) * 128) + [bxT], writes=[bps])
                    s, bs = nstg()
                    evac("act" if i % 2 == 0 else "dve", s, ps, [bps], [bs], scale=scale)
                    dma("sp", dst[2 * i, 0:64, cols], s[0:64, :], reads=[bs])
                    dma("sp", dst[2 * i + 1, 0:64, cols], s[64:128, :], reads=[bs])
            for (c0, dst) in ((1024, Vf), (2568, Vc)):
                vt, bvt = vst[vi[0] % 4], b_vst[vi[0] % 4]; vi[0] += 1
                for t in range(4):
                    ps, bps = nps()
                    for c in range(8):
                        P.op("pe", X.matmul(ps, lhsT=xT[:, c, t * 128:(t + 1) * 128], rhs=Win[:, c, c0:c0 + 512], start=(c == 0), stop=(c == 7)), reads=[bxT] + b_Win_of(c0, c0 + 512), writes=[bps])
                    evac("act" if t % 2 == 0 else "dve", vt[:, t, :], ps, [bps], [bvt])
                dma("sp", dst[cols].rearrange("(t p) h e -> p t (h e)", p=128), vt, reads=[bvt])
            ps, bps = nps()
            for c in range(8):
                P.op("pe", X.matmul(ps[0:8, :], lhsT=Win[:, c, 1536:1544], rhs=xT[:, c, :], start=(c == 0), stop=(c == 7)), reads=b_Win_of(1536, 1544) + [bxT], writes=[bps])
            P.op("act", X.activation(out=ls[0:8, :], in_=ps[0:8, :], func=AF.Exp, scale=-1.0, bias=nb[0:8, 0:1]), reads=[bps, b_nb], writes=[b_ls])
            P.op("act", X.activation(out=ls[0:8, :], in_=ls[0:8, :], func=AF.Ln, bias=1.0), reads=[b_ls], writes=[b_ls])
            init = 0.0 if blk == 0 else cum[0:8, blk * 512 - 1:blk * 512]
            P.op("dve", X.tensor_tensor_scan(out=cum[0:8, cols], data0=ones32[0:8, :], data1=ls[0:8, :], initial=init, op0=ALU.mult, op1=ALU.add), reads=[b_ls, b_o32, b_cum], writes=[b_cum])
            ak, aq, ba = augk[blk % 2], augq[blk % 2], b_aug[blk % 2]
            P.op("dve", X.tensor_copy(out=ak[0:8, 0, :], in_=cum[0:8, cols]), reads=[b_cum], writes=[ba])
            P.op("dve", X.tensor_tensor(out=r1[0:8, :], in0=cum[0:8, cols], in1=ak[0:8, 0, :], op=ALU.subtract), reads=[b_cum, ba], writes=[b_r])
            P.op("dve", X.tensor_copy(out=ak[0:8, 1, :], in_=r1[0:8, :]), reads=[b_r], writes=[ba])
            P.op("dve", X.tensor_tensor(out=r2[0:8, :], in0=r1[0:8, :], in1=ak[0:8, 1, :], op=ALU.subtract), reads=[b_r, ba], writes=[b_r])
            P.op("dve", X.tensor_copy(out=ak[0:8, 2, :], in_=r2[0:8, :]), reads=[b_r], writes=[ba])
            P.op("dve", X.tensor_scalar(out=aq[0:8], in0=ak[0:8], scalar1=-1.0, scalar2=None, op0=ALU.mult), reads=[ba], writes=[ba])
            dma("sp", KTf[:, 67:70, cols], ak[0:8], reads=[ba])
            dma("sp", QTf[:, 64:67, cols], aq[0:8], reads=[ba])
        A.release()
        P.barrier()

    def prep_bias(shape, load_fn):
        bf = A.alloc(shape, BF16); b = Buf()
        f = A.alloc(shape, F32)
        load_fn(f, b)
        P.op("dve", X.tensor_copy(out=bf, in_=f), reads=[b], writes=[b])
        return bf, b

    inproj_ab(x_in)
    A.mark()
    Wo = A.alloc([8, D], BF16); b_Wo = Buf()
    abf, b_al = prep_bias([8, 2, 128], lambda f, b: dma("poolq", f, alibi_d, writes=[b]))
    sk = A.alloc([8], F32); b_sk = Buf()
    dma("poolq", sk, ab_sinks.rearrange("(o h) -> o h", o=1).broadcast_to([128, 8]), writes=[b_sk])
    P.op("act", X.activation(out=sk, in_=sk, func=AF.Exp), reads=[b_sk], writes=[b_sk])
    load_w_cast(Wo, ab_w_out, D, b_Wo)
    attention_full(QTa, KTa, Va, 96, mla_mask_d, 0)
    attention_band(QTs, KTs, Vs, 2, 1, abf, b_al, 512, sink=sk, b_sink=b_sk)
    outproj_ln(Wo, b_Wo, x_in, X1, 0)
    A.release()
    ffn(0, X1, X2)
    inproj_cd(X2)
    A.mark()
    Wo = A.alloc([8, D], BF16); b_Wo = Buf()

    def load_ck(f, b):
        cm = A.alloc([5, 128], F32)
        dma("poolq", f, ck_bias_d, writes=[b])
        dma("poolq", cm, ck_mask_d, writes=[b])
        for h in range(8):
            P.op("pool", X.tensor_tensor(out=f[:, h], in0=f[:, h], in1=cm, op=ALU.add), reads=[b], writes=[b])
    cbf, b_cb = prep_bias([8, 5, 128], load_ck)
    load_w_cast(Wo, cd_w_out, D, b_Wo)
    attention_full(QTf, KTf, Vf, 70, causal_mask_d, 0)
    attention_band(QTc, KTc, Vc, 8, 4, cbf, b_cb, 512)
    outproj_ln(Wo, b_Wo, X2, X3, 1)
    A.release()
    ffn(1, X3, y_out)
    P.emit()
    es.close()
    return nc


_CACHE = {}


def kernel(**inputs):
    import ml_dtypes
    hc = host_constants()
    idx = hc.pop("_ck_idx")
    rel = np.asarray(inputs["cd_rel_bias"], np.float32)[0]
    hc["ck_bias"] = np.ascontiguousarray(np.transpose(rel[idx], (0, 3, 1, 2)))
    if "nc" not in _CACHE:
        _CACHE["nc"] = build_nc()
    nc = _CACHE["nc"]
    x = np.asarray(inputs["x"], np.float32)
    shared = {
        "ab_w_in": inputs["ab_w_in"][0], "ab_q_norm": inputs["ab_q_norm"][0], "ab_w_uq": inputs["ab_w_uq"][0],
        "ab_kv_norm": inputs["ab_kv_norm"][0], "ab_w_ukv": inputs["ab_w_ukv"][0], "ab_sinks": inputs["ab_sinks"][0],
        "ab_w_out": inputs["ab_w_out"][0], "cd_w_in": inputs["cd_w_in"][0], "cd_b_forget": inputs["cd_b_forget"][0],
        "cd_w_out": inputs["cd_w_out"][0],
        "ln1_g": inputs["ln1_g"], "ln1_b": inputs["ln1_b"], "ln2_g": inputs["ln2_g"], "ln2_b": inputs["ln2_b"],
        "ffn_w_gate": inputs["ffn_w_gate"], "ffn_w_up": inputs["ffn_w_up"], "ffn_w_down": inputs["ffn_w_down"],
    }
    shared = {k: np.ascontiguousarray(np.asarray(v, np.float32)) for k, v in shared.items()}
    shared.update(hc)
    in_maps = [dict(shared, x=np.ascontiguousarray(x[b])) for b in range(8)]
    res = run_bass_kernel_spmd(nc, in_maps, core_ids=list(range(8)))
    return np.stack([np.asarray(r["y"], np.float32) for r in res.results], axis=0)
```

```python
import numpy as np
import concourse.bass as bass
import concourse.mybir as mybir
from concourse.bass_utils import run_bass_kernel_spmd
from contextlib import ExitStack

F32 = mybir.dt.float32
BF16 = mybir.dt.bfloat16
AF = mybir.ActivationFunctionType
ALU = mybir.AluOpType
AX = mybir.AxisListType

COMPUTE = ("pe", "act", "dve", "pool")
QUEUES = ("sp", "poolq")
NDMASEM = 12


class _Rec:
    def __init__(self, name):
        self.name = name

    def __call__(self, *a, **k):
        name = self.name
        return lambda e: getattr(e, name)(*a, **k)


class _Recorder:
    def __getattr__(self, name):
        return _Rec(name)


X = _Recorder()


class Buf:
    __slots__ = ("name", "w", "r", "rd", "excl")

    def __init__(self, name="", excl=False):
        self.name = name
        self.excl = excl
        self.w = None
        self.r = {}
        self.rd = []


class Op:
    __slots__ = ("eng", "fn", "deps", "sig", "sem", "val", "ndma", "isdma", "prev")


class Prog:
    def __init__(self, nc):
        self.nc = nc
        self.streams = {"pe": [], "act": [], "dve": [], "pool": [], "sp": []}
        self.barrier_deps = []
        self.since_barrier_dma = []
        self.last = {}
        self.wgroups = {}

    def _stream_of(self, eng):
        return "pool" if eng == "poolq" else eng

    def op(self, eng, fn, reads=(), writes=(), ndma=0):
        o = Op()
        o.eng = eng
        o.fn = fn
        o.ndma = ndma
        o.isdma = ndma > 0
        o.sig = False
        o.sem = None
        o.val = 0
        st = self._stream_of(eng)
        deps = []
        xr = [b for b in reads if b.excl]
        if xr:
            reads = [b for b in reads if not b.excl]
            writes = list(writes) + xr
        for b in reads:
            if b.w is not None:
                deps.append(b.w)
            deps.extend(self.wgroups.get(id(b), ()))
        for b in writes:
            if b.w is not None:
                deps.append(b.w)
            if id(b) in self.wgroups:
                deps.extend(self.wgroups.pop(id(b)))
            deps.extend(b.r.values())
            deps.extend(b.rd)
        deps.extend(self.barrier_deps)
        if eng == "pe":
            deps = [d for d in deps if d.isdma or d.eng != "pe"]
        seen = set()
        dd = []
        for d in deps:
            if id(d) not in seen and d is not o:
                seen.add(id(d))
                dd.append(d)
                d.sig = True
        o.deps = dd
        for b in reads:
            if o.isdma:
                b.rd.append(o)
            else:
                b.r[eng] = o
        for b in writes:
            b.w = o
            b.r = {}
            b.rd = []
        self.streams[st].append(o)
        if o.isdma:
            self.since_barrier_dma.append(o)
            o.sig = True
        else:
            self.last[eng] = o
        return o

    def barrier(self):
        deps = list(self.last.values()) + list(self.since_barrier_dma)
        for d in deps:
            d.sig = True
        self.barrier_deps = deps
        self.since_barrier_dma = []

    def emit(self, final_wait_eng="sp"):
        nc = self.nc
        for d in list(self.last.values()):
            d.sig = True
        with ExitStack() as es:
            sems = {}
            for e in COMPUTE:
                sems[e] = es.enter_context(nc.semaphore("s_" + e))
            dsems = {}
            for q in QUEUES:
                dsems[q] = [es.enter_context(nc.semaphore("d_%s%d" % (q, i))) for i in range(NDMASEM)]
            for st, ops in self.streams.items():
                cnt = 0
                qcnt = {q: 0 for q in QUEUES}
                qval = {q: [0] * NDMASEM for q in QUEUES}
                for o in ops:
                    if o.isdma:
                        k = qcnt[o.eng] % NDMASEM
                        qcnt[o.eng] += 1
                        o.sem = dsems[o.eng][k]
                        o.prev = qval[o.eng][k]
                        qval[o.eng][k] += 16 * o.ndma
                        o.val = qval[o.eng][k]
                    elif o.sig:
                        cnt += 1
                        o.sem = sems[o.eng]
                        o.val = cnt
            tail = list(self.barrier_deps) + list(self.last.values()) + list(self.since_barrier_dma)
            for d in tail:
                assert d.sig or d.isdma
            self.maxval = 0
            es.enter_context(nc.allow_non_contiguous_dma(reason='small strided param loads'))
            block = es.enter_context(nc.Block())
            handles = {"pe": "tensor", "act": "scalar", "dve": "vector", "pool": "gpsimd", "sp": "sync"}

            def run_stream(st, eng):
                waited = {}

                def wait(sem, val):
                    if val <= 0:
                        return
                    key = id(sem)
                    if waited.get(key, 0) >= val:
                        return
                    waited[key] = val
                    self.maxval = max(self.maxval, val)
                    eng.wait_ge(sem, val)

                for o in self.streams[st]:
                    for d in o.deps:
                        wait(d.sem, d.val)
                    if o.isdma:
                        wait(o.sem, o.prev)
                        ins = o.fn(eng)
                        assert len(ins) == o.ndma, (len(ins), o.ndma)
                        for i in ins:
                            i.then_inc(o.sem, 16)
                    else:
                        i = o.fn(eng)
                        if o.sig:
                            i.then_inc(o.sem, 1)
                if st == final_wait_eng:
                    for d in tail:
                        wait(d.sem, d.val)

            for st in self.streams:
                getattr(block, handles[st])(lambda eng, st=st: run_stream(st, eng))


class Arena:
    def __init__(self, nc, es, nbytes, name="arena"):
        self.t = es.enter_context(nc.sbuf_tensor(name, [128, nbytes // 4], F32))
        self.nbytes = nbytes
        self.off = 0
        self.marks = []

    def alloc(self, free_shape, dtype, parts=128, pbase=0):
        esz = 2 if dtype == BF16 else 4
        n = int(np.prod(free_shape))
        nb = (n * esz + 31) // 32 * 32
        assert self.off + nb <= self.nbytes, "SBUF arena overflow: %d + %d > %d" % (self.off, nb, self.nbytes)
        w0 = self.off // 4
        ap = self.t[:, w0:w0 + nb // 4]
        if dtype != F32:
            ap = ap.bitcast(dtype)
        ap = ap[:, 0:n]
        self.off += nb
        if len(free_shape) == 2:
            ap = ap.rearrange("p (a b) -> p a b", b=free_shape[1])
        elif len(free_shape) == 3:
            ap = ap.rearrange("p (a b c) -> p a b c", b=free_shape[1], c=free_shape[2])
        return ap

    def mark(self):
        self.marks.append(self.off)

    def release(self):
        self.off = self.marks.pop()


S = 4096
D = 1024
NT = 32
NB = 8
DFF = 2816
NF = 22
ALPHA = 4.0 ** 0.25
LN_EPS = 1e-5
RMS_EPS = 1e-6
NEG = -30000.0
import os as _os
DBGSTOP = int(_os.environ.get('DBGSTOP', '0'))


def host_constants():
    c = {}
    inv = (10000.0 ** (-np.arange(0, 32, 2, dtype=np.float32) / 32)).astype(np.float32)
    ang = np.arange(S, dtype=np.float32)[:, None] * inv[None, :]
    cos, sin = np.cos(ang).astype(np.float32), np.sin(ang).astype(np.float32)
    cs = np.zeros((2, 32, S), np.float32)
    cs[0, 0:16] = cos.T
    cs[0, 16:32] = cos.T
    cs[1, 0:16] = sin.T
    cs[1, 16:32] = sin.T
    c["rope_cs"] = cs
    k = np.arange(128)[:, None]
    q = np.arange(128)[None, :]
    c["mla_mask"] = np.where((k >= 64) & (q < 64), NEG, 0.0).astype(np.float32)
    c["causal_mask"] = np.where(k > q, NEG, 0.0).astype(np.float32)

    def band_mask(nof, L):
        m = np.zeros((128, nof, 128), np.float32)
        for o in range(nof):
            dc = 2 * o + (q >= 64).astype(np.int32) - (k >= 64).astype(np.int32)
            m[:, o, :] = np.where((dc >= 0) & (dc <= L), 0.0, NEG)
        return m
    slopes = np.exp2(-8.0 * np.arange(1, 9, dtype=np.float32) / 8).astype(np.float32)
    al = np.zeros((128, 8, 2, 128), np.float32)
    bm = band_mask(2, 2)
    for h in range(8):
        for o in range(2):
            dist = 128 * o + q - k
            al[:, h, o, :] = -slopes[h] * np.abs(dist).astype(np.float32) + bm[:, o, :]
    c["alibi"] = al
    c["ck_mask"] = band_mask(5, 8)
    idx = np.zeros((128, 5, 128), np.int64)
    for o in range(5):
        idx[:, o, :] = np.clip(128 * o + q - k, -63, 256) + 63
    c["_ck_idx"] = idx
    return c


def build_nc(dbg=False, nphase=99):
    nc = bass.Bass("TRN2", target_bir_lowering=False)

    def din(name, shape, dt=F32):
        return nc.dram_tensor(name, list(shape), dt, kind="ExternalInput").ap()
    skind = "ExternalOutput" if dbg else "Internal"

    def dsc(name, shape, dt):
        return nc.dram_tensor(name, list(shape), dt, kind=skind).ap()

    x_in = din("x", [S, D])
    ab_w_in = din("ab_w_in", [D, 1440]); ab_q_norm = din("ab_q_norm", [384]); ab_w_uq = din("ab_w_uq", [384, 768])
    ab_kv_norm = din("ab_kv_norm", [256]); ab_w_ukv = din("ab_w_ukv", [256, 1024]); ab_sinks = din("ab_sinks", [8])
    ab_w_out = din("ab_w_out", [D, D]); cd_w_in = din("cd_w_in", [D, 3080]); cd_b_forget = din("cd_b_forget", [8])
    cd_w_out = din("cd_w_out", [D, D])
    ln1_g = din("ln1_g", [2, D]); ln1_b = din("ln1_b", [2, D]); ln2_g = din("ln2_g", [2, D]); ln2_b = din("ln2_b", [2, D])
    w_gate = din("ffn_w_gate", [2, D, DFF]); w_up = din("ffn_w_up", [2, D, DFF]); w_down = din("ffn_w_down", [2, DFF, D])
    rope_cs = din("rope_cs", [2, 32, S]); mla_mask_d = din("mla_mask", [128, 128]); causal_mask_d = din("causal_mask", [128, 128])
    alibi_d = din("alibi", [128, 8, 2, 128]); ck_mask_d = din("ck_mask", [128, 5, 128]); ck_bias_d = din("ck_bias", [128, 8, 5, 128])
    y_out = nc.dram_tensor("y", [S, D], F32, kind="ExternalOutput").ap()

    QTa = dsc("QTa", [8, 96, S], BF16); KTa = dsc("KTa", [8, 96, S], BF16); Va = dsc("Va", [S, 8, 64], BF16)
    QTs = dsc("QTs", [8, 64, S], BF16); KTs = dsc("KTs", [2, 64, S], BF16); Vs = dsc("Vs", [S, 2, 64], BF16)
    QTf = dsc("QTf", [8, 70, S], BF16); KTf = dsc("KTf", [8, 70, S], BF16); Vf = dsc("Vf", [S, 8, 64], BF16)
    QTc = dsc("QTc", [8, 64, S], BF16); KTc = dsc("KTc", [8, 64, S], BF16); Vc = dsc("Vc", [S, 8, 64], BF16)
    OT = dsc("OT", [D, S], BF16)
    HT = dsc("HT", [DFF, S], BF16)
    X1 = dsc("X1", [S, D], F32); X2 = dsc("X2", [S, D], F32); X3 = dsc("X3", [S, D], F32)

    P = Prog(nc)
    es = ExitStack()
    A = Arena(nc, es, 176 * 1024)
    PP = [es.enter_context(nc.psum_tensor("pp%d" % i, [128, 1024], F32))[:, :] for i in range(4)]
    PS = []
    for i in range(4):
        PS += [PP[i][:, 0:512], PP[i][:, 512:1024]]

    def dma(q, out, in_, reads=(), writes=()):
        return P.op(q, lambda e: [e.dma_start(out=out, in_=in_)], reads, writes, ndma=1)

    def evac(eng, out, in_, reads, writes, scale=None):
        if eng == "act":
            if scale is None:
                return P.op("act", X.copy(out=out, in_=in_), reads, writes)
            return P.op("act", X.mul(out=out, in_=in_, mul=float(scale)), reads, writes)
        if scale is None:
            return P.op(eng, X.tensor_copy(out=out, in_=in_), reads, writes)
        return P.op(eng, X.tensor_scalar(out=out, in0=in_, scalar1=float(scale), scalar2=None, op0=ALU.mult), reads, writes)

    ident = A.alloc([128], F32); b_ident = Buf()
    P.op("pool", X.memset(ident, 1.0), writes=[b_ident])
    P.op("pool", X.affine_select(out=ident, in_=ident, pattern=[[-1, 128]], compare_op=ALU.is_equal, fill=0.0, base=0, channel_multiplier=1), reads=[b_ident], writes=[b_ident])
    ident_bf = A.alloc([128], BF16)
    P.op("dve", X.tensor_copy(out=ident_bf, in_=ident), reads=[b_ident], writes=[b_ident])
    ones_bf = A.alloc([128], BF16); b_ones = Buf()
    P.op("pool", X.memset(ones_bf, 1.0), writes=[b_ones])
    lng = A.alloc([4, D], F32); b_lng = Buf()
    eps_ln = A.alloc([1], F32); eps_rms = A.alloc([1], F32); b_eps = Buf()
    P.op("pool", X.memset(eps_ln, LN_EPS), writes=[b_eps])
    P.op("pool", X.memset(eps_rms, RMS_EPS), writes=[b_eps])

    def psbufs(n):
        return [Buf() for _ in range(n)]

    def pbufs(n):
        return [Buf(excl=True) for _ in range(n)]

    def load_x(blk, Xsrc, xt, b_xt):
        dma("sp", xt, Xsrc[blk * 512:(blk + 1) * 512, :].rearrange("(t p) d -> p t d", p=128), writes=[b_xt])

    def make_xT(xt, b_xt, xT, b_xT, pst, b_pst, cnt):
        for c in range(8):
            j = cnt[0] % len(pst); cnt[0] += 1
            for t in range(4):
                P.op("pe", X.transpose(out=pst[j][:, t * 128:(t + 1) * 128], in_=xt[:, t, c * 128:(c + 1) * 128], identity=ident),
                     reads=[b_xt, b_ident], writes=[b_pst[j]])
            evac("act" if c % 2 == 0 else "dve", xT[:, c, :], pst[j], [b_pst[j]], [b_xT])

    def ln_stage1(z, b_z, gi, tmp):
        st, mv, sc, b_st = tmp
        for hh in range(2):
            P.op("dve", X.bn_stats(out=st[:, hh, :], in_=z[:, hh * 512:(hh + 1) * 512]), reads=[b_z], writes=[b_st])
        P.op("dve", X.bn_aggr(out=mv, in_=st), reads=[b_st], writes=[b_st])
        P.op("act", X.activation(out=sc[:, 0:1], in_=mv[:, 1:2], func=AF.Ln, bias=eps_ln[:, 0:1]), reads=[b_st, b_eps], writes=[b_st])
        P.op("act", X.activation(out=sc[:, 0:1], in_=sc[:, 0:1], func=AF.Exp, scale=-0.5), reads=[b_st], writes=[b_st])

    def ln_stage2(z, b_z, gi, tmp):
        st, mv, sc, b_st = tmp
        P.op("dve", X.scalar_tensor_tensor(out=sc[:, 1:2], in0=mv[:, 0:1], scalar=-1.0, in1=sc[:, 0:1], op0=ALU.mult, op1=ALU.mult), reads=[b_st], writes=[b_st])
        P.op("act", X.activation(out=z, in_=z, func=AF.Identity, scale=sc[:, 0:1], bias=sc[:, 1:2]), reads=[b_z, b_st], writes=[b_z])

    def ln_stage3(z, b_z, gi, Xdst, rows):
        P.op("dve", X.tensor_tensor(out=z[:, 0:512], in0=z[:, 0:512], in1=lng[:, gi, 0:512], op=ALU.mult), reads=[b_z, b_lng], writes=[b_z])
        P.op("pool", X.tensor_tensor(out=z[:, 512:1024], in0=z[:, 512:1024], in1=lng[:, gi, 512:1024], op=ALU.mult), reads=[b_z, b_lng], writes=[b_z])
        P.op("pool", X.tensor_tensor(out=z, in0=z, in1=lng[:, gi + 1, :], op=ALU.add), reads=[b_z, b_lng], writes=[b_z])
        dma("sp", Xdst[rows, :], z, reads=[b_z])

    def load_ln_params(g_d, b_d, layer, gi):
        dma("sp", lng[:, gi, :], g_d[layer:layer + 1, :].partition_broadcast(128) if False else g_d[layer:layer + 1, :].broadcast_to([128, D]), writes=[b_lng])
        dma("sp", lng[:, gi + 1, :], b_d[layer:layer + 1, :].broadcast_to([128, D]), writes=[b_lng])

    def load_w_cast(dst, src_rows_by_cols, ncols, b):
        C = dst.shape[1]
        step = 1024
        pairs = []
        for c in range(C):
            for c0 in range(0, ncols, step):
                c1 = min(ncols, c0 + step)
                pairs.append((dst[:, c, c0:c1], src_rows_by_cols[c * 128:(c + 1) * 128, c0:c1]))
        GR = 8
        for g0 in range(0, len(pairs), GR):
            grp = pairs[g0:g0 + GR]
            P.op("poolq", (lambda grp: (lambda e: [e.dma_start(out=o, in_=i) for (o, i) in grp]))(grp), writes=[Buf()], ndma=len(grp))
        P.wgroups.setdefault(id(b), []).extend(P.streams["pool"][-((len(pairs) + GR - 1) // GR):])
        b.w = None

    def load_w_colblocks(dsts_srcs, ncols, cb=512):
        out = [[] for _ in dsts_srcs]
        for c0 in range(0, ncols, cb):
            c1 = min(ncols, c0 + cb)
            for wi, (dst, src) in enumerate(dsts_srcs):
                C = dst.shape[1]
                grp = [(dst[:, c, c0:c1], src[c * 128:(c + 1) * 128, c0:c1]) for c in range(C)]
                b = Buf()
                P.op("poolq", (lambda grp: (lambda e: [e.dma_start(out=o, in_=i) for (o, i) in grp]))(grp), writes=[b], ndma=len(grp))
                out[wi].append((c1, b))
        return out

    def colbuf(blist, c_hi):
        for (c1, b) in blist:
            if c_hi <= c1:
                return b
        raise AssertionError

    def attention_full(QT_d, KT_d, V_d, R, mask_d, orow0):
        A.mark()
        mask32 = A.alloc([128], F32); mask = A.alloc([128], BF16); b_mask = Buf()
        dma("sp", mask32, mask_d, writes=[b_mask])
        P.op("dve", X.tensor_copy(out=mask, in_=mask32), reads=[b_mask], writes=[b_mask])
        sets = []
        for i in range(2):
            qt = A.alloc([S], BF16); kt_ = A.alloc([S], BF16); va = A.alloc([NT, 128], BF16)
            bq, bk, bv = Buf(), Buf(), Buf()
            P.op("pool", X.memset(va[:, :, 64:128], 1.0), writes=[bv])
            sets.append((qt, kt_, va, bq, bk, bv))
        NPB = 3
        PTs = [A.alloc([1024], BF16) for _ in range(NPB)]; b_PT = psbufs(NPB)
        Rc = [A.alloc([512], F32) for _ in range(2)]; b_Rc = psbufs(2)
        On = [A.alloc([512], BF16) for _ in range(2)]; b_On = psbufs(2)
        b_S = pbufs(NPB); b_O = pbufs(2)
        Sp = PP[0:3]; Ob = [PP[3][:, 0:512], PP[3][:, 512:1024]]

        def load_head(h):
            qt, kt_, va, bq, bk, bv = sets[h % 2]
            dma("sp", qt[0:R, :], QT_d[h], writes=[bq])
            dma("sp", kt_[0:R, :], KT_d[h], writes=[bk])
            dma("sp", va[:, :, 0:64], V_d[:, h, :].rearrange("(k p) e -> p k e", p=128), writes=[bv])
        load_head(0)
        gi = [0]
        for h in range(8):
            if h + 1 < 8:
                load_head(h + 1)
            qt, kt_, va, bq, bk, bv = sets[h % 2]
            pairs = [(Q, kp) for Q in range(NB) for kp in range(2 * Q + 2)]
            base = gi[0]

            def col0(Q, kt):
                return 0 if kt < 4 * Q else 128 * (kt - 4 * Q)

            def qk(i):
                Q, kp = pairs[i]; j = (base + i) % NPB
                for u in range(2):
                    kt = 2 * kp + u; n0 = col0(Q, kt); diag = kt >= 4 * Q
                    Sv = Sp[j][:, u * 512:(u + 1) * 512]
                    if not diag:
                        P.op("pe", X.matmul(Sv[:, 0:512], lhsT=kt_[0:R, kt * 128:(kt + 1) * 128], rhs=qt[0:R, Q * 512:(Q + 1) * 512], start=True, stop=True),
                             reads=[bq, bk], writes=[b_S[j]])
                    else:
                        if n0 + 128 < 512:
                            P.op("pe", X.matmul(Sv[:, n0 + 128:512], lhsT=kt_[0:R, kt * 128:(kt + 1) * 128], rhs=qt[0:R, Q * 512 + n0 + 128:(Q + 1) * 512], start=True, stop=True),
                                 reads=[bq, bk], writes=[b_S[j]])
                        P.op("pe", X.matmul(Sv[:, n0:n0 + 128], lhsT=kt_[0:R, kt * 128:(kt + 1) * 128], rhs=qt[0:R, Q * 512 + n0:Q * 512 + n0 + 128], start=True, stop=False),
                             reads=[bq, bk], writes=[b_S[j]])
                        P.op("pe", X.matmul(Sv[:, n0:n0 + 128], lhsT=ident_bf, rhs=mask, start=False, stop=True),
                             reads=[b_ident, b_mask], writes=[b_S[j]])

            def ex(i):
                Q, kp = pairs[i]; j = (base + i) % NPB
                if 2 * kp + 1 < 4 * Q:
                    P.op("act", X.activation(out=PTs[j], in_=Sp[j], func=AF.Exp), reads=[b_S[j]], writes=[b_PT[j]])
                else:
                    for u in range(2):
                        n0 = col0(Q, 2 * kp + u)
                        P.op("act", X.activation(out=PTs[j][:, u * 512 + n0:(u + 1) * 512], in_=Sp[j][:, u * 512 + n0:(u + 1) * 512], func=AF.Exp), reads=[b_S[j]], writes=[b_PT[j]])

            def pv(i):
                Q, kp = pairs[i]; j = (base + i) % NPB; ob = Q % 2
                for u in range(2):
                    kt = 2 * kp + u; n0 = col0(Q, kt)
                    P.op("pe", X.matmul(Ob[ob][:, n0:512], lhsT=va[:, kt, :], rhs=PTs[j][:, u * 512 + n0:(u + 1) * 512], start=(kt == 0), stop=(kt == 4 * Q + 3)),
                         reads=[bv, b_PT[j]], writes=[b_O[ob]])

            def norm(Q):
                ob = Q % 2
                P.op("dve", X.reciprocal(out=Rc[ob][0:64, :], in_=Ob[ob][64:128, :]), reads=[b_O[ob]], writes=[b_Rc[ob]])
                P.op("dve", X.tensor_tensor(out=On[ob][0:64, :], in0=Ob[ob][0:64, :], in1=Rc[ob][0:64, :], op=ALU.mult), reads=[b_O[ob], b_Rc[ob]], writes=[b_On[ob]])
                dma("sp", OT[orow0 + h * 64:orow0 + (h + 1) * 64, Q * 512:(Q + 1) * 512], On[ob][0:64, :], reads=[b_On[ob]])
            n = len(pairs)
            qk(0)
            if n > 1:
                qk(1)
            for i in range(n):
                if i + 2 < n:
                    qk(i + 2)
                ex(i)
                pv(i)
                Q, kp = pairs[i]
                if kp == 2 * Q + 1:
                    norm(Q)
            gi[0] += n
        A.release()
        P.barrier()

    def attention_band(QT_d, KT_d, V_d, nkv, omax, bhi, b_bias, orow0, sink=None, b_sink=None):
        A.mark()
        G = 8 // nkv
        sets = []
        for i in range(2):
            qt = A.alloc([S], BF16); bq = Buf()
            sets.append((qt, bq))
        ksets = []
        for i in range(2):
            kt_ = A.alloc([S], BF16); va = A.alloc([NT, 128], BF16); bk, bv = Buf(), Buf()
            P.op("pool", X.memset(va[:, :, 64:128], 1.0), writes=[bv])
            ksets.append((kt_, va, bk, bv))
        NSB = 6; LA = 4
        PTs = [A.alloc([512], BF16) for _ in range(NSB)]; b_PT = psbufs(NSB)
        Rc = [A.alloc([512], F32) for _ in range(2)]; b_Rc = psbufs(2)
        On = [A.alloc([512], BF16) for _ in range(2)]; b_On = psbufs(2)
        b_S = pbufs(NSB); b_O = pbufs(2)
        Sb = PS[0:NSB]; Ob = PS[6:8]

        def load_q(h):
            qt, bq = sets[h % 2]
            dma("sp", qt[0:64, :], QT_d[h], writes=[bq])

        def load_kv(g):
            kt_, va, bk, bv = ksets[g % 2]
            dma("sp", kt_[0:64, :], KT_d[g], writes=[bk])
            dma("sp", va[:, :, 0:64], V_d[:, g, :].rearrange("(k p) e -> p k e", p=128), writes=[bv])
        load_q(0); load_kv(0)
        gi = [0]
        for h in range(8):
            g = h // G
            if h + 1 < 8:
                load_q(h + 1)
                if (h + 1) // G != g:
                    load_kv((h + 1) // G)
            qt, bq = sets[h % 2]
            kt_, va, bk, bv = ksets[g % 2]
            steps = []
            for Q in range(NB):
                kts = list(range(max(0, 4 * Q - omax), 4 * Q + 4))
                for kt in kts:
                    m0 = max(kt, 4 * Q); m1 = min(kt + omax, 4 * Q + 3)
                    steps.append((Q, kt, m0, m1, kt == kts[0], kt == kts[-1]))
            base = gi[0]

            def qk(i):
                Q, kt, m0, m1, first, last = steps[i]; j = (base + i) % NSB
                c0 = (m0 - 4 * Q) * 128; c1 = (m1 - 4 * Q + 1) * 128
                o0 = m0 - kt; o1 = m1 - kt
                P.op("pe", X.matmul(Sb[j][:, c0:c1], lhsT=kt_[0:64, kt * 128:(kt + 1) * 128], rhs=qt[0:64, Q * 512 + c0:Q * 512 + c1], start=True, stop=False),
                     reads=[bq, bk], writes=[b_S[j]])
                P.op("pe", X.matmul(Sb[j][:, c0:c1].rearrange("p (o q) -> p o q", q=128), lhsT=ident_bf, rhs=bhi[:, h, o0:o1 + 1, :], start=False, stop=True),
                     reads=[b_ident, b_bias], writes=[b_S[j]])

            def ex(i):
                Q, kt, m0, m1, first, last = steps[i]; j = (base + i) % NSB
                c0 = (m0 - 4 * Q) * 128; c1 = (m1 - 4 * Q + 1) * 128
                P.op("act", X.activation(out=PTs[j][:, c0:c1], in_=Sb[j][:, c0:c1], func=AF.Exp), reads=[b_S[j]], writes=[b_PT[j]])

            def pv(i):
                Q, kt, m0, m1, first, last = steps[i]; j = (base + i) % NSB; ob = Q % 2
                c0 = (m0 - 4 * Q) * 128; c1 = (m1 - 4 * Q + 1) * 128
                P.op("pe", X.matmul(Ob[ob][:, c0:c1], lhsT=va[:, kt, :], rhs=PTs[j][:, c0:c1], start=first, stop=last, skip_group_check=True),
                     reads=[bv, b_PT[j]], writes=[b_O[ob]])

            def norm(Q):
                ob = Q % 2
                if sink is not None:
                    P.op("dve", X.tensor_scalar(out=Rc[ob][0:64, :], in0=Ob[ob][64:128, :], scalar1=sink[64:128, h:h + 1], scalar2=None, op0=ALU.add), reads=[b_O[ob], b_sink], writes=[b_Rc[ob]])
                    P.op("act", X.activation(out=Rc[ob][0:64, :], in_=Rc[ob][0:64, :], func=AF.Ln), reads=[b_Rc[ob]], writes=[b_Rc[ob]])
                    P.op("act", X.activation(out=Rc[ob][0:64, :], in_=Rc[ob][0:64, :], func=AF.Exp, scale=-1.0), reads=[b_Rc[ob]], writes=[b_Rc[ob]])
                else:
                    P.op("dve", X.reciprocal(out=Rc[ob][0:64, :], in_=Ob[ob][64:128, :]), reads=[b_O[ob]], writes=[b_Rc[ob]])
                P.op("dve", X.tensor_tensor(out=On[ob][0:64, :], in0=Ob[ob][0:64, :], in1=Rc[ob][0:64, :], op=ALU.mult), reads=[b_O[ob], b_Rc[ob]], writes=[b_On[ob]])
                dma("sp", OT[orow0 + h * 64:orow0 + (h + 1) * 64, Q * 512:(Q + 1) * 512], On[ob][0:64, :], reads=[b_On[ob]])
            for i in range(min(LA, len(steps))):
                qk(i)
            pending = []
            for i in range(len(steps)):
                if i + LA < len(steps):
                    qk(i + LA)
                ex(i)
                pv(i)
                while pending and pending[0][0] <= i:
                    norm(pending.pop(0)[1])
                if steps[i][5]:
                    pending.append((i + 3, steps[i][0]))
            for _, Qp in pending:
                norm(Qp)
            gi[0] += len(steps)
        A.release()
        P.barrier()

    def outproj_ln(Wo, b_Wo, Xsrc, Xdst, layer):
        A.mark()
        load_ln_params(ln1_g, ln1_b, layer, 0)
        NZ = 5
        tmps = [(A.alloc([2, 6], F32), A.alloc([2], F32), A.alloc([2], F32), Buf()) for _ in range(NZ)]
        xts = [A.alloc([4, D], F32) for _ in range(3)]; b_xt = psbufs(3)
        ots = [A.alloc([8, 512], BF16) for _ in range(3)]; b_ot = psbufs(3)
        zs = [A.alloc([D], F32) for _ in range(NZ)]; b_z = psbufs(NZ)
        b_ps = pbufs(4); pi = [0]

        def loads(blk):
            dma("sp", xts[blk % 3], Xsrc[blk * 512:(blk + 1) * 512, :].rearrange("(t p) d -> p t d", p=128), writes=[b_xt[blk % 3]])
            dma("sp", ots[blk % 3], OT[:, blk * 512:(blk + 1) * 512].rearrange("(c p) s -> p c s", p=128), writes=[b_ot[blk % 3]])

        def s1(i):
            blk, t = divmod(i, 4)
            if t == 0 and blk + 2 < NB:
                loads(blk + 2)
            xt, bx = xts[blk % 3], b_xt[blk % 3]; ot, bo = ots[blk % 3], b_ot[blk % 3]
            z, bz = zs[i % NZ], b_z[i % NZ]
            for hh in range(2):
                pj = pi[0] % 4; pi[0] += 1
                for c in range(8):
                    P.op("pe", X.matmul(PS[pj], lhsT=ot[:, c, t * 128:(t + 1) * 128], rhs=Wo[:, c, hh * 512:(hh + 1) * 512], start=(c == 0), stop=(c == 7)),
                         reads=[bo, b_Wo], writes=[b_ps[pj]])
                P.op("dve", X.scalar_tensor_tensor(out=z[:, hh * 512:(hh + 1) * 512], in0=xt[:, t, hh * 512:(hh + 1) * 512], scalar=ALPHA, in1=PS[pj], op0=ALU.mult, op1=ALU.add),
                     reads=[bx, b_ps[pj]], writes=[bz])
            ln_stage1(z, bz, 0, tmps[i % NZ])

        def s2(i):
            ln_stage2(zs[i % NZ], b_z[i % NZ], 0, tmps[i % NZ])

        def s3(i):
            blk, t = divmod(i, 4)
            ln_stage3(zs[i % NZ], b_z[i % NZ], 0, Xdst, slice(blk * 512 + t * 128, blk * 512 + (t + 1) * 128))
        loads(0); loads(1)
        s1(0); s1(1); s2(0)
        for i in range(NT):
            if i + 2 < NT:
                s1(i + 2)
            if i + 1 < NT:
                s2(i + 1)
            s3(i)
        A.release()
        P.barrier()

    def ffn(layer, Xsrc, Xdst):
        A.mark()
        NFA = 14
        Wd_a = A.alloc([NFA, D], BF16)
        A.mark()
        Wg = A.alloc([8, DFF], BF16); Wu = A.alloc([8, DFF], BF16)
        bl_Wg, bl_Wu = load_w_colblocks([(Wg, w_gate[layer]), (Wu, w_up[layer])], DFF, cb=256)
        b_Wd = []
        for f0 in range(0, NFA, 2):
            grp = [(Wd_a[:, f, :], w_down[layer][f * 128:(f + 1) * 128, :]) for f in (f0, f0 + 1)]
            b = Buf()
            P.op("poolq", (lambda grp: (lambda e: [e.dma_start(out=o, in_=i) for (o, i) in grp]))(grp), writes=[b], ndma=2)
            b_Wd += [b, b]
        xts = [A.alloc([4, D], F32)] * 2; b_xt = [Buf()] * 2
        xTs = [A.alloc([8, 512], BF16) for _ in range(2)]; b_xT = psbufs(2)
        sgs = [A.alloc([512], F32) for _ in range(2)]; b_sg = psbufs(2)
        hts = [A.alloc([512], BF16) for _ in range(4)]; b_ht = psbufs(4)
        pst = PS[0:2]; b_pst = pbufs(2); cnt = [0]
        b_g = pbufs(2); b_u = pbufs(2); fi = 0
        load_x(0, Xsrc, xts[0], b_xt[0])
        for blk in range(NB):
            xt, bx = xts[blk % 2], b_xt[blk % 2]; xT, bxT = xTs[blk % 2], b_xT[blk % 2]
            make_xT(xt, bx, xT, bxT, pst, b_pst, cnt)
            if blk + 1 < NB:
                load_x(blk + 1, Xsrc, xts[(blk + 1) % 2], b_xt[(blk + 1) % 2])
            for f in range(NF):
                j = fi % 2; hj = fi % 4; fi += 1
                pg, pu = PS[2 + j], PS[4 + j]
                for c in range(8):
                    P.op("pe", X.matmul(pg, lhsT=Wg[:, c, f * 128:(f + 1) * 128], rhs=xT[:, c, :], start=(c == 0), stop=(c == 7)), reads=[colbuf(bl_Wg, (f + 1) * 128), bxT], writes=[b_g[j]])
                for c in range(8):
                    P.op("pe", X.matmul(pu, lhsT=Wu[:, c, f * 128:(f + 1) * 128], rhs=xT[:, c, :], start=(c == 0), stop=(c == 7)), reads=[colbuf(bl_Wu, (f + 1) * 128), bxT], writes=[b_u[j]])
                P.op("act", X.activation(out=sgs[j], in_=pg, func=AF.Silu), reads=[b_g[j]], writes=[b_sg[j]])
                P.op("dve", X.tensor_tensor(out=hts[hj], in0=pu, in1=sgs[j], op=ALU.mult), reads=[b_u[j], b_sg[j]], writes=[b_ht[hj]])
                dma("sp", HT[f * 128:(f + 1) * 128, blk * 512:(blk + 1) * 512], hts[hj], reads=[b_ht[hj]])
        A.release()
        P.barrier()
        A.mark()
        Wd_b = A.alloc([NF - NFA, D], BF16)
        for f0 in range(NFA, NF, 2):
            grp = [(Wd_b[:, f - NFA, :], w_down[layer][f * 128:(f + 1) * 128, :]) for f in (f0, f0 + 1)]
            b = Buf()
            P.op("poolq", (lambda grp: (lambda e: [e.dma_start(out=o, in_=i) for (o, i) in grp]))(grp), writes=[b], ndma=2)
            b_Wd += [b, b]

        def Wd_of(f):
            return Wd_a[:, f] if f < NFA else Wd_b[:, f - NFA]
        load_ln_params(ln2_g, ln2_b, layer, 2)
        NZ = 5
        tmps = [(A.alloc([2, 6], F32), A.alloc([2], F32), A.alloc([2], F32), Buf()) for _ in range(NZ)]
        xts = [A.alloc([4, D], F32) for _ in range(2)]; b_xt = psbufs(2)
        hbs = [A.alloc([NF, 512], BF16) for _ in range(2)]; b_hb = psbufs(2)
        zs = [A.alloc([D], F32) for _ in range(NZ)]; b_z = psbufs(NZ)
        b_ps = pbufs(4); pi = [0]

        def loads(blk):
            dma("sp", xts[blk % 2], Xsrc[blk * 512:(blk + 1) * 512, :].rearrange("(t p) d -> p t d", p=128), writes=[b_xt[blk % 2]])
            dma("sp", hbs[blk % 2], HT[:, blk * 512:(blk + 1) * 512].rearrange("(f p) s -> p f s", p=128), writes=[b_hb[blk % 2]])

        def s1(i):
            blk, t = divmod(i, 4)
            if t == 0 and blk + 1 < NB:
                loads(blk + 1)
            xt, bx = xts[blk % 2], b_xt[blk % 2]; hb, bh = hbs[blk % 2], b_hb[blk % 2]
            z, bz = zs[i % NZ], b_z[i % NZ]
            for hh in range(2):
                pj = pi[0] % 4; pi[0] += 1
                for f in range(NF):
                    P.op("pe", X.matmul(PS[pj], lhsT=hb[:, f, t * 128:(t + 1) * 128], rhs=Wd_of(f)[:, hh * 512:(hh + 1) * 512], start=(f == 0), stop=(f == NF - 1)),
                         reads=[bh, b_Wd[f]], writes=[b_ps[pj]])
                P.op("dve", X.scalar_tensor_tensor(out=z[:, hh * 512:(hh + 1) * 512], in0=xt[:, t, hh * 512:(hh + 1) * 512], scalar=ALPHA, in1=PS[pj], op0=ALU.mult, op1=ALU.add),
                     reads=[bx, b_ps[pj]], writes=[bz])
            ln_stage1(z, bz, 2, tmps[i % NZ])

        def s2(i):
            ln_stage2(zs[i % NZ], b_z[i % NZ], 2, tmps[i % NZ])

        def s3(i):
            blk, t = divmod(i, 4)
            ln_stage3(zs[i % NZ], b_z[i % NZ], 2, Xdst, slice(blk * 512 + t * 128, blk * 512 + (t + 1) * 128))
        loads(0)
        s1(0); s1(1); s2(0)
        for i in range(NT):
            if i + 2 < NT:
                s1(i + 2)
            if i + 1 < NT:
                s2(i + 1)
            s3(i)
        A.release()
        P.barrier()
        A.release()

    def inproj_ab(Xsrc):
        A.mark()
        Win = A.alloc([8, 1440], BF16); b_Win = Buf()
        load_w_cast(Win, ab_w_in, 1440, b_Win)
        Wuq = A.alloc([3, 768], BF16); b_Wuq = Buf()
        load_w_cast(Wuq, ab_w_uq, 768, b_Wuq)
        Wukv = A.alloc([2, 1024], BF16); b_Wukv = Buf()
        load_w_cast(Wukv, ab_w_ukv, 1024, b_Wukv)
        Wkr_rot = A.alloc([8, 32], BF16); Wuq_rot = A.alloc([3, 8, 32], BF16); b_rot = Buf()
        P.op("act", X.mul(out=Wkr_rot[:, :, 0:16], in_=Win[:, :, 656:672], mul=-1.0), reads=[b_Win], writes=[b_rot])
        P.op("act", X.copy(out=Wkr_rot[:, :, 16:32], in_=Win[:, :, 640:656]), reads=[b_Win], writes=[b_rot])
        Wuq4 = Wuq.rearrange("p c (h e) -> p c h e", e=96)
        for c in range(3):
            P.op("act", X.mul(out=Wuq_rot[:, c, :, 0:16], in_=Wuq4[:, c, :, 80:96], mul=-1.0), reads=[b_Wuq], writes=[b_rot])
            P.op("act", X.copy(out=Wuq_rot[:, c, :, 16:32], in_=Wuq4[:, c, :, 64:80]), reads=[b_Wuq], writes=[b_rot])
        Wq_nope = A.alloc([3, 512], BF16); Wq_rope = A.alloc([3, 256], BF16); Wk_nope = A.alloc([2, 512], BF16); b_wc = Buf()
        Wukv4 = Wukv.rearrange("p c (h e) -> p c h e", e=128)
        for c in range(3):
            P.op("dve", X.tensor_copy(out=Wq_nope[:, c, :].rearrange("p (h e) -> p h e", e=64), in_=Wuq4[:, c, :, 0:64]), reads=[b_Wuq], writes=[b_wc])
            P.op("pool", X.tensor_copy(out=Wq_rope[:, c, :].rearrange("p (h e) -> p h e", e=32), in_=Wuq4[:, c, :, 64:96]), reads=[b_Wuq], writes=[b_wc])
        for c in range(2):
            P.op("dve", X.tensor_copy(out=Wk_nope[:, c, :].rearrange("p (h e) -> p h e", e=64), in_=Wukv4[:, c, :, 0:64]), reads=[b_Wukv], writes=[b_wc])
        Wq_rot = Wuq_rot.rearrange("p c h e -> p c (h e)")
        gn = A.alloc([5], F32); b_gn = Buf()
        dma("sp", gn[:, 0:3], ab_q_norm.rearrange("(c p) -> p c", p=128), writes=[b_gn])
        dma("sp", gn[:, 3:5], ab_kv_norm.rearrange("(c p) -> p c", p=128), writes=[b_gn])
        xts = [A.alloc([4, D], F32) for _ in range(2)]; b_xt = psbufs(2)
        xTs = [A.alloc([8, 512], BF16) for _ in range(2)]; b_xT = psbufs(2)
        cs = [A.alloc([2, 512], F32) for _ in range(2)]; b_cs = psbufs(2)
        c32 = A.alloc([5, 512], F32); b_c32 = Buf()
        sq = A.alloc([5, 512], BF16); b_sq = Buf()
        rbc = A.alloc([2, 512], F32); b_rbc = Buf()
        cn = A.alloc([5, 512], BF16); b_cn = Buf()
        NST = 6
        stg = [A.alloc([512], BF16) for _ in range(NST)]; b_stg = psbufs(NST)
        t1s = [A.alloc([512], F32) for _ in range(2)]; t2s = [A.alloc([512], F32) for _ in range(2)]; b_t = psbufs(2)
        vst = [A.alloc([4, 512], BF16) for _ in range(2)]; b_vst = psbufs(2)
        vss = [A.alloc([4, 128], BF16) for _ in range(2)]; b_vss = psbufs(2)
        pst = PS[0:2]; b_pst = pbufs(2); cnt = [0]
        b_ps = pbufs(6); pi = [0]; si = [0]; ti = [0]
        SC_MLA = 96.0 ** -0.5
        SC_SWA = 64.0 ** -0.5

        def nps():
            j = pi[0] % 6; pi[0] += 1
            return PS[2 + j], b_ps[j]

        def nstg():
            j = si[0] % NST; si[0] += 1
            return stg[j], b_stg[j]

        def fm_proj(ps, bps, M, wsl, W_b, rhs_of_c, nchunks, rhs_b):
            for c in range(nchunks):
                P.op("pe", X.matmul(ps[0:M, :], lhsT=wsl(c), rhs=rhs_of_c(c), start=(c == 0), stop=(c == nchunks - 1)), reads=[W_b, rhs_b], writes=[bps])

        def rope_combine(psA, bA, psB, bB, cst, bcs, scale, dst_dram, np_=32):
            j = ti[0] % 2; ti[0] += 1
            t1, t2, bt = t1s[j], t2s[j], b_t[j]
            P.op("dve", X.scalar_tensor_tensor(out=t1[0:np_, :], in0=psA[0:np_, :], scalar=float(scale), in1=cst[0:np_, 0, :], op0=ALU.mult, op1=ALU.mult), reads=[bA, bcs], writes=[bt])
            P.op("dve", X.scalar_tensor_tensor(out=t2[0:np_, :], in0=psB[0:np_, :], scalar=float(scale), in1=cst[0:np_, 1, :], op0=ALU.mult, op1=ALU.mult), reads=[bB, bcs], writes=[bt])
            s, bs = nstg()
            P.op("pool", X.tensor_tensor(out=s[0:np_, :], in0=t1[0:np_, :], in1=t2[0:np_, :], op=ALU.add), reads=[bt], writes=[bs])
            return s, bs

        for blk in range(NB):
            cols = slice(blk * 512, (blk + 1) * 512)
            xt, bx = xts[blk % 2], b_xt[blk % 2]; xT, bxT = xTs[blk % 2], b_xT[blk % 2]
            cst, bcs = cs[blk % 2], b_cs[blk % 2]
            if blk == 0:
                load_x(0, Xsrc, xts[0], b_xt[0])
            if blk + 1 < NB:
                load_x(blk + 1, Xsrc, xts[(blk + 1) % 2], b_xt[(blk + 1) % 2])
            make_xT(xt, bx, xT, bxT, pst, b_pst, cnt)
            P.op("sp", (lambda cst, cols: (lambda e: [e.dma_start(out=cst[32 * r:32 * r + 32], in_=rope_cs[:, :, cols].rearrange("a p s -> p a s")) for r in range(4)]))(cst, cols), writes=[bcs], ndma=4)
            if dbg and blk == 0:
                dxt = dsc("DBG_xT", [128, 8, 512], BF16); dwin = dsc("DBG_Win", [128, 8, 1440], BF16)
                dma("sp", dxt, xT, reads=[bxT]); dma("sp", dwin, Win, reads=[b_Win])
            if DBGSTOP == 2: continue
            for i in range(5):
                ps, bps = nps()
                fm_proj(ps, bps, 128, lambda c, i=i: Win[:, c, i * 128:(i + 1) * 128], b_Win, lambda c: xT[:, c, :], 8, bxT)
                P.op("dve", X.tensor_copy(out=c32[:, i, :], in_=ps), reads=[bps], writes=[b_c32])
                P.op("act", X.activation(out=sq[:, i, :], in_=ps, func=AF.Square), reads=[bps], writes=[b_sq])
            for (k, i0, n, R_) in ((0, 0, 3, 384.0), (1, 3, 2, 256.0)):
                ps, bps = nps()
                for i in range(n):
                    P.op("pe", X.matmul(ps, lhsT=ones_bf, rhs=sq[:, i0 + i, :], start=(i == 0), stop=(i == n - 1)), reads=[b_ones, b_sq], writes=[bps])
                P.op("act", X.activation(out=rbc[:, k, :], in_=ps, func=AF.Ln, scale=1.0 / R_, bias=eps_rms[:, 0:1]), reads=[bps, b_eps], writes=[b_rbc])
                P.op("act", X.activation(out=rbc[:, k, :], in_=rbc[:, k, :], func=AF.Exp, scale=-0.5), reads=[b_rbc], writes=[b_rbc])
                for i in range(n):
                    P.op("dve", X.scalar_tensor_tensor(out=cn[:, i0 + i, :], in0=c32[:, i0 + i, :], scalar=gn[:, i0 + i:i0 + i + 1], in1=rbc[:, k, :], op0=ALU.mult, op1=ALU.mult),
                         reads=[b_c32, b_gn, b_rbc], writes=[b_cn])
            if DBGSTOP == 3: continue
            psA, bA = nps(); psB, bB = nps()
            fm_proj(psA, bA, 32, lambda c: Win[:, c, 640:672], b_Win, lambda c: xT[:, c, :], 8, bxT)
            fm_proj(psB, bB, 32, lambda c: Wkr_rot[:, c, :], b_rot, lambda c: xT[:, c, :], 8, bxT)
            s, bs = rope_combine(psA, bA, psB, bB, cst, bcs, 1.0, None)
            for h in range(8):
                dma("sp", KTa[h, 0:32, cols], s[0:32, :], reads=[bs])
            if DBGSTOP == 4: continue
            for i in range(4):
                ps, bps = nps()
                fm_proj(ps, bps, 128, lambda c, i=i: Win[:, c, 672 + i * 128:672 + (i + 1) * 128], b_Win, lambda c: xT[:, c, :], 8, bxT)
                s, bs = nstg()
                evac("act" if i % 2 == 0 else "dve", s, ps, [bps], [bs], scale=SC_SWA)
                dma("sp", QTs[2 * i:2 * i + 2, :, cols].rearrange("h p s -> (h p) s"), s, reads=[bs])
            ps, bps = nps()
            fm_proj(ps, bps, 128, lambda c: Win[:, c, 1184:1312], b_Win, lambda c: xT[:, c, :], 8, bxT)
            s, bs = nstg()
            evac("act", s, ps, [bps], [bs])
            dma("sp", KTs[:, :, cols].rearrange("h p s -> (h p) s"), s, reads=[bs])
            ps, bps = nps()
            for t in range(4):
                for c in range(8):
                    P.op("pe", X.matmul(ps[:, t * 128:(t + 1) * 128], lhsT=xT[:, c, t * 128:(t + 1) * 128], rhs=Win[:, c, 1312:1440], start=(c == 0), stop=(c == 7)), reads=[bxT, b_Win], writes=[bps])
            vs_, bvs = vss[blk % 2], b_vss[blk % 2]
            evac("dve", vs_, ps.rearrange("p (t e) -> p t e", e=128), [bps], [bvs])
            dma("sp", Vs[cols].rearrange("(t p) g e -> p t (g e)", p=128), vs_, reads=[bvs])
            if DBGSTOP == 5: continue
            Wuq4_ = Wuq.rearrange("p c (h e) -> p c h e", e=96)
            for g in range(4):
                ps, bps = nps()
                fm_proj(ps, bps, 128, lambda c, g=g: Wq_nope[:, c, 128 * g:128 * g + 128], b_wc, lambda c: cn[:, c, :], 3, b_cn)
                s, bs = nstg()
                evac("act", s, ps, [bps], [bs], scale=SC_MLA)
                dma("sp", QTa[2 * g, 32:96, cols], s[0:64, :], reads=[bs])
                dma("sp", QTa[2 * g + 1, 32:96, cols], s[64:128, :], reads=[bs])
            for g in range(2):
                psA, bA = nps(); psB, bB = nps()
                fm_proj(psA, bA, 128, lambda c, g=g: Wq_rope[:, c, 128 * g:128 * g + 128], b_wc, lambda c: cn[:, c, :], 3, b_cn)
                fm_proj(psB, bB, 128, lambda c, g=g: Wq_rot[:, c, 128 * g:128 * g + 128], b_rot, lambda c: cn[:, c, :], 3, b_cn)
                s, bs = rope_combine(psA, bA, psB, bB, cst, bcs, SC_MLA, None, np_=128)
                for r in range(4):
                    dma("sp", QTa[4 * g + r, 0:32, cols], s[32 * r:32 * r + 32, :], reads=[bs])
            if DBGSTOP == 6: continue
            Wukv4_ = Wukv.rearrange("p c (h e) -> p c h e", e=128)
            for g in range(4):
                ps, bps = nps()
                fm_proj(ps, bps, 128, lambda c, g=g: Wk_nope[:, c, 128 * g:128 * g + 128], b_wc, lambda c: cn[:, 3 + c, :], 2, b_cn)
                s, bs = nstg()
                evac("dve" if g % 2 == 0 else "act", s, ps, [bps], [bs])
                dma("sp", KTa[2 * g, 32:96, cols], s[0:64, :], reads=[bs])
                dma("sp", KTa[2 * g + 1, 32:96, cols], s[64:128, :], reads=[bs])
            if DBGSTOP == 7: continue
            vt, bvt = vst[blk % 2], b_vst[blk % 2]
            for t in range(4):
                ps, bps = nps()
                for c in range(2):
                    P.op("pe", X.matmul(ps.rearrange("p (h e) -> p h e", e=64), lhsT=cn[:, 3 + c, t * 128:(t + 1) * 128], rhs=Wukv[:, c, :].rearrange("p (h e) -> p h e", e=128)[:, :, 64:128], start=(c == 0), stop=(c == 1)),
                         reads=[b_cn, b_Wukv], writes=[bps])
                evac("act" if t % 2 == 0 else "dve", vt[:, t, :], ps, [bps], [bvt])
            dma("sp", Va[cols].rearrange("(t p) h e -> p t (h e)", p=128), vt, reads=[bvt])
        A.release()
        P.barrier()

    def inproj_cd(Xsrc):
        A.mark()
        Win = A.alloc([8, 3080], BF16)
        bl_Win = load_w_colblocks([(Win, cd_w_in)], 3080, cb=440)[0]

        def b_Win_of(c_lo, c_hi):
            bs = []
            for (c1, b) in bl_Win:
                if c1 > c_lo and c1 - 440 < c_hi:
                    bs.append(b)
            return bs
        nb = A.alloc([1], F32); b_nb = Buf()
        dma("sp", nb[0:8, :], cd_b_forget.rearrange("(h o) -> h o", o=1), writes=[b_nb])
        P.op("dve", X.tensor_scalar(out=nb[0:8, :], in0=nb[0:8, :], scalar1=-1.0, scalar2=None, op0=ALU.mult), reads=[b_nb], writes=[b_nb])
        ones32 = A.alloc([512], F32); b_o32 = Buf()
        P.op("pool", X.memset(ones32, 1.0), writes=[b_o32])
        onesS = A.alloc([3, 512], BF16); b_oS = Buf()
        P.op("pool", X.memset(onesS[0:8], 1.0), writes=[b_oS])
        cum = A.alloc([S], F32); b_cum = Buf()
        xts = [A.alloc([4, D], F32)] * 2; b_xt = [Buf()] * 2
        xTs = [A.alloc([8, 512], BF16) for _ in range(2)]; b_xT = psbufs(2)
        NST = 6
        stg = [A.alloc([512], BF16) for _ in range(NST)]; b_stg = psbufs(NST)
        vst = [A.alloc([4, 512], BF16) for _ in range(2)] * 2; b_vst = psbufs(2) * 2
        ls = A.alloc([512], F32); b_ls = Buf()
        r1 = A.alloc([512], F32); r2 = A.alloc([512], F32); b_r = Buf()
        augk = [A.alloc([3, 512], BF16) for _ in range(2)]; augq = [A.alloc([3, 512], BF16) for _ in range(2)]; b_aug = psbufs(2)
        pst = PS[0:2]; b_pst = pbufs(2); cnt = [0]
        b_ps = pbufs(6); pi = [0]; si = [0]; vi = [0]
        SC = 64.0 ** -0.5

        def nps():
            j = pi[0] % 6; pi[0] += 1
            return PS[2 + j], b_ps[j]

        def nstg():
            j = si[0] % NST; si[0] += 1
            return stg[j], b_stg[j]

        for blk in range(NB):
            cols = slice(blk * 512, (blk + 1) * 512)
            xt, bx = xts[blk % 2], b_xt[blk % 2]; xT, bxT = xTs[blk % 2], b_xT[blk % 2]
            if blk == 0:
                load_x(0, Xsrc, xt, bx)
            make_xT(xt, bx, xT, bxT, pst, b_pst, cnt)
            if blk + 1 < NB:
                load_x(blk + 1, Xsrc, xt, bx)
            dma("sp", QTf[:, 67:70, blk * 512:(blk + 1) * 512], onesS[0:8], reads=[b_oS])
            dma("sp", KTf[:, 64:67, blk * 512:(blk + 1) * 512], onesS[0:8], reads=[b_oS])
            for (c0, dst, scale) in ((0, QTf, SC), (512, KTf, None), (1544, QTc, SC), (2056, KTc, None)):
                for i in range(4):
                    ps, bps = nps()
                    for c in range(8):
                        P.op("pe", X.matmul(ps, lhsT=Win[:, c, c0 + i * 128:c0 + (i + 1) * 128], rhs=xT[:, c, :], start=(c == 0), stop=(c == 7)), reads=b_Win_of(c0 + i * 128, c0 + (i + 1) * 128) + [bxT], writes=[bps])
                    s, bs = nstg()
                    evac("act" if i % 2 == 0 else "dve", s, ps, [bps], [bs], scale=scale)
                    dma("sp", dst[2 * i, 0:64, cols], s[0:64, :], reads=[bs])
                    dma("sp", dst[2 * i + 1, 0:64, cols], s[64:128, :], reads=[bs])
            for (c0, dst) in ((1024, Vf), (2568, Vc)):
                vt, bvt = vst[vi[0] % 4], b_vst[vi[0] % 4]; vi[0] += 1
                for t in range(4):
                    ps, bps = nps()
                    for c in range(8):
                        P.op("pe", X.matmul(ps, lhsT=xT[:, c, t * 128:(t + 1) * 128], rhs=Win[:, c, c0:c0 + 512], start=(c == 0), stop=(c == 7)), reads=[bxT] + b_Win_of(c0, c0 + 512), writes=[bps])
                    evac("act" if t % 2 == 0 else "dve", vt[:, t, :], ps, [bps], [bvt])
                dma("sp", dst[cols].rearrange("(t p) h e -> p t (h e)", p=128), vt, reads=[bvt])
            ps, bps = nps()
            for c in range(8):
                P.op("pe", X.matmul(ps[0:8, :], lhsT=Win[:, c, 1536:1544], rhs=xT[:, c, :], start=(c == 0), stop=(c == 7)), reads=b_Win_of(1536, 1544) + [bxT], writes=[bps])
            P.op("act", X.activation(out=ls[0:8, :], in_=ps[0:8, :], func=AF.Exp, scale=-1.0, bias=nb[0:8, 0:1]), reads=[bps, b_nb], writes=[b_ls])
            P.op("act", X.activation(out=ls[0:8, :], in_=ls[0:8, :], func=AF.Ln, bias=1.0), reads=[b_ls], writes=[b_ls])
            init = 0.0 if blk == 0 else cum[0:8, blk * 512 - 1:blk * 512]
            P.op("dve", X.tensor_tensor_scan(out=cum[0:8, cols], data0=ones32[0:8, :], data1=ls[0:8, :], initial=init, op0=ALU.mult, op1=ALU.add), reads=[b_ls, b_o32, b_cum], writes=[b_cum])
            ak, aq, ba = augk[blk % 2], augq[blk % 2], b_aug[blk % 2]
            P.op("dve", X.tensor_copy(out=ak[0:8, 0, :], in_=cum[0:8, cols]), reads=[b_cum], writes=[ba])
            P.op("dve", X.tensor_tensor(out=r1[0:8, :], in0=cum[0:8, cols], in1=ak[0:8, 0, :], op=ALU.subtract), reads=[b_cum, ba], writes=[b_r])
            P.op("dve", X.tensor_copy(out=ak[0:8, 1, :], in_=r1[0:8, :]), reads=[b_r], writes=[ba])
            P.op("dve", X.tensor_tensor(out=r2[0:8, :], in0=r1[0:8, :], in1=ak[0:8, 1, :], op=ALU.subtract), reads=[b_r, ba], writes=[b_r])
            P.op("dve", X.tensor_copy(out=ak[0:8, 2, :], in_=r2[0:8, :]), reads=[b_r], writes=[ba])
            P.op("dve", X.tensor_scalar(out=aq[0:8], in0=ak[0:8], scalar1=-1.0, scalar2=None, op0=ALU.mult), reads=[ba], writes=[ba])
            dma("sp", KTf[:, 67:70, cols], ak[0:8], reads=[ba])
            dma("sp", QTf[:, 64:67, cols], aq[0:8], reads=[ba])
        A.release()
        P.barrier()

    def prep_bias(shape, load_fn):
        bf = A.alloc(shape, BF16); b = Buf()
        f = A.alloc(shape, F32)
        load_fn(f, b)
        P.op("dve", X.tensor_copy(out=bf, in_=f), reads=[b], writes=[b])
        return bf, b

    inproj_ab(x_in)
    A.mark()
    Wo = A.alloc([8, D], BF16); b_Wo = Buf()
    abf, b_al = prep_bias([8, 2, 128], lambda f, b: dma("poolq", f, alibi_d, writes=[b]))
    sk = A.alloc([8], F32); b_sk = Buf()
    dma("poolq", sk, ab_sinks.rearrange("(o h) -> o h", o=1).broadcast_to([128, 8]), writes=[b_sk])
    P.op("act", X.activation(out=sk, in_=sk, func=AF.Exp), reads=[b_sk], writes=[b_sk])
    load_w_cast(Wo, ab_w_out, D, b_Wo)
    attention_full(QTa, KTa, Va, 96, mla_mask_d, 0)
    attention_band(QTs, KTs, Vs, 2, 1, abf, b_al, 512, sink=sk, b_sink=b_sk)
    outproj_ln(Wo, b_Wo, x_in, X1, 0)
    A.release()
    ffn(0, X1, X2)
    inproj_cd(X2)
    A.mark()
    Wo = A.alloc([8, D], BF16); b_Wo = Buf()

    def load_ck(f, b):
        cm = A.alloc([5, 128], F32)
        dma("poolq", f, ck_bias_d, writes=[b])
        dma("poolq", cm, ck_mask_d, writes=[b])
        for h in range(8):
            P.op("pool", X.tensor_tensor(out=f[:, h], in0=f[:, h], in1=cm, op=ALU.add), reads=[b], writes=[b])
    cbf, b_cb = prep_bias([8, 5, 128], load_ck)
    load_w_cast(Wo, cd_w_out, D, b_Wo)
    attention_full(QTf, KTf, Vf, 70, causal_mask_d, 0)
    attention_band(QTc, KTc, Vc, 8, 4, cbf, b_cb, 512)
    outproj_ln(Wo, b_Wo, X2, X3, 1)
    A.release()
    ffn(1, X3, y_out)
    P.emit()
    es.close()
    return nc


_CACHE = {}


def kernel(**inputs):
    import ml_dtypes
    hc = host_constants()
    idx = hc.pop("_ck_idx")
    rel = np.asarray(inputs["cd_rel_bias"], np.float32)[0]
    hc["ck_bias"] = np.ascontiguousarray(np.transpose(rel[idx], (0, 3, 1, 2)))
    if "nc" not in _CACHE:
        _CACHE["nc"] = build_nc()
    nc = _CACHE["nc"]
    x = np.asarray(inputs["x"], np.float32)
    shared = {
        "ab_w_in": inputs["ab_w_in"][0], "ab_q_norm": inputs["ab_q_norm"][0], "ab_w_uq": inputs["ab_w_uq"][0],
        "ab_kv_norm": inputs["ab_kv_norm"][0], "ab_w_ukv": inputs["ab_w_ukv"][0], "ab_sinks": inputs["ab_sinks"][0],
        "ab_w_out": inputs["ab_w_out"][0], "cd_w_in": inputs["cd_w_in"][0], "cd_b_forget": inputs["cd_b_forget"][0],
        "cd_w_out": inputs["cd_w_out"][0],
        "ln1_g": inputs["ln1_g"], "ln1_b": inputs["ln1_b"], "ln2_g": inputs["ln2_g"], "ln2_b": inputs["ln2_b"],
        "ffn_w_gate": inputs["ffn_w_gate"], "ffn_w_up": inputs["ffn_w_up"], "ffn_w_down": inputs["ffn_w_down"],
    }
    shared = {k: np.ascontiguousarray(np.asarray(v, np.float32)) for k, v in shared.items()}
    shared.update(hc)
    in_maps = [dict(shared, x=np.ascontiguousarray(x[b])) for b in range(8)]
    res = run_bass_kernel_spmd(nc, in_maps, core_ids=list(range(8)))
    return np.stack([np.asarray(r["y"], np.float32) for r in res.results], axis=0)
```

```python
import numpy as np
import concourse.bass as bass
import concourse.mybir as mybir
from concourse.bass_utils import run_bass_kernel_spmd
from contextlib import ExitStack

F32 = mybir.dt.float32
BF16 = mybir.dt.bfloat16
AF = mybir.ActivationFunctionType
ALU = mybir.AluOpType
AX = mybir.AxisListType

COMPUTE = ("pe", "act", "dve", "pool")
QUEUES = ("sp", "poolq")
NDMASEM = 12


class _Rec:
    def __init__(self, name):
        self.name = name

    def __call__(self, *a, **k):
        name = self.name
        return lambda e: getattr(e, name)(*a, **k)


class _Recorder:
    def __getattr__(self, name):
        return _Rec(name)


X = _Recorder()


class Buf:
    __slots__ = ("name", "w", "r", "rd", "excl")

    def __init__(self, name="", excl=False):
        self.name = name
        self.excl = excl
        self.w = None
        self.r = {}
        self.rd = []


class Op:
    __slots__ = ("eng", "fn", "deps", "sig", "sem", "val", "ndma", "isdma", "prev")


class Prog:
    def __init__(self, nc):
        self.nc = nc
        self.streams = {"pe": [], "act": [], "dve": [], "pool": [], "sp": []}
        self.barrier_deps = []
        self.since_barrier_dma = []
        self.last = {}
        self.wgroups = {}

    def _stream_of(self, eng):
        return "pool" if eng == "poolq" else eng

    def op(self, eng, fn, reads=(), writes=(), ndma=0):
        o = Op()
        o.eng = eng
        o.fn = fn
        o.ndma = ndma
        o.isdma = ndma > 0
        o.sig = False
        o.sem = None
        o.val = 0
        st = self._stream_of(eng)
        deps = []
        xr = [b for b in reads if b.excl]
        if xr:
            reads = [b for b in reads if not b.excl]
            writes = list(writes) + xr
        for b in reads:
            if b.w is not None:
                deps.append(b.w)
            deps.extend(self.wgroups.get(id(b), ()))
        for b in writes:
            if b.w is not None:
                deps.append(b.w)
            if id(b) in self.wgroups:
                deps.extend(self.wgroups.pop(id(b)))
            deps.extend(b.r.values())
            deps.extend(b.rd)
        deps.extend(self.barrier_deps)
        if eng == "pe":
            deps = [d for d in deps if d.isdma or d.eng != "pe"]
        seen = set()
        dd = []
        for d in deps:
            if id(d) not in seen and d is not o:
                seen.add(id(d))
                dd.append(d)
                d.sig = True
        o.deps = dd
        for b in reads:
            if o.isdma:
                b.rd.append(o)
            else:
                b.r[eng] = o
        for b in writes:
            b.w = o
            b.r = {}
            b.rd = []
        self.streams[st].append(o)
        if o.isdma:
            self.since_barrier_dma.append(o)
            o.sig = True
        else:
            self.last[eng] = o
        return o

    def barrier(self):
        deps = list(self.last.values()) + list(self.since_barrier_dma)
        for d in deps:
            d.sig = True
        self.barrier_deps = deps
        self.since_barrier_dma = []

    def emit(self, final_wait_eng="sp"):
        nc = self.nc
        for d in list(self.last.values()):
            d.sig = True
        with ExitStack() as es:
            sems = {}
            for e in COMPUTE:
                sems[e] = es.enter_context(nc.semaphore("s_" + e))
            dsems = {}
            for q in QUEUES:
                dsems[q] = [es.enter_context(nc.semaphore("d_%s%d" % (q, i))) for i in range(NDMASEM)]
            for st, ops in self.streams.items():
                cnt = 0
                qcnt = {q: 0 for q in QUEUES}
                qval = {q: [0] * NDMASEM for q in QUEUES}
                for o in ops:
                    if o.isdma:
                        k = qcnt[o.eng] % NDMASEM
                        qcnt[o.eng] += 1
                        o.sem = dsems[o.eng][k]
                        o.prev = qval[o.eng][k]
                        qval[o.eng][k] += 16 * o.ndma
                        o.val = qval[o.eng][k]
                    elif o.sig:
                        cnt += 1
                        o.sem = sems[o.eng]
                        o.val = cnt
            tail = list(self.barrier_deps) + list(self.last.values()) + list(self.since_barrier_dma)
            for d in tail:
                assert d.sig or d.isdma
            self.maxval = 0
            es.enter_context(nc.allow_non_contiguous_dma(reason='small strided param loads'))
            block = es.enter_context(nc.Block())
            handles = {"pe": "tensor", "act": "scalar", "dve": "vector", "pool": "gpsimd", "sp": "sync"}

            def run_stream(st, eng):
                waited = {}

                def wait(sem, val):
                    if val <= 0:
                        return
                    key = id(sem)
                    if waited.get(key, 0) >= val:
                        return
                    waited[key] = val
                    self.maxval = max(self.maxval, val)
                    eng.wait_ge(sem, val)

                for o in self.streams[st]:
                    for d in o.deps:
                        wait(d.sem, d.val)
                    if o.isdma:
                        wait(o.sem, o.prev)
                        ins = o.fn(eng)
                        assert len(ins) == o.ndma, (len(ins), o.ndma)
                        for i in ins:
                            i.then_inc(o.sem, 16)
                    else:
                        i = o.fn(eng)
                        if o.sig:
                            i.then_inc(o.sem, 1)
                if st == final_wait_eng:
                    for d in tail:
                        wait(d.sem, d.val)

            for st in self.streams:
                getattr(block, handles[st])(lambda eng, st=st: run_stream(st, eng))


class Arena:
    def __init__(self, nc, es, nbytes, name="arena"):
        self.t = es.enter_context(nc.sbuf_tensor(name, [128, nbytes // 4], F32))
        self.nbytes = nbytes
        self.off = 0
        self.marks = []

    def alloc(self, free_shape, dtype, parts=128, pbase=0):
        esz = 2 if dtype == BF16 else 4
        n = int(np.prod(free_shape))
        nb = (n * esz + 31) // 32 * 32
        assert self.off + nb <= self.nbytes, "SBUF arena overflow: %d + %d > %d" % (self.off, nb, self.nbytes)
        w0 = self.off // 4
        ap = self.t[:, w0:w0 + nb // 4]
        if dtype != F32:
            ap = ap.bitcast(dtype)
        ap = ap[:, 0:n]
        self.off += nb
        if len(free_shape) == 2:
            ap = ap.rearrange("p (a b) -> p a b", b=free_shape[1])
        elif len(free_shape) == 3:
            ap = ap.rearrange("p (a b c) -> p a b c", b=free_shape[1], c=free_shape[2])
        return ap

    def mark(self):
        self.marks.append(self.off)

    def release(self):
        self.off = self.marks.pop()


S = 4096
D = 1024
NT = 32
NB = 8
DFF = 2816
NF = 22
ALPHA = 4.0 ** 0.25
LN_EPS = 1e-5
RMS_EPS = 1e-6
NEG = -30000.0
import os as _os
DBGSTOP = int(_os.environ.get('DBGSTOP', '0'))


def host_constants():
    c = {}
    inv = (10000.0 ** (-np.arange(0, 32, 2, dtype=np.float32) / 32)).astype(np.float32)
    ang = np.arange(S, dtype=np.float32)[:, None] * inv[None, :]
    cos, sin = np.cos(ang).astype(np.float32), np.sin(ang).astype(np.float32)
    cs = np.zeros((2, 32, S), np.float32)
    cs[0, 0:16] = cos.T
    cs[0, 16:32] = cos.T
    cs[1, 0:16] = sin.T
    cs[1, 16:32] = sin.T
    c["rope_cs"] = cs
    k = np.arange(128)[:, None]
    q = np.arange(128)[None, :]
    c["mla_mask"] = np.where((k >= 64) & (q < 64), NEG, 0.0).astype(np.float32)
    c["causal_mask"] = np.where(k > q, NEG, 0.0).astype(np.float32)

    def band_mask(nof, L):
        m = np.zeros((128, nof, 128), np.float32)
        for o in range(nof):
            dc = 2 * o + (q >= 64).astype(np.int32) - (k >= 64).astype(np.int32)
            m[:, o, :] = np.where((dc >= 0) & (dc <= L), 0.0, NEG)
        return m
    slopes = np.exp2(-8.0 * np.arange(1, 9, dtype=np.float32) / 8).astype(np.float32)
    al = np.zeros((128, 8, 2, 128), np.float32)
    bm = band_mask(2, 2)
    for h in range(8):
        for o in range(2):
            dist = 128 * o + q - k
            al[:, h, o, :] = -slopes[h] * np.abs(dist).astype(np.float32) + bm[:, o, :]
    c["alibi"] = al
    c["ck_mask"] = band_mask(5, 8)
    idx = np.zeros((128, 5, 128), np.int64)
    for o in range(5):
        idx[:, o, :] = np.clip(128 * o + q - k, -63, 256) + 63
    c["_ck_idx"] = idx
    return c


def build_nc(dbg=False, nphase=99):
    nc = bass.Bass("TRN2", target_bir_lowering=False)

    def din(name, shape, dt=F32):
        return nc.dram_tensor(name, list(shape), dt, kind="ExternalInput").ap()
    skind = "ExternalOutput" if dbg else "Internal"

    def dsc(name, shape, dt):
        return nc.dram_tensor(name, list(shape), dt, kind=skind).ap()

    x_in = din("x", [S, D])
    ab_w_in = din("ab_w_in", [D, 1440]); ab_q_norm = din("ab_q_norm", [384]); ab_w_uq = din("ab_w_uq", [384, 768])
    ab_kv_norm = din("ab_kv_norm", [256]); ab_w_ukv = din("ab_w_ukv", [256, 1024]); ab_sinks = din("ab_sinks", [8])
    ab_w_out = din("ab_w_out", [D, D]); cd_w_in = din("cd_w_in", [D, 3080]); cd_b_forget = din("cd_b_forget", [8])
    cd_w_out = din("cd_w_out", [D, D])
    ln1_g = din("ln1_g", [2, D]); ln1_b = din("ln1_b", [2, D]); ln2_g = din("ln2_g", [2, D]); ln2_b = din("ln2_b", [2, D])
    w_gate = din("ffn_w_gate", [2, D, DFF]); w_up = din("ffn_w_up", [2, D, DFF]); w_down = din("ffn_w_down", [2, DFF, D])
    rope_cs = din("rope_cs", [2, 32, S]); mla_mask_d = din("mla_mask", [128, 128]); causal_mask_d = din("causal_mask", [128, 128])
    alibi_d = din("alibi", [128, 8, 2, 128]); ck_mask_d = din("ck_mask", [128, 5, 128]); ck_bias_d = din("ck_bias", [128, 8, 5, 128])
    y_out = nc.dram_tensor("y", [S, D], F32, kind="ExternalOutput").ap()

    QTa = dsc("QTa", [8, 96, S], BF16); KTa = dsc("KTa", [8, 96, S], BF16); Va = dsc("Va", [S, 8, 64], BF16)
    QTs = dsc("QTs", [8, 64, S], BF16); KTs = dsc("KTs", [2, 64, S], BF16); Vs = dsc("Vs", [S, 2, 64], BF16)
    QTf = dsc("QTf", [8, 70, S], BF16); KTf = dsc("KTf", [8, 70, S], BF16); Vf = dsc("Vf", [S, 8, 64], BF16)
    QTc = dsc("QTc", [8, 64, S], BF16); KTc = dsc("KTc", [8, 64, S], BF16); Vc = dsc("Vc", [S, 8, 64], BF16)
    OT = dsc("OT", [D, S], BF16)
    HT = dsc("HT", [DFF, S], BF16)
    X1 = dsc("X1", [S, D], F32); X2 = dsc("X2", [S, D], F32); X3 = dsc("X3", [S, D], F32)

    P = Prog(nc)
    es = ExitStack()
    A = Arena(nc, es, 176 * 1024)
    PP = [es.enter_context(nc.psum_tensor("pp%d" % i, [128, 1024], F32))[:, :] for i in range(4)]
    PS = []
    for i in range(4):
        PS += [PP[i][:, 0:512], PP[i][:, 512:1024]]

    def dma(q, out, in_, reads=(), writes=()):
        return P.op(q, lambda e: [e.dma_start(out=out, in_=in_)], reads, writes, ndma=1)

    def evac(eng, out, in_, reads, writes, scale=None):
        if eng == "act":
            if scale is None:
                return P.op("act", X.copy(out=out, in_=in_), reads, writes)
            return P.op("act", X.mul(out=out, in_=in_, mul=float(scale)), reads, writes)
        if scale is None:
            return P.op(eng, X.tensor_copy(out=out, in_=in_), reads, writes)
        return P.op(eng, X.tensor_scalar(out=out, in0=in_, scalar1=float(scale), scalar2=None, op0=ALU.mult), reads, writes)

    ident = A.alloc([128], F32); b_ident = Buf()
    P.op("pool", X.memset(ident, 1.0), writes=[b_ident])
    P.op("pool", X.affine_select(out=ident, in_=ident, pattern=[[-1, 128]], compare_op=ALU.is_equal, fill=0.0, base=0, channel_multiplier=1), reads=[b_ident], writes=[b_ident])
    ident_bf = A.alloc([128], BF16)
    P.op("dve", X.tensor_copy(out=ident_bf, in_=ident), reads=[b_ident], writes=[b_ident])
    ones_bf = A.alloc([128], BF16); b_ones = Buf()
    P.op("pool", X.memset(ones_bf, 1.0), writes=[b_ones])
    lng = A.alloc([4, D], F32); b_lng = Buf()
    eps_ln = A.alloc([1], F32); eps_rms = A.alloc([1], F32); b_eps = Buf()
    P.op("pool", X.memset(eps_ln, LN_EPS), writes=[b_eps])
    P.op("pool", X.memset(eps_rms, RMS_EPS), writes=[b_eps])

    def psbufs(n):
        return [Buf() for _ in range(n)]

    def pbufs(n):
        return [Buf(excl=True) for _ in range(n)]

    def load_x(blk, Xsrc, xt, b_xt):
        dma("sp", xt, Xsrc[blk * 512:(blk + 1) * 512, :].rearrange("(t p) d -> p t d", p=128), writes=[b_xt])

    def make_xT(xt, b_xt, xT, b_xT, pst, b_pst, cnt):
        for c in range(8):
            j = cnt[0] % len(pst); cnt[0] += 1
            for t in range(4):
                P.op("pe", X.transpose(out=pst[j][:, t * 128:(t + 1) * 128], in_=xt[:, t, c * 128:(c + 1) * 128], identity=ident),
                     reads=[b_xt, b_ident], writes=[b_pst[j]])
            evac("act" if c % 2 == 0 else "dve", xT[:, c, :], pst[j], [b_pst[j]], [b_xT])

    def ln_stage1(z, b_z, gi, tmp):
        st, mv, sc, b_st = tmp
        for hh in range(2):
            P.op("dve", X.bn_stats(out=st[:, hh, :], in_=z[:, hh * 512:(hh + 1) * 512]), reads=[b_z], writes=[b_st])
        P.op("dve", X.bn_aggr(out=mv, in_=st), reads=[b_st], writes=[b_st])
        P.op("act", X.activation(out=sc[:, 0:1], in_=mv[:, 1:2], func=AF.Ln, bias=eps_ln[:, 0:1]), reads=[b_st, b_eps], writes=[b_st])
        P.op("act", X.activation(out=sc[:, 0:1], in_=sc[:, 0:1], func=AF.Exp, scale=-0.5), reads=[b_st], writes=[b_st])

    def ln_stage2(z, b_z, gi, tmp):
        st, mv, sc, b_st = tmp
        P.op("dve", X.scalar_tensor_tensor(out=sc[:, 1:2], in0=mv[:, 0:1], scalar=-1.0, in1=sc[:, 0:1], op0=ALU.mult, op1=ALU.mult), reads=[b_st], writes=[b_st])
        P.op("act", X.activation(out=z, in_=z, func=AF.Identity, scale=sc[:, 0:1], bias=sc[:, 1:2]), reads=[b_z, b_st], writes=[b_z])

    def ln_stage3(z, b_z, gi, Xdst, rows):
        P.op("dve", X.tensor_tensor(out=z[:, 0:512], in0=z[:, 0:512], in1=lng[:, gi, 0:512], op=ALU.mult), reads=[b_z, b_lng], writes=[b_z])
        P.op("pool", X.tensor_tensor(out=z[:, 512:1024], in0=z[:, 512:1024], in1=lng[:, gi, 512:1024], op=ALU.mult), reads=[b_z, b_lng], writes=[b_z])
        P.op("pool", X.tensor_tensor(out=z, in0=z, in1=lng[:, gi + 1, :], op=ALU.add), reads=[b_z, b_lng], writes=[b_z])
        dma("sp", Xdst[rows, :], z, reads=[b_z])

    def load_ln_params(g_d, b_d, layer, gi):
        dma("sp", lng[:, gi, :], g_d[layer:layer + 1, :].partition_broadcast(128) if False else g_d[layer:layer + 1, :].broadcast_to([128, D]), writes=[b_lng])
        dma("sp", lng[:, gi + 1, :], b_d[layer:layer + 1, :].broadcast_to([128, D]), writes=[b_lng])

    def load_w_cast(dst, src_rows_by_cols, ncols, b):
        C = dst.shape[1]
        step = 1024
        pairs = []
        for c in range(C):
            for c0 in range(0, ncols, step):
                c1 = min(ncols, c0 + step)
                pairs.append((dst[:, c, c0:c1], src_rows_by_cols[c * 128:(c + 1) * 128, c0:c1]))
        GR = 8
        for g0 in range(0, len(pairs), GR):
            grp = pairs[g0:g0 + GR]
            P.op("poolq", (lambda grp: (lambda e: [e.dma_start(out=o, in_=i) for (o, i) in grp]))(grp), writes=[Buf()], ndma=len(grp))
        P.wgroups.setdefault(id(b), []).extend(P.streams["pool"][-((len(pairs) + GR - 1) // GR):])
        b.w = None

    def load_w_colblocks(dsts_srcs, ncols, cb=512):
        out = [[] for _ in dsts_srcs]
        for c0 in range(0, ncols, cb):
            c1 = min(ncols, c0 + cb)
            for wi, (dst, src) in enumerate(dsts_srcs):
                grp = [(dst[:, :, c0:c1], src.rearrange("(c p) f -> p c f", p=128)[:, :, c0:c1])]
                b = Buf()
                P.op("poolq", (lambda grp: (lambda e: [e.dma_start(out=o, in_=i) for (o, i) in grp]))(grp), writes=[b], ndma=len(grp))
                out[wi].append((c1, b))
        return out

    def colbuf(blist, c_hi):
        for (c1, b) in blist:
            if c_hi <= c1:
                return b
        raise AssertionError

    def attention_full(QT_d, KT_d, V_d, R, mask_d, orow0):
        A.mark()
        mask32 = A.alloc([128], F32); mask = A.alloc([128], BF16); b_mask = Buf()
        dma("sp", mask32, mask_d, writes=[b_mask])
        P.op("dve", X.tensor_copy(out=mask, in_=mask32), reads=[b_mask], writes=[b_mask])
        sets = []
        for i in range(2):
            qt = A.alloc([S], BF16); kt_ = A.alloc([S], BF16); va = A.alloc([NT, 128], BF16)
            bq, bk, bv = Buf(), Buf(), Buf()
            P.op("pool", X.memset(va[:, :, 64:128], 1.0), writes=[bv])
            sets.append((qt, kt_, va, bq, bk, bv))
        NPB = 3
        PTs = [A.alloc([1024], BF16) for _ in range(NPB)]; b_PT = psbufs(NPB)
        Rc = [A.alloc([512], F32) for _ in range(2)]; b_Rc = psbufs(2)
        On = [A.alloc([512], BF16) for _ in range(2)]; b_On = psbufs(2)
        b_S = pbufs(NPB); b_O = pbufs(2)
        Sp = PP[0:3]; Ob = [PP[3][:, 0:512], PP[3][:, 512:1024]]

        def load_head(h):
            qt, kt_, va, bq, bk, bv = sets[h % 2]
            dma("sp", qt[0:R, :], QT_d[h], writes=[bq])
            dma("sp", kt_[0:R, :], KT_d[h], writes=[bk])
            dma("sp", va[:, :, 0:64], V_d[:, h, :].rearrange("(k p) e -> p k e", p=128), writes=[bv])
        load_head(0)
        gi = [0]
        for h in range(8):
            if h + 1 < 8:
                load_head(h + 1)
            qt, kt_, va, bq, bk, bv = sets[h % 2]
            pairs = [(Q, kp) for Q in range(NB) for kp in range(2 * Q + 2)]
            base = gi[0]

            def col0(Q, kt):
                return 0 if kt < 4 * Q else 128 * (kt - 4 * Q)

            def qk(i):
                Q, kp = pairs[i]; j = (base + i) % NPB
                for u in range(2):
                    kt = 2 * kp + u; n0 = col0(Q, kt); diag = kt >= 4 * Q
                    Sv = Sp[j][:, u * 512:(u + 1) * 512]
                    if not diag:
                        P.op("pe", X.matmul(Sv[:, 0:512], lhsT=kt_[0:R, kt * 128:(kt + 1) * 128], rhs=qt[0:R, Q * 512:(Q + 1) * 512], start=True, stop=True),
                             reads=[bq, bk], writes=[b_S[j]])
                    else:
                        if n0 + 128 < 512:
                            P.op("pe", X.matmul(Sv[:, n0 + 128:512], lhsT=kt_[0:R, kt * 128:(kt + 1) * 128], rhs=qt[0:R, Q * 512 + n0 + 128:(Q + 1) * 512], start=True, stop=True),
                                 reads=[bq, bk], writes=[b_S[j]])
                        P.op("pe", X.matmul(Sv[:, n0:n0 + 128], lhsT=kt_[0:R, kt * 128:(kt + 1) * 128], rhs=qt[0:R, Q * 512 + n0:Q * 512 + n0 + 128], start=True, stop=False),
                             reads=[bq, bk], writes=[b_S[j]])
                        P.op("pe", X.matmul(Sv[:, n0:n0 + 128], lhsT=ident_bf, rhs=mask, start=False, stop=True),
                             reads=[b_ident, b_mask], writes=[b_S[j]])

            def ex(i):
                Q, kp = pairs[i]; j = (base + i) % NPB
                if 2 * kp + 1 < 4 * Q:
                    P.op("act", X.activation(out=PTs[j], in_=Sp[j], func=AF.Exp), reads=[b_S[j]], writes=[b_PT[j]])
                else:
                    for u in range(2):
                        n0 = col0(Q, 2 * kp + u)
                        P.op("act", X.activation(out=PTs[j][:, u * 512 + n0:(u + 1) * 512], in_=Sp[j][:, u * 512 + n0:(u + 1) * 512], func=AF.Exp), reads=[b_S[j]], writes=[b_PT[j]])

            def pv(i):
                Q, kp = pairs[i]; j = (base + i) % NPB; ob = Q % 2
                for u in range(2):
                    kt = 2 * kp + u; n0 = col0(Q, kt)
                    P.op("pe", X.matmul(Ob[ob][:, n0:512], lhsT=va[:, kt, :], rhs=PTs[j][:, u * 512 + n0:(u + 1) * 512], start=(kt == 0), stop=(kt == 4 * Q + 3)),
                         reads=[bv, b_PT[j]], writes=[b_O[ob]])

            def norm(Q):
                ob = Q % 2
                P.op("dve", X.reciprocal(out=Rc[ob][0:64, :], in_=Ob[ob][64:128, :]), reads=[b_O[ob]], writes=[b_Rc[ob]])
                P.op("dve", X.tensor_tensor(out=On[ob][0:64, :], in0=Ob[ob][0:64, :], in1=Rc[ob][0:64, :], op=ALU.mult), reads=[b_O[ob], b_Rc[ob]], writes=[b_On[ob]])
                dma("sp", OT[orow0 + h * 64:orow0 + (h + 1) * 64, Q * 512:(Q + 1) * 512], On[ob][0:64, :], reads=[b_On[ob]])
            n = len(pairs)
            qk(0)
            if n > 1:
                qk(1)
            for i in range(n):
                if i + 2 < n:
                    qk(i + 2)
                ex(i)
                pv(i)
                Q, kp = pairs[i]
                if kp == 2 * Q + 1:
                    norm(Q)
            gi[0] += n
        A.release()
        P.barrier()

    def attention_band(QT_d, KT_d, V_d, nkv, omax, bhi, b_bias, orow0, sink=None, b_sink=None):
        A.mark()
        G = 8 // nkv
        sets = []
        for i in range(2):
            qt = A.alloc([S], BF16); bq = Buf()
            sets.append((qt, bq))
        ksets = []
        for i in range(2):
            kt_ = A.alloc([S], BF16); va = A.alloc([NT, 128], BF16); bk, bv = Buf(), Buf()
            P.op("pool", X.memset(va[:, :, 64:128], 1.0), writes=[bv])
            ksets.append((kt_, va, bk, bv))
        NSB = 6; LA = 4
        PTs = [A.alloc([512], BF16) for _ in range(NSB)]; b_PT = psbufs(NSB)
        Rc = [A.alloc([512], F32) for _ in range(2)]; b_Rc = psbufs(2)
        On = [A.alloc([512], BF16) for _ in range(2)]; b_On = psbufs(2)
        b_S = pbufs(NSB); b_O = pbufs(2)
        Sb = PS[0:NSB]; Ob = PS[6:8]

        def load_q(h):
            qt, bq = sets[h % 2]
            dma("sp", qt[0:64, :], QT_d[h], writes=[bq])

        def load_kv(g):
            kt_, va, bk, bv = ksets[g % 2]
            dma("sp", kt_[0:64, :], KT_d[g], writes=[bk])
            dma("sp", va[:, :, 0:64], V_d[:, g, :].rearrange("(k p) e -> p k e", p=128), writes=[bv])
        load_q(0); load_kv(0)
        gi = [0]
        for h in range(8):
            g = h // G
            if h + 1 < 8:
                load_q(h + 1)
                if (h + 1) // G != g:
                    load_kv((h + 1) // G)
            qt, bq = sets[h % 2]
            kt_, va, bk, bv = ksets[g % 2]
            steps = []
            for Q in range(NB):
                kts = list(range(max(0, 4 * Q - omax), 4 * Q + 4))
                for kt in kts:
                    m0 = max(kt, 4 * Q); m1 = min(kt + omax, 4 * Q + 3)
                    steps.append((Q, kt, m0, m1, kt == kts[0], kt == kts[-1]))
            base = gi[0]

            def qk(i):
                Q, kt, m0, m1, first, last = steps[i]; j = (base + i) % NSB
                c0 = (m0 - 4 * Q) * 128; c1 = (m1 - 4 * Q + 1) * 128
                o0 = m0 - kt; o1 = m1 - kt
                P.op("pe", X.matmul(Sb[j][:, c0:c1], lhsT=kt_[0:64, kt * 128:(kt + 1) * 128], rhs=qt[0:64, Q * 512 + c0:Q * 512 + c1], start=True, stop=False),
                     reads=[bq, bk], writes=[b_S[j]])
                P.op("pe", X.matmul(Sb[j][:, c0:c1].rearrange("p (o q) -> p o q", q=128), lhsT=ident_bf, rhs=bhi[:, h, o0:o1 + 1, :], start=False, stop=True),
                     reads=[b_ident, b_bias], writes=[b_S[j]])

            def ex(i):
                Q, kt, m0, m1, first, last = steps[i]; j = (base + i) % NSB
                c0 = (m0 - 4 * Q) * 128; c1 = (m1 - 4 * Q + 1) * 128
                P.op("act", X.activation(out=PTs[j][:, c0:c1], in_=Sb[j][:, c0:c1], func=AF.Exp), reads=[b_S[j]], writes=[b_PT[j]])

            def pv(i):
                Q, kt, m0, m1, first, last = steps[i]; j = (base + i) % NSB; ob = Q % 2
                c0 = (m0 - 4 * Q) * 128; c1 = (m1 - 4 * Q + 1) * 128
                P.op("pe", X.matmul(Ob[ob][:, c0:c1], lhsT=va[:, kt, :], rhs=PTs[j][:, c0:c1], start=first, stop=last, skip_group_check=True),
                     reads=[bv, b_PT[j]], writes=[b_O[ob]])

            def norm(Q):
                ob = Q % 2
                if sink is not None:
                    P.op("dve", X.tensor_scalar(out=Rc[ob][0:64, :], in0=Ob[ob][64:128, :], scalar1=sink[64:128, h:h + 1], scalar2=None, op0=ALU.add), reads=[b_O[ob], b_sink], writes=[b_Rc[ob]])
                    P.op("act", X.activation(out=Rc[ob][0:64, :], in_=Rc[ob][0:64, :], func=AF.Ln), reads=[b_Rc[ob]], writes=[b_Rc[ob]])
                    P.op("act", X.activation(out=Rc[ob][0:64, :], in_=Rc[ob][0:64, :], func=AF.Exp, scale=-1.0), reads=[b_Rc[ob]], writes=[b_Rc[ob]])
                else:
                    P.op("dve", X.reciprocal(out=Rc[ob][0:64, :], in_=Ob[ob][64:128, :]), reads=[b_O[ob]], writes=[b_Rc[ob]])
                P.op("dve", X.tensor_tensor(out=On[ob][0:64, :], in0=Ob[ob][0:64, :], in1=Rc[ob][0:64, :], op=ALU.mult), reads=[b_O[ob], b_Rc[ob]], writes=[b_On[ob]])
                dma("sp", OT[orow0 + h * 64:orow0 + (h + 1) * 64, Q * 512:(Q + 1) * 512], On[ob][0:64, :], reads=[b_On[ob]])
            for i in range(min(LA, len(steps))):
                qk(i)
            pending = []
            for i in range(len(steps)):
                if i + LA < len(steps):
                    qk(i + LA)
                ex(i)
                pv(i)
                while pending and pending[0][0] <= i:
                    norm(pending.pop(0)[1])
                if steps[i][5]:
                    pending.append((i + 3, steps[i][0]))
            for _, Qp in pending:
                norm(Qp)
            gi[0] += len(steps)
        A.release()
        P.barrier()

    def outproj_ln(Wo, b_Wo, Xsrc, Xdst, layer):
        A.mark()
        load_ln_params(ln1_g, ln1_b, layer, 0)
        NZ = 5
        tmps = [(A.alloc([2, 6], F32), A.alloc([2], F32), A.alloc([2], F32), Buf()) for _ in range(NZ)]
        xts = [A.alloc([4, D], F32) for _ in range(3)]; b_xt = psbufs(3)
        ots = [A.alloc([8, 512], BF16) for _ in range(3)]; b_ot = psbufs(3)
        zs = [A.alloc([D], F32) for _ in range(NZ)]; b_z = psbufs(NZ)
        b_ps = pbufs(4); pi = [0]

        def loads(blk):
            dma("sp", xts[blk % 3], Xsrc[blk * 512:(blk + 1) * 512, :].rearrange("(t p) d -> p t d", p=128), writes=[b_xt[blk % 3]])
            dma("sp", ots[blk % 3], OT[:, blk * 512:(blk + 1) * 512].rearrange("(c p) s -> p c s", p=128), writes=[b_ot[blk % 3]])

        def s1(i):
            blk, t = divmod(i, 4)
            if t == 0 and blk + 2 < NB:
                loads(blk + 2)
            xt, bx = xts[blk % 3], b_xt[blk % 3]; ot, bo = ots[blk % 3], b_ot[blk % 3]
            z, bz = zs[i % NZ], b_z[i % NZ]
            for hh in range(2):
                pj = pi[0] % 4; pi[0] += 1
                for c in range(8):
                    P.op("pe", X.matmul(PS[pj], lhsT=ot[:, c, t * 128:(t + 1) * 128], rhs=Wo[:, c, hh * 512:(hh + 1) * 512], start=(c == 0), stop=(c == 7)),
                         reads=[bo, b_Wo], writes=[b_ps[pj]])
                P.op("dve", X.scalar_tensor_tensor(out=z[:, hh * 512:(hh + 1) * 512], in0=xt[:, t, hh * 512:(hh + 1) * 512], scalar=ALPHA, in1=PS[pj], op0=ALU.mult, op1=ALU.add),
                     reads=[bx, b_ps[pj]], writes=[bz])
            ln_stage1(z, bz, 0, tmps[i % NZ])

        def s2(i):
            ln_stage2(zs[i % NZ], b_z[i % NZ], 0, tmps[i % NZ])

        def s3(i):
            blk, t = divmod(i, 4)
            ln_stage3(zs[i % NZ], b_z[i % NZ], 0, Xdst, slice(blk * 512 + t * 128, blk * 512 + (t + 1) * 128))
        loads(0); loads(1)
        s1(0); s1(1); s2(0)
        for i in range(NT):
            if i + 2 < NT:
                s1(i + 2)
            if i + 1 < NT:
                s2(i + 1)
            s3(i)
        A.release()
        P.barrier()

    def ffn(layer, Xsrc, Xdst):
        A.mark()
        NFA = 14
        Wd_a = A.alloc([NFA, D], BF16)
        A.mark()
        Wg = A.alloc([8, DFF], BF16); Wu = A.alloc([8, DFF], BF16)
        bl_Wg, bl_Wu = load_w_colblocks([(Wg, w_gate[layer]), (Wu, w_up[layer])], DFF, cb=256)
        b_Wd = []
        for f0 in range(0, NFA, 2):
            grp = [(Wd_a[:, f, :], w_down[layer][f * 128:(f + 1) * 128, :]) for f in (f0, f0 + 1)]
            b = Buf()
            P.op("poolq", (lambda grp: (lambda e: [e.dma_start(out=o, in_=i) for (o, i) in grp]))(grp), writes=[b], ndma=2)
            b_Wd += [b, b]
        xts = [A.alloc([4, D], F32)] * 2; b_xt = [Buf()] * 2
        xTs = [A.alloc([8, 512], BF16) for _ in range(2)]; b_xT = psbufs(2)
        sgs = [A.alloc([512], F32) for _ in range(2)]; b_sg = psbufs(2)
        hts = [A.alloc([512], BF16) for _ in range(4)]; b_ht = psbufs(4)
        pst = PS[0:2]; b_pst = pbufs(2); cnt = [0]
        b_g = pbufs(2); b_u = pbufs(2); fi = 0
        load_x(0, Xsrc, xts[0], b_xt[0])
        for blk in range(NB):
            xt, bx = xts[blk % 2], b_xt[blk % 2]; xT, bxT = xTs[blk % 2], b_xT[blk % 2]
            make_xT(xt, bx, xT, bxT, pst, b_pst, cnt)
            if blk + 1 < NB:
                load_x(blk + 1, Xsrc, xts[(blk + 1) % 2], b_xt[(blk + 1) % 2])
            for f in range(NF):
                j = fi % 2; hj = fi % 4; fi += 1
                pg, pu = PS[2 + j], PS[4 + j]
                for c in range(8):
                    P.op("pe", X.matmul(pg, lhsT=Wg[:, c, f * 128:(f + 1) * 128], rhs=xT[:, c, :], start=(c == 0), stop=(c == 7)), reads=[colbuf(bl_Wg, (f + 1) * 128), bxT], writes=[b_g[j]])
                for c in range(8):
                    P.op("pe", X.matmul(pu, lhsT=Wu[:, c, f * 128:(f + 1) * 128], rhs=xT[:, c, :], start=(c == 0), stop=(c == 7)), reads=[colbuf(bl_Wu, (f + 1) * 128), bxT], writes=[b_u[j]])
                P.op("act", X.activation(out=sgs[j], in_=pg, func=AF.Silu), reads=[b_g[j]], writes=[b_sg[j]])
                P.op("dve", X.tensor_tensor(out=hts[hj], in0=pu, in1=sgs[j], op=ALU.mult), reads=[b_u[j], b_sg[j]], writes=[b_ht[hj]])
                dma("sp", HT[f * 128:(f + 1) * 128, blk * 512:(blk + 1) * 512], hts[hj], reads=[b_ht[hj]])
        A.release()
        P.barrier()
        A.mark()
        Wd_b = A.alloc([NF - NFA, D], BF16)
        for f0 in range(NFA, NF, 2):
            grp = [(Wd_b[:, f - NFA, :], w_down[layer][f * 128:(f + 1) * 128, :]) for f in (f0, f0 + 1)]
            b = Buf()
            P.op("poolq", (lambda grp: (lambda e: [e.dma_start(out=o, in_=i) for (o, i) in grp]))(grp), writes=[b], ndma=2)
            b_Wd += [b, b]

        def Wd_of(f):
            return Wd_a[:, f] if f < NFA else Wd_b[:, f - NFA]
        load_ln_params(ln2_g, ln2_b, layer, 2)
        NZ = 5
        tmps = [(A.alloc([2, 6], F32), A.alloc([2], F32), A.alloc([2], F32), Buf()) for _ in range(NZ)]
        xts = [A.alloc([4, D], F32) for _ in range(2)]; b_xt = psbufs(2)
        hbs = [A.alloc([NF, 512], BF16) for _ in range(2)]; b_hb = psbufs(2)
        zs = [A.alloc([D], F32) for _ in range(NZ)]; b_z = psbufs(NZ)
        b_ps = pbufs(4); pi = [0]

        def loads(blk):
            dma("sp", xts[blk % 2], Xsrc[blk * 512:(blk + 1) * 512, :].rearrange("(t p) d -> p t d", p=128), writes=[b_xt[blk % 2]])
            dma("sp", hbs[blk % 2], HT[:, blk * 512:(blk + 1) * 512].rearrange("(f p) s -> p f s", p=128), writes=[b_hb[blk % 2]])

        def s1(i):
            blk, t = divmod(i, 4)
            if t == 0 and blk + 1 < NB:
                loads(blk + 1)
            xt, bx = xts[blk % 2], b_xt[blk % 2]; hb, bh = hbs[blk % 2], b_hb[blk % 2]
            z, bz = zs[i % NZ], b_z[i % NZ]
            for hh in range(2):
                pj = pi[0] % 4; pi[0] += 1
                for f in range(NF):
                    P.op("pe", X.matmul(PS[pj], lhsT=hb[:, f, t * 128:(t + 1) * 128], rhs=Wd_of(f)[:, hh * 512:(hh + 1) * 512], start=(f == 0), stop=(f == NF - 1)),
                         reads=[bh, b_Wd[f]], writes=[b_ps[pj]])
                P.op("dve", X.scalar_tensor_tensor(out=z[:, hh * 512:(hh + 1) * 512], in0=xt[:, t, hh * 512:(hh + 1) * 512], scalar=ALPHA, in1=PS[pj], op0=ALU.mult, op1=ALU.add),
                     reads=[bx, b_ps[pj]], writes=[bz])
            ln_stage1(z, bz, 2, tmps[i % NZ])

        def s2(i):
            ln_stage2(zs[i % NZ], b_z[i % NZ], 2, tmps[i % NZ])

        def s3(i):
            blk, t = divmod(i, 4)
            ln_stage3(zs[i % NZ], b_z[i % NZ], 2, Xdst, slice(blk * 512 + t * 128, blk * 512 + (t + 1) * 128))
        loads(0)
        s1(0); s1(1); s2(0)
        for i in range(NT):
            if i + 2 < NT:
                s1(i + 2)
            if i + 1 < NT:
                s2(i + 1)
            s3(i)
        A.release()
        P.barrier()
        A.release()

    def inproj_ab(Xsrc):
        A.mark()
        Win = A.alloc([8, 1440], BF16); b_Win = Buf()
        load_w_cast(Win, ab_w_in, 1440, b_Win)
        Wuq = A.alloc([3, 768], BF16); b_Wuq = Buf()
        load_w_cast(Wuq, ab_w_uq, 768, b_Wuq)
        Wukv = A.alloc([2, 1024], BF16); b_Wukv = Buf()
        load_w_cast(Wukv, ab_w_ukv, 1024, b_Wukv)
        Wkr_rot = A.alloc([8, 32], BF16); Wuq_rot = A.alloc([3, 8, 32], BF16); b_rot = Buf()
        P.op("act", X.mul(out=Wkr_rot[:, :, 0:16], in_=Win[:, :, 656:672], mul=-1.0), reads=[b_Win], writes=[b_rot])
        P.op("act", X.copy(out=Wkr_rot[:, :, 16:32], in_=Win[:, :, 640:656]), reads=[b_Win], writes=[b_rot])
        Wuq4 = Wuq.rearrange("p c (h e) -> p c h e", e=96)
        for c in range(3):
            P.op("act", X.mul(out=Wuq_rot[:, c, :, 0:16], in_=Wuq4[:, c, :, 80:96], mul=-1.0), reads=[b_Wuq], writes=[b_rot])
            P.op("act", X.copy(out=Wuq_rot[:, c, :, 16:32], in_=Wuq4[:, c, :, 64:80]), reads=[b_Wuq], writes=[b_rot])
        Wq_nope = A.alloc([3, 512], BF16); Wq_rope = A.alloc([3, 256], BF16); Wk_nope = A.alloc([2, 512], BF16); b_wc = Buf()
        Wukv4 = Wukv.rearrange("p c (h e) -> p c h e", e=128)
        for c in range(3):
            P.op("dve", X.tensor_copy(out=Wq_nope[:, c, :].rearrange("p (h e) -> p h e", e=64), in_=Wuq4[:, c, :, 0:64]), reads=[b_Wuq], writes=[b_wc])
            P.op("pool", X.tensor_copy(out=Wq_rope[:, c, :].rearrange("p (h e) -> p h e", e=32), in_=Wuq4[:, c, :, 64:96]), reads=[b_Wuq], writes=[b_wc])
        for c in range(2):
            P.op("dve", X.tensor_copy(out=Wk_nope[:, c, :].rearrange("p (h e) -> p h e", e=64), in_=Wukv4[:, c, :, 0:64]), reads=[b_Wukv], writes=[b_wc])
        Wq_rot = Wuq_rot.rearrange("p c h e -> p c (h e)")
        gn = A.alloc([5], F32); b_gn = Buf()
        dma("sp", gn[:, 0:3], ab_q_norm.rearrange("(c p) -> p c", p=128), writes=[b_gn])
        dma("sp", gn[:, 3:5], ab_kv_norm.rearrange("(c p) -> p c", p=128), writes=[b_gn])
        xts = [A.alloc([4, D], F32) for _ in range(2)]; b_xt = psbufs(2)
        xTs = [A.alloc([8, 512], BF16) for _ in range(2)]; b_xT = psbufs(2)
        cs = [A.alloc([2, 512], F32) for _ in range(2)]; b_cs = psbufs(2)
        c32 = A.alloc([5, 512], F32); b_c32 = Buf()
        sq = A.alloc([5, 512], BF16); b_sq = Buf()
        rbc = A.alloc([2, 512], F32); b_rbc = Buf()
        cn = A.alloc([5, 512], BF16); b_cn = Buf()
        NST = 6
        stg = [A.alloc([512], BF16) for _ in range(NST)]; b_stg = psbufs(NST)
        t1s = [A.alloc([512], F32) for _ in range(2)]; t2s = [A.alloc([512], F32) for _ in range(2)]; b_t = psbufs(2)
        vst = [A.alloc([4, 512], BF16) for _ in range(2)]; b_vst = psbufs(2)
        vss = [A.alloc([4, 128], BF16) for _ in range(2)]; b_vss = psbufs(2)
        pst = PS[0:2]; b_pst = pbufs(2); cnt = [0]
        b_ps = pbufs(6); pi = [0]; si = [0]; ti = [0]
        SC_MLA = 96.0 ** -0.5
        SC_SWA = 64.0 ** -0.5

        def nps():
            j = pi[0] % 6; pi[0] += 1
            return PS[2 + j], b_ps[j]

        def nstg():
            j = si[0] % NST; si[0] += 1
            return stg[j], b_stg[j]

        def fm_proj(ps, bps, M, wsl, W_b, rhs_of_c, nchunks, rhs_b):
            for c in range(nchunks):
                P.op("pe", X.matmul(ps[0:M, :], lhsT=wsl(c), rhs=rhs_of_c(c), start=(c == 0), stop=(c == nchunks - 1)), reads=[W_b, rhs_b], writes=[bps])

        def rope_combine(psA, bA, psB, bB, cst, bcs, scale, dst_dram, np_=32):
            j = ti[0] % 2; ti[0] += 1
            t1, t2, bt = t1s[j], t2s[j], b_t[j]
            P.op("dve", X.scalar_tensor_tensor(out=t1[0:np_, :], in0=psA[0:np_, :], scalar=float(scale), in1=cst[0:np_, 0, :], op0=ALU.mult, op1=ALU.mult), reads=[bA, bcs], writes=[bt])
            P.op("dve", X.scalar_tensor_tensor(out=t2[0:np_, :], in0=psB[0:np_, :], scalar=float(scale), in1=cst[0:np_, 1, :], op0=ALU.mult, op1=ALU.mult), reads=[bB, bcs], writes=[bt])
            s, bs = nstg()
            P.op("pool", X.tensor_tensor(out=s[0:np_, :], in0=t1[0:np_, :], in1=t2[0:np_, :], op=ALU.add), reads=[bt], writes=[bs])
            return s, bs

        for blk in range(NB):
            cols = slice(blk * 512, (blk + 1) * 512)
            xt, bx = xts[blk % 2], b_xt[blk % 2]; xT, bxT = xTs[blk % 2], b_xT[blk % 2]
            cst, bcs = cs[blk % 2], b_cs[blk % 2]
            if blk == 0:
                load_x(0, Xsrc, xts[0], b_xt[0])
            if blk + 1 < NB:
                load_x(blk + 1, Xsrc, xts[(blk + 1) % 2], b_xt[(blk + 1) % 2])
            make_xT(xt, bx, xT, bxT, pst, b_pst, cnt)
            P.op("sp", (lambda cst, cols: (lambda e: [e.dma_start(out=cst[32 * r:32 * r + 32], in_=rope_cs[:, :, cols].rearrange("a p s -> p a s")) for r in range(4)]))(cst, cols), writes=[bcs], ndma=4)
            if dbg and blk == 0:
                dxt = dsc("DBG_xT", [128, 8, 512], BF16); dwin = dsc("DBG_Win", [128, 8, 1440], BF16)
                dma("sp", dxt, xT, reads=[bxT]); dma("sp", dwin, Win, reads=[b_Win])
            if DBGSTOP == 2: continue
            for i in range(5):
                ps, bps = nps()
                fm_proj(ps, bps, 128, lambda c, i=i: Win[:, c, i * 128:(i + 1) * 128], b_Win, lambda c: xT[:, c, :], 8, bxT)
                P.op("dve", X.tensor_copy(out=c32[:, i, :], in_=ps), reads=[bps], writes=[b_c32])
                P.op("act", X.activation(out=sq[:, i, :], in_=ps, func=AF.Square), reads=[bps], writes=[b_sq])
            for (k, i0, n, R_) in ((0, 0, 3, 384.0), (1, 3, 2, 256.0)):
                ps, bps = nps()
                for i in range(n):
                    P.op("pe", X.matmul(ps, lhsT=ones_bf, rhs=sq[:, i0 + i, :], start=(i == 0), stop=(i == n - 1)), reads=[b_ones, b_sq], writes=[bps])
                P.op("act", X.activation(out=rbc[:, k, :], in_=ps, func=AF.Ln, scale=1.0 / R_, bias=eps_rms[:, 0:1]), reads=[bps, b_eps], writes=[b_rbc])
                P.op("act", X.activation(out=rbc[:, k, :], in_=rbc[:, k, :], func=AF.Exp, scale=-0.5), reads=[b_rbc], writes=[b_rbc])
                for i in range(n):
                    P.op("dve", X.scalar_tensor_tensor(out=cn[:, i0 + i, :], in0=c32[:, i0 + i, :], scalar=gn[:, i0 + i:i0 + i + 1], in1=rbc[:, k, :], op0=ALU.mult, op1=ALU.mult),
                         reads=[b_c32, b_gn, b_rbc], writes=[b_cn])
            if DBGSTOP == 3: continue
            psA, bA = nps(); psB, bB = nps()
            fm_proj(psA, bA, 32, lambda c: Win[:, c, 640:672], b_Win, lambda c: xT[:, c, :], 8, bxT)
            fm_proj(psB, bB, 32, lambda c: Wkr_rot[:, c, :], b_rot, lambda c: xT[:, c, :], 8, bxT)
            s, bs = rope_combine(psA, bA, psB, bB, cst, bcs, 1.0, None)
            for h in range(8):
                dma("sp", KTa[h, 0:32, cols], s[0:32, :], reads=[bs])
            if DBGSTOP == 4: continue
            for i in range(4):
                ps, bps = nps()
                fm_proj(ps, bps, 128, lambda c, i=i: Win[:, c, 672 + i * 128:672 + (i + 1) * 128], b_Win, lambda c: xT[:, c, :], 8, bxT)
                s, bs = nstg()
                evac("act" if i % 2 == 0 else "dve", s, ps, [bps], [bs], scale=SC_SWA)
                dma("sp", QTs[2 * i:2 * i + 2, :, cols].rearrange("h p s -> (h p) s"), s, reads=[bs])
            ps, bps = nps()
            fm_proj(ps, bps, 128, lambda c: Win[:, c, 1184:1312], b_Win, lambda c: xT[:, c, :], 8, bxT)
            s, bs = nstg()
            evac("act", s, ps, [bps], [bs])
            dma("sp", KTs[:, :, cols].rearrange("h p s -> (h p) s"), s, reads=[bs])
            ps, bps = nps()
            for t in range(4):
                for c in range(8):
                    P.op("pe", X.matmul(ps[:, t * 128:(t + 1) * 128], lhsT=xT[:, c, t * 128:(t + 1) * 128], rhs=Win[:, c, 1312:1440], start=(c == 0), stop=(c == 7)), reads=[bxT, b_Win], writes=[bps])
            vs_, bvs = vss[blk % 2], b_vss[blk % 2]
            evac("dve", vs_, ps.rearrange("p (t e) -> p t e", e=128), [bps], [bvs])
            dma("sp", Vs[cols].rearrange("(t p) g e -> p t (g e)", p=128), vs_, reads=[bvs])
            if DBGSTOP == 5: continue
            Wuq4_ = Wuq.rearrange("p c (h e) -> p c h e", e=96)
            for g in range(4):
                ps, bps = nps()
                fm_proj(ps, bps, 128, lambda c, g=g: Wq_nope[:, c, 128 * g:128 * g + 128], b_wc, lambda c: cn[:, c, :], 3, b_cn)
                s, bs = nstg()
                evac("act", s, ps, [bps], [bs], scale=SC_MLA)
                dma("sp", QTa[2 * g, 32:96, cols], s[0:64, :], reads=[bs])
                dma("sp", QTa[2 * g + 1, 32:96, cols], s[64:128, :], reads=[bs])
            for g in range(2):
                psA, bA = nps(); psB, bB = nps()
                fm_proj(psA, bA, 128, lambda c, g=g: Wq_rope[:, c, 128 * g:128 * g + 128], b_wc, lambda c: cn[:, c, :], 3, b_cn)
                fm_proj(psB, bB, 128, lambda c, g=g: Wq_rot[:, c, 128 * g:128 * g + 128], b_rot, lambda c: cn[:, c, :], 3, b_cn)
                s, bs = rope_combine(psA, bA, psB, bB, cst, bcs, SC_MLA, None, np_=128)
                for r in range(4):
                    dma("sp", QTa[4 * g + r, 0:32, cols], s[32 * r:32 * r + 32, :], reads=[bs])
            if DBGSTOP == 6: continue
            Wukv4_ = Wukv.rearrange("p c (h e) -> p c h e", e=128)
            for g in range(4):
                ps, bps = nps()
                fm_proj(ps, bps, 128, lambda c, g=g: Wk_nope[:, c, 128 * g:128 * g + 128], b_wc, lambda c: cn[:, 3 + c, :], 2, b_cn)
                s, bs = nstg()
                evac("dve" if g % 2 == 0 else "act", s, ps, [bps], [bs])
                dma("sp", KTa[2 * g, 32:96, cols], s[0:64, :], reads=[bs])
                dma("sp", KTa[2 * g + 1, 32:96, cols], s[64:128, :], reads=[bs])
            if DBGSTOP == 7: continue
            vt, bvt = vst[blk % 2], b_vst[blk % 2]
            for t in range(4):
                ps, bps = nps()
                for c in range(2):
                    P.op("pe", X.matmul(ps.rearrange("p (h e) -> p h e", e=64), lhsT=cn[:, 3 + c, t * 128:(t + 1) * 128], rhs=Wukv[:, c, :].rearrange("p (h e) -> p h e", e=128)[:, :, 64:128], start=(c == 0), stop=(c == 1)),
                         reads=[b_cn, b_Wukv], writes=[bps])
                evac("act" if t % 2 == 0 else "dve", vt[:, t, :], ps, [bps], [bvt])
            dma("sp", Va[cols].rearrange("(t p) h e -> p t (h e)", p=128), vt, reads=[bvt])
        A.release()
        P.barrier()

    def inproj_cd(Xsrc):
        A.mark()
        Win = A.alloc([8, 3080], BF16)
        bl_Win = load_w_colblocks([(Win, cd_w_in)], 3080, cb=440)[0]

        def b_Win_of(c_lo, c_hi):
            bs = []
            for (c1, b) in bl_Win:
                if c1 > c_lo and c1 - 440 < c_hi:
                    bs.append(b)
            return bs
        nb = A.alloc([1], F32); b_nb = Buf()
        dma("sp", nb[0:8, :], cd_b_forget.rearrange("(h o) -> h o", o=1), writes=[b_nb])
        P.op("dve", X.tensor_scalar(out=nb[0:8, :], in0=nb[0:8, :], scalar1=-1.0, scalar2=None, op0=ALU.mult), reads=[b_nb], writes=[b_nb])
        ones32 = A.alloc([512], F32); b_o32 = Buf()
        P.op("pool", X.memset(ones32, 1.0), writes=[b_o32])
        onesS = A.alloc([3, 512], BF16); b_oS = Buf()
        P.op("pool", X.memset(onesS[0:8], 1.0), writes=[b_oS])
        cum = A.alloc([S], F32); b_cum = Buf()
        xts = [A.alloc([4, D], F32)] * 2; b_xt = [Buf()] * 2
        xTs = [A.alloc([8, 512], BF16) for _ in range(2)]; b_xT = psbufs(2)
        NST = 6
        stg = [A.alloc([512], BF16) for _ in range(NST)]; b_stg = psbufs(NST)
        vst = [A.alloc([4, 512], BF16) for _ in range(2)] * 2; b_vst = psbufs(2) * 2
        ls = A.alloc([512], F32); b_ls = Buf()
        r1 = A.alloc([512], F32); r2 = A.alloc([512], F32); b_r = Buf()
        augk = [A.alloc([3, 512], BF16) for _ in range(2)]; augq = [A.alloc([3, 512], BF16) for _ in range(2)]; b_aug = psbufs(2)
        pst = PS[0:2]; b_pst = pbufs(2); cnt = [0]
        b_ps = pbufs(6); pi = [0]; si = [0]; vi = [0]
        SC = 64.0 ** -0.5

        def nps():
            j = pi[0] % 6; pi[0] += 1
            return PS[2 + j], b_ps[j]

        def nstg():
            j = si[0] % NST; si[0] += 1
            return stg[j], b_stg[j]

        for blk in range(NB):
            cols = slice(blk * 512, (blk + 1) * 512)
            xt, bx = xts[blk % 2], b_xt[blk % 2]; xT, bxT = xTs[blk % 2], b_xT[blk % 2]
            if blk == 0:
                load_x(0, Xsrc, xt, bx)
            make_xT(xt, bx, xT, bxT, pst, b_pst, cnt)
            if blk + 1 < NB:
                load_x(blk + 1, Xsrc, xt, bx)
            dma("sp", QTf[:, 67:70, blk * 512:(blk + 1) * 512], onesS[0:8], reads=[b_oS])
            dma("sp", KTf[:, 64:67, blk * 512:(blk + 1) * 512], onesS[0:8], reads=[b_oS])
            for (c0, dst, scale) in ((0, QTf, SC), (512, KTf, None), (1544, QTc, SC), (2056, KTc, None)):
                for i in range(4):
                    ps, bps = nps()
                    for c in range(8):
                        P.op("pe", X.matmul(ps, lhsT=Win[:, c, c0 + i * 128:c0 + (i + 1) * 128], rhs=xT[:, c, :], start=(c == 0), stop=(c == 7)), reads=b_Win_of(c0 + i * 128, c0 + (i + 1) * 128) + [bxT], writes=[bps])
                    s, bs = nstg()
                    evac("act" if i % 2 == 0 else "dve", s, ps, [bps], [bs], scale=scale)
                    dma("sp", dst[2 * i, 0:64, cols], s[0:64, :], reads=[bs])
                    dma("sp", dst[2 * i + 1, 0:64, cols], s[64:128, :], reads=[bs])
            for (c0, dst) in ((1024, Vf), (2568, Vc)):
                vt, bvt = vst[vi[0] % 4], b_vst[vi[0] % 4]; vi[0] += 1
                for t in range(4):
                    ps, bps = nps()
                    for c in range(8):
                        P.op("pe", X.matmul(ps, lhsT=xT[:, c, t * 128:(t + 1) * 128], rhs=Win[:, c, c0:c0 + 512], start=(c == 0), stop=(c == 7)), reads=[bxT] + b_Win_of(c0, c0 + 512), writes=[bps])
                    evac("act" if t % 2 == 0 else "dve", vt[:, t, :], ps, [bps], [bvt])
                dma("sp", dst[cols].rearrange("(t p) h e -> p t (h e)", p=128), vt, reads=[bvt])
            ps, bps = nps()
            for c in range(8):
                P.op("pe", X.matmul(ps[0:8, :], lhsT=Win[:, c, 1536:1544], rhs=xT[:, c, :], start=(c == 0), stop=(c == 7)), reads=b_Win_of(1536, 1544) + [bxT], writes=[bps])
            P.op("act", X.activation(out=ls[0:8, :], in_=ps[0:8, :], func=AF.Exp, scale=-1.0, bias=nb[0:8, 0:1]), reads=[bps, b_nb], writes=[b_ls])
            P.op("act", X.activation(out=ls[0:8, :], in_=ls[0:8, :], func=AF.Ln, bias=1.0), reads=[b_ls], writes=[b_ls])
            init = 0.0 if blk == 0 else cum[0:8, blk * 512 - 1:blk * 512]
            P.op("dve", X.tensor_tensor_scan(out=cum[0:8, cols], data0=ones32[0:8, :], data1=ls[0:8, :], initial=init, op0=ALU.mult, op1=ALU.add), reads=[b_ls, b_o32, b_cum], writes=[b_cum])
            ak, aq, ba = augk[blk % 2], augq[blk % 2], b_aug[blk % 2]
            P.op("dve", X.tensor_copy(out=ak[0:8, 0, :], in_=cum[0:8, cols]), reads=[b_cum], writes=[ba])
            P.op("dve", X.tensor_tensor(out=r1[0:8, :], in0=cum[0:8, cols], in1=ak[0:8, 0, :], op=ALU.subtract), reads=[b_cum, ba], writes=[b_r])
            P.op("dve", X.tensor_copy(out=ak[0:8, 1, :], in_=r1[0:8, :]), reads=[b_r], writes=[ba])
            P.op("dve", X.tensor_tensor(out=r2[0:8, :], in0=r1[0:8, :], in1=ak[0:8, 1, :], op=ALU.subtract), reads=[b_r, ba], writes=[b_r])
            P.op("dve", X.tensor_copy(out=ak[0:8, 2, :], in_=r2[0:8, :]), reads=[b_r], writes=[ba])
            P.op("dve", X.tensor_scalar(out=aq[0:8], in0=ak[0:8], scalar1=-1.0, scalar2=None, op0=ALU.mult), reads=[ba], writes=[ba])
            dma("sp", KTf[:, 67:70, cols], ak[0:8], reads=[ba])
            dma("sp", QTf[:, 64:67, cols], aq[0:8], reads=[ba])
        A.release()
        P.barrier()

    def prep_bias(shape, load_fn):
        bf = A.alloc(shape, BF16); b = Buf()
        f = A.alloc(shape, F32)
        load_fn(f, b)
        P.op("dve", X.tensor_copy(out=bf, in_=f), reads=[b], writes=[b])
        return bf, b

    inproj_ab(x_in)
    A.mark()
    Wo = A.alloc([8, D], BF16); b_Wo = Buf()
    abf, b_al = prep_bias([8, 2, 128], lambda f, b: dma("poolq", f, alibi_d, writes=[b]))
    sk = A.alloc([8], F32); b_sk = Buf()
    dma("poolq", sk, ab_sinks.rearrange("(o h) -> o h", o=1).broadcast_to([128, 8]), writes=[b_sk])
    P.op("act", X.activation(out=sk, in_=sk, func=AF.Exp), reads=[b_sk], writes=[b_sk])
    load_w_cast(Wo, ab_w_out, D, b_Wo)
    attention_full(QTa, KTa, Va, 96, mla_mask_d, 0)
    attention_band(QTs, KTs, Vs, 2, 1, abf, b_al, 512, sink=sk, b_sink=b_sk)
    outproj_ln(Wo, b_Wo, x_in, X1, 0)
    A.release()
    ffn(0, X1, X2)
    inproj_cd(X2)
    A.mark()
    Wo = A.alloc([8, D], BF16); b_Wo = Buf()

    def load_ck(f, b):
        cm = A.alloc([5, 128], F32)
        dma("poolq", f, ck_bias_d, writes=[b])
        dma("poolq", cm, ck_mask_d, writes=[b])
        for h in range(8):
            P.op("pool", X.tensor_tensor(out=f[:, h], in0=f[:, h], in1=cm, op=ALU.add), reads=[b], writes=[b])
    cbf, b_cb = prep_bias([8, 5, 128], load_ck)
    load_w_cast(Wo, cd_w_out, D, b_Wo)
    attention_full(QTf, KTf, Vf, 70, causal_mask_d, 0)
    attention_band(QTc, KTc, Vc, 8, 4, cbf, b_cb, 512)
    outproj_ln(Wo, b_Wo, X2, X3, 1)
    A.release()
    ffn(1, X3, y_out)
    P.emit()
    es.close()
    return nc


_CACHE = {}


def kernel(**inputs):
    import ml_dtypes
    hc = host_constants()
    idx = hc.pop("_ck_idx")
    rel = np.asarray(inputs["cd_rel_bias"], np.float32)[0]
    hc["ck_bias"] = np.ascontiguousarray(np.transpose(rel[idx], (0, 3, 1, 2)))
    if "nc" not in _CACHE:
        _CACHE["nc"] = build_nc()
    nc = _CACHE["nc"]
    x = np.asarray(inputs["x"], np.float32)
    shared = {
        "ab_w_in": inputs["ab_w_in"][0], "ab_q_norm": inputs["ab_q_norm"][0], "ab_w_uq": inputs["ab_w_uq"][0],
        "ab_kv_norm": inputs["ab_kv_norm"][0], "ab_w_ukv": inputs["ab_w_ukv"][0], "ab_sinks": inputs["ab_sinks"][0],
        "ab_w_out": inputs["ab_w_out"][0], "cd_w_in": inputs["cd_w_in"][0], "cd_b_forget": inputs["cd_b_forget"][0],
        "cd_w_out": inputs["cd_w_out"][0],
        "ln1_g": inputs["ln1_g"], "ln1_b": inputs["ln1_b"], "ln2_g": inputs["ln2_g"], "ln2_b": inputs["ln2_b"],
        "ffn_w_gate": inputs["ffn_w_gate"], "ffn_w_up": inputs["ffn_w_up"], "ffn_w_down": inputs["ffn_w_down"],
    }
    shared = {k: np.ascontiguousarray(np.asarray(v, np.float32)) for k, v in shared.items()}
    shared.update(hc)
    in_maps = [dict(shared, x=np.ascontiguousarray(x[b])) for b in range(8)]
    res = run_bass_kernel_spmd(nc, in_maps, core_ids=list(range(8)))
    return np.stack([np.asarray(r["y"], np.float32) for r in res.results], axis=0)
```

```python
import numpy as np
import concourse.bass as bass
import concourse.mybir as mybir
from concourse.bass_utils import run_bass_kernel_spmd
from contextlib import ExitStack

F32 = mybir.dt.float32
BF16 = mybir.dt.bfloat16
AF = mybir.ActivationFunctionType
ALU = mybir.AluOpType
AX = mybir.AxisListType

COMPUTE = ("pe", "act", "dve", "pool")
QUEUES = ("sp", "poolq")
NDMASEM = 12


class _Rec:
    def __init__(self, name):
        self.name = name

    def __call__(self, *a, **k):
        name = self.name
        return lambda e: getattr(e, name)(*a, **k)


class _Recorder:
    def __getattr__(self, name):
        return _Rec(name)


X = _Recorder()


class Buf:
    __slots__ = ("name", "w", "r", "rd", "excl")

    def __init__(self, name="", excl=False):
        self.name = name
        self.excl = excl
        self.w = None
        self.r = {}
        self.rd = []


class Op:
    __slots__ = ("eng", "fn", "deps", "sig", "sem", "val", "ndma", "isdma", "prev")


class Prog:
    def __init__(self, nc):
        self.nc = nc
        self.streams = {"pe": [], "act": [], "dve": [], "pool": [], "sp": []}
        self.barrier_deps = []
        self.since_barrier_dma = []
        self.last = {}
        self.wgroups = {}

    def _stream_of(self, eng):
        return "pool" if eng == "poolq" else eng

    def op(self, eng, fn, reads=(), writes=(), ndma=0):
        o = Op()
        o.eng = eng
        o.fn = fn
        o.ndma = ndma
        o.isdma = ndma > 0
        o.sig = False
        o.sem = None
        o.val = 0
        st = self._stream_of(eng)
        deps = []
        xr = [b for b in reads if b.excl]
        if xr:
            reads = [b for b in reads if not b.excl]
            writes = list(writes) + xr
        for b in reads:
            if b.w is not None:
                deps.append(b.w)
            deps.extend(self.wgroups.get(id(b), ()))
        for b in writes:
            if b.w is not None:
                deps.append(b.w)
            if id(b) in self.wgroups:
                deps.extend(self.wgroups.pop(id(b)))
            deps.extend(b.r.values())
            deps.extend(b.rd)
        deps.extend(self.barrier_deps)
        if eng == "pe":
            deps = [d for d in deps if d.isdma or d.eng != "pe"]
        seen = set()
        dd = []
        for d in deps:
            if id(d) not in seen and d is not o:
                seen.add(id(d))
                dd.append(d)
                d.sig = True
        o.deps = dd
        for b in reads:
            if o.isdma:
                b.rd.append(o)
            else:
                b.r[eng] = o
        for b in writes:
            b.w = o
            b.r = {}
            b.rd = []
        self.streams[st].append(o)
        if o.isdma:
            self.since_barrier_dma.append(o)
            o.sig = True
        else:
            self.last[eng] = o
        return o

    def barrier(self):
        deps = list(self.last.values()) + list(self.since_barrier_dma)
        for d in deps:
            d.sig = True
        self.barrier_deps = deps
        self.since_barrier_dma = []

    def emit(self, final_wait_eng="sp"):
        nc = self.nc
        for d in list(self.last.values()):
            d.sig = True
        with ExitStack() as es:
            sems = {}
            for e in COMPUTE:
                sems[e] = es.enter_context(nc.semaphore("s_" + e))
            dsems = {}
            for q in QUEUES:
                dsems[q] = [es.enter_context(nc.semaphore("d_%s%d" % (q, i))) for i in range(NDMASEM)]
            for st, ops in self.streams.items():
                cnt = 0
                qcnt = {q: 0 for q in QUEUES}
                qval = {q: [0] * NDMASEM for q in QUEUES}
                for o in ops:
                    if o.isdma:
                        k = qcnt[o.eng] % NDMASEM
                        qcnt[o.eng] += 1
                        o.sem = dsems[o.eng][k]
                        o.prev = qval[o.eng][k]
                        qval[o.eng][k] += 16 * o.ndma
                        o.val = qval[o.eng][k]
                    elif o.sig:
                        cnt += 1
                        o.sem = sems[o.eng]
                        o.val = cnt
            tail = list(self.barrier_deps) + list(self.last.values()) + list(self.since_barrier_dma)
            for d in tail:
                assert d.sig or d.isdma
            self.maxval = 0
            es.enter_context(nc.allow_non_contiguous_dma(reason='small strided param loads'))
            block = es.enter_context(nc.Block())
            handles = {"pe": "tensor", "act": "scalar", "dve": "vector", "pool": "gpsimd", "sp": "sync"}

            def run_stream(st, eng):
                waited = {}

                def wait(sem, val):
                    if val <= 0:
                        return
                    key = id(sem)
                    if waited.get(key, 0) >= val:
                        return
                    waited[key] = val
                    self.maxval = max(self.maxval, val)
                    eng.wait_ge(sem, val)

                for o in self.streams[st]:
                    for d in o.deps:
                        wait(d.sem, d.val)
                    if o.isdma:
                        wait(o.sem, o.prev)
                        ins = o.fn(eng)
                        assert len(ins) == o.ndma, (len(ins), o.ndma)
                        for i in ins:
                            i.then_inc(o.sem, 16)
                    else:
                        i = o.fn(eng)
                        if o.sig:
                            i.then_inc(o.sem, 1)
                if st == final_wait_eng:
                    for d in tail:
                        wait(d.sem, d.val)

            for st in self.streams:
                getattr(block, handles[st])(lambda eng, st=st: run_stream(st, eng))


class Arena:
    def __init__(self, nc, es, nbytes, name="arena"):
        self.t = es.enter_context(nc.sbuf_tensor(name, [128, nbytes // 4], F32))
        self.nbytes = nbytes
        self.off = 0
        self.marks = []

    def alloc(self, free_shape, dtype, parts=128, pbase=0):
        esz = 2 if dtype == BF16 else 4
        n = int(np.prod(free_shape))
        nb = (n * esz + 31) // 32 * 32
        assert self.off + nb <= self.nbytes, "SBUF arena overflow: %d + %d > %d" % (self.off, nb, self.nbytes)
        w0 = self.off // 4
        ap = self.t[:, w0:w0 + nb // 4]
        if dtype != F32:
            ap = ap.bitcast(dtype)
        ap = ap[:, 0:n]
        self.off += nb
        if len(free_shape) == 2:
            ap = ap.rearrange("p (a b) -> p a b", b=free_shape[1])
        elif len(free_shape) == 3:
            ap = ap.rearrange("p (a b c) -> p a b c", b=free_shape[1], c=free_shape[2])
        return ap

    def mark(self):
        self.marks.append(self.off)

    def release(self):
        self.off = self.marks.pop()


S = 4096
D = 1024
NT = 32
NB = 8
DFF = 2816
NF = 22
ALPHA = 4.0 ** 0.25
LN_EPS = 1e-5
RMS_EPS = 1e-6
NEG = -30000.0
import os as _os
DBGSTOP = int(_os.environ.get('DBGSTOP', '0'))


def host_constants():
    c = {}
    inv = (10000.0 ** (-np.arange(0, 32, 2, dtype=np.float32) / 32)).astype(np.float32)
    ang = np.arange(S, dtype=np.float32)[:, None] * inv[None, :]
    cos, sin = np.cos(ang).astype(np.float32), np.sin(ang).astype(np.float32)
    cs = np.zeros((2, 32, S), np.float32)
    cs[0, 0:16] = cos.T
    cs[0, 16:32] = cos.T
    cs[1, 0:16] = sin.T
    cs[1, 16:32] = sin.T
    c["rope_cs"] = cs
    k = np.arange(128)[:, None]
    q = np.arange(128)[None, :]
    c["mla_mask"] = np.where((k >= 64) & (q < 64), NEG, 0.0).astype(np.float32)
    c["causal_mask"] = np.where(k > q, NEG, 0.0).astype(np.float32)

    def band_mask(nof, L):
        m = np.zeros((128, nof, 128), np.float32)
        for o in range(nof):
            dc = 2 * o + (q >= 64).astype(np.int32) - (k >= 64).astype(np.int32)
            m[:, o, :] = np.where((dc >= 0) & (dc <= L), 0.0, NEG)
        return m
    slopes = np.exp2(-8.0 * np.arange(1, 9, dtype=np.float32) / 8).astype(np.float32)
    al = np.zeros((128, 8, 2, 128), np.float32)
    bm = band_mask(2, 2)
    for h in range(8):
        for o in range(2):
            dist = 128 * o + q - k
            al[:, h, o, :] = -slopes[h] * np.abs(dist).astype(np.float32) + bm[:, o, :]
    c["alibi"] = al
    c["ck_mask"] = band_mask(5, 8)
    idx = np.zeros((128, 5, 128), np.int64)
    for o in range(5):
        idx[:, o, :] = np.clip(128 * o + q - k, -63, 256) + 63
    c["_ck_idx"] = idx
    return c


def build_nc(dbg=False, nphase=99):
    nc = bass.Bass("TRN2", target_bir_lowering=False)

    def din(name, shape, dt=F32):
        return nc.dram_tensor(name, list(shape), dt, kind="ExternalInput").ap()
    skind = "ExternalOutput" if dbg else "Internal"

    def dsc(name, shape, dt):
        return nc.dram_tensor(name, list(shape), dt, kind=skind).ap()

    x_in = din("x", [S, D])
    ab_w_in = din("ab_w_in", [D, 1440]); ab_q_norm = din("ab_q_norm", [384]); ab_w_uq = din("ab_w_uq", [384, 768])
    ab_kv_norm = din("ab_kv_norm", [256]); ab_w_ukv = din("ab_w_ukv", [256, 1024]); ab_sinks = din("ab_sinks", [8])
    ab_w_out = din("ab_w_out", [D, D]); cd_w_in = din("cd_w_in", [D, 3080]); cd_b_forget = din("cd_b_forget", [8])
    cd_w_out = din("cd_w_out", [D, D])
    ln1_g = din("ln1_g", [2, D]); ln1_b = din("ln1_b", [2, D]); ln2_g = din("ln2_g", [2, D]); ln2_b = din("ln2_b", [2, D])
    w_gate = din("ffn_w_gate", [2, D, DFF]); w_up = din("ffn_w_up", [2, D, DFF]); w_down = din("ffn_w_down", [2, DFF, D])
    rope_cs = din("rope_cs", [2, 32, S]); mla_mask_d = din("mla_mask", [128, 128]); causal_mask_d = din("causal_mask", [128, 128])
    alibi_d = din("alibi", [128, 8, 2, 128]); ck_mask_d = din("ck_mask", [128, 5, 128]); ck_bias_d = din("ck_bias", [128, 8, 5, 128])
    y_out = nc.dram_tensor("y", [S, D], F32, kind="ExternalOutput").ap()

    QTa = dsc("QTa", [8, 96, S], BF16); KTa = dsc("KTa", [8, 96, S], BF16); Va = dsc("Va", [S, 8, 64], BF16)
    QTs = dsc("QTs", [8, 64, S], BF16); KTs = dsc("KTs", [2, 64, S], BF16); Vs = dsc("Vs", [S, 2, 64], BF16)
    QTf = dsc("QTf", [8, 70, S], BF16); KTf = dsc("KTf", [8, 70, S], BF16); Vf = dsc("Vf", [S, 8, 64], BF16)
    QTc = dsc("QTc", [8, 64, S], BF16); KTc = dsc("KTc", [8, 64, S], BF16); Vc = dsc("Vc", [S, 8, 64], BF16)
    OT = dsc("OT", [D, S], BF16)
    HT = dsc("HT", [DFF, S], BF16)
    X1 = dsc("X1", [S, D], F32); X2 = dsc("X2", [S, D], F32); X3 = dsc("X3", [S, D], F32)

    P = Prog(nc)
    es = ExitStack()
    A = Arena(nc, es, 176 * 1024)
    PP = [es.enter_context(nc.psum_tensor("pp%d" % i, [128, 1024], F32))[:, :] for i in range(4)]
    PS = []
    for i in range(4):
        PS += [PP[i][:, 0:512], PP[i][:, 512:1024]]

    def dma(q, out, in_, reads=(), writes=()):
        return P.op(q, lambda e: [e.dma_start(out=out, in_=in_)], reads, writes, ndma=1)

    def evac(eng, out, in_, reads, writes, scale=None):
        if eng == "act":
            if scale is None:
                return P.op("act", X.copy(out=out, in_=in_), reads, writes)
            return P.op("act", X.mul(out=out, in_=in_, mul=float(scale)), reads, writes)
        if scale is None:
            return P.op(eng, X.tensor_copy(out=out, in_=in_), reads, writes)
        return P.op(eng, X.tensor_scalar(out=out, in0=in_, scalar1=float(scale), scalar2=None, op0=ALU.mult), reads, writes)

    ident = A.alloc([128], F32); b_ident = Buf()
    P.op("pool", X.memset(ident, 1.0), writes=[b_ident])
    P.op("pool", X.affine_select(out=ident, in_=ident, pattern=[[-1, 128]], compare_op=ALU.is_equal, fill=0.0, base=0, channel_multiplier=1), reads=[b_ident], writes=[b_ident])
    ident_bf = A.alloc([128], BF16)
    P.op("dve", X.tensor_copy(out=ident_bf, in_=ident), reads=[b_ident], writes=[b_ident])
    ones_bf = A.alloc([128], BF16); b_ones = Buf()
    P.op("pool", X.memset(ones_bf, 1.0), writes=[b_ones])
    lng = A.alloc([4, D], F32); b_lng = Buf()
    eps_ln = A.alloc([1], F32); eps_rms = A.alloc([1], F32); b_eps = Buf()
    P.op("pool", X.memset(eps_ln, LN_EPS), writes=[b_eps])
    P.op("pool", X.memset(eps_rms, RMS_EPS), writes=[b_eps])

    def psbufs(n):
        return [Buf() for _ in range(n)]

    def pbufs(n):
        return [Buf(excl=True) for _ in range(n)]

    def load_x(blk, Xsrc, xt, b_xt):
        dma("sp", xt, Xsrc[blk * 512:(blk + 1) * 512, :].rearrange("(t p) d -> p t d", p=128), writes=[b_xt])

    def make_xT(xt, b_xt, xT, b_xT, pst, b_pst, cnt):
        for c in range(8):
            j = cnt[0] % len(pst); cnt[0] += 1
            for t in range(4):
                P.op("pe", X.transpose(out=pst[j][:, t * 128:(t + 1) * 128], in_=xt[:, t, c * 128:(c + 1) * 128], identity=ident),
                     reads=[b_xt, b_ident], writes=[b_pst[j]])
            evac("act" if c % 2 == 0 else "dve", xT[:, c, :], pst[j], [b_pst[j]], [b_xT])

    def ln_stage1(z, b_z, gi, tmp):
        st, mv, sc, b_st = tmp
        for hh in range(2):
            P.op("dve", X.bn_stats(out=st[:, hh, :], in_=z[:, hh * 512:(hh + 1) * 512]), reads=[b_z], writes=[b_st])
        P.op("dve", X.bn_aggr(out=mv, in_=st), reads=[b_st], writes=[b_st])
        P.op("act", X.activation(out=sc[:, 0:1], in_=mv[:, 1:2], func=AF.Ln, bias=eps_ln[:, 0:1]), reads=[b_st, b_eps], writes=[b_st])
        P.op("act", X.activation(out=sc[:, 0:1], in_=sc[:, 0:1], func=AF.Exp, scale=-0.5), reads=[b_st], writes=[b_st])

    def ln_stage2(z, b_z, gi, tmp):
        st, mv, sc, b_st = tmp
        P.op("dve", X.scalar_tensor_tensor(out=sc[:, 1:2], in0=mv[:, 0:1], scalar=-1.0, in1=sc[:, 0:1], op0=ALU.mult, op1=ALU.mult), reads=[b_st], writes=[b_st])
        P.op("act", X.activation(out=z, in_=z, func=AF.Identity, scale=sc[:, 0:1], bias=sc[:, 1:2]), reads=[b_z, b_st], writes=[b_z])

    def ln_stage3(z, b_z, gi, Xdst, rows):
        P.op("dve", X.tensor_tensor(out=z[:, 0:512], in0=z[:, 0:512], in1=lng[:, gi, 0:512], op=ALU.mult), reads=[b_z, b_lng], writes=[b_z])
        P.op("pool", X.tensor_tensor(out=z[:, 512:1024], in0=z[:, 512:1024], in1=lng[:, gi, 512:1024], op=ALU.mult), reads=[b_z, b_lng], writes=[b_z])
        P.op("pool", X.tensor_tensor(out=z, in0=z, in1=lng[:, gi + 1, :], op=ALU.add), reads=[b_z, b_lng], writes=[b_z])
        dma("sp", Xdst[rows, :], z, reads=[b_z])

    def load_ln_params(g_d, b_d, layer, gi):
        dma("sp", lng[:, gi, :], g_d[layer:layer + 1, :].partition_broadcast(128) if False else g_d[layer:layer + 1, :].broadcast_to([128, D]), writes=[b_lng])
        dma("sp", lng[:, gi + 1, :], b_d[layer:layer + 1, :].broadcast_to([128, D]), writes=[b_lng])

    def load_w_cast(dst, src_rows_by_cols, ncols, b):
        step = 1024
        srcv = src_rows_by_cols.rearrange("(c p) f -> p c f", p=128)
        pairs = []
        for c0 in range(0, ncols, step):
            c1 = min(ncols, c0 + step)
            pairs.append((dst[:, :, c0:c1], srcv[:, :, c0:c1]))
        P.op("poolq", (lambda grp: (lambda e: [e.dma_start(out=o, in_=i) for (o, i) in grp]))(pairs), writes=[Buf()], ndma=len(pairs))
        P.wgroups.setdefault(id(b), []).append(P.streams["pool"][-1])
        b.w = None

    def load_w_colblocks(dsts_srcs, ncols, cb=512):
        out = [[] for _ in dsts_srcs]
        for c0 in range(0, ncols, cb):
            c1 = min(ncols, c0 + cb)
            for wi, (dst, src) in enumerate(dsts_srcs):
                grp = [(dst[:, :, c0:c1], src.rearrange("(c p) f -> p c f", p=128)[:, :, c0:c1])]
                b = Buf()
                P.op("poolq", (lambda grp: (lambda e: [e.dma_start(out=o, in_=i) for (o, i) in grp]))(grp), writes=[b], ndma=len(grp))
                out[wi].append((c1, b))
        return out

    def colbuf(blist, c_hi):
        for (c1, b) in blist:
            if c_hi <= c1:
                return b
        raise AssertionError

    def attention_full(QT_d, KT_d, V_d, R, mask_d, orow0):
        A.mark()
        mask32 = A.alloc([128], F32); mask = A.alloc([128], BF16); b_mask = Buf()
        dma("sp", mask32, mask_d, writes=[b_mask])
        P.op("dve", X.tensor_copy(out=mask, in_=mask32), reads=[b_mask], writes=[b_mask])
        sets = []
        for i in range(2):
            qt = A.alloc([S], BF16); kt_ = A.alloc([S], BF16); va = A.alloc([NT, 128], BF16)
            bq, bk, bv = Buf(), Buf(), Buf()
            P.op("pool", X.memset(va[:, :, 64:128], 1.0), writes=[bv])
            sets.append((qt, kt_, va, bq, bk, bv))
        NPB = 3
        PTs = [A.alloc([1024], BF16) for _ in range(NPB)]; b_PT = psbufs(NPB)
        Rc = [A.alloc([512], F32) for _ in range(2)]; b_Rc = psbufs(2)
        On = [A.alloc([512], BF16) for _ in range(2)]; b_On = psbufs(2)
        b_S = pbufs(NPB); b_O = pbufs(2)
        Sp = PP[0:3]; Ob = [PP[3][:, 0:512], PP[3][:, 512:1024]]

        def load_head(h):
            qt, kt_, va, bq, bk, bv = sets[h % 2]
            dma("sp", qt[0:R, :], QT_d[h], writes=[bq])
            dma("sp", kt_[0:R, :], KT_d[h], writes=[bk])
            dma("sp", va[:, :, 0:64], V_d[:, h, :].rearrange("(k p) e -> p k e", p=128), writes=[bv])
        load_head(0)
        gi = [0]
        for h in range(8):
            if h + 1 < 8:
                load_head(h + 1)
            qt, kt_, va, bq, bk, bv = sets[h % 2]
            pairs = [(Q, kp) for Q in range(NB) for kp in range(2 * Q + 2)]
            base = gi[0]

            def col0(Q, kt):
                return 0 if kt < 4 * Q else 128 * (kt - 4 * Q)

            def qk(i):
                Q, kp = pairs[i]; j = (base + i) % NPB
                for u in range(2):
                    kt = 2 * kp + u; n0 = col0(Q, kt); diag = kt >= 4 * Q
                    Sv = Sp[j][:, u * 512:(u + 1) * 512]
                    if not diag:
                        P.op("pe", X.matmul(Sv[:, 0:512], lhsT=kt_[0:R, kt * 128:(kt + 1) * 128], rhs=qt[0:R, Q * 512:(Q + 1) * 512], start=True, stop=True),
                             reads=[bq, bk], writes=[b_S[j]])
                    else:
                        if n0 + 128 < 512:
                            P.op("pe", X.matmul(Sv[:, n0 + 128:512], lhsT=kt_[0:R, kt * 128:(kt + 1) * 128], rhs=qt[0:R, Q * 512 + n0 + 128:(Q + 1) * 512], start=True, stop=True),
                                 reads=[bq, bk], writes=[b_S[j]])
                        P.op("pe", X.matmul(Sv[:, n0:n0 + 128], lhsT=kt_[0:R, kt * 128:(kt + 1) * 128], rhs=qt[0:R, Q * 512 + n0:Q * 512 + n0 + 128], start=True, stop=False),
                             reads=[bq, bk], writes=[b_S[j]])
                        P.op("pe", X.matmul(Sv[:, n0:n0 + 128], lhsT=ident_bf, rhs=mask, start=False, stop=True),
                             reads=[b_ident, b_mask], writes=[b_S[j]])

            def ex(i):
                Q, kp = pairs[i]; j = (base + i) % NPB
                if 2 * kp + 1 < 4 * Q:
                    P.op("act", X.activation(out=PTs[j], in_=Sp[j], func=AF.Exp), reads=[b_S[j]], writes=[b_PT[j]])
                else:
                    for u in range(2):
                        n0 = col0(Q, 2 * kp + u)
                        P.op("act", X.activation(out=PTs[j][:, u * 512 + n0:(u + 1) * 512], in_=Sp[j][:, u * 512 + n0:(u + 1) * 512], func=AF.Exp), reads=[b_S[j]], writes=[b_PT[j]])

            def pv(i):
                Q, kp = pairs[i]; j = (base + i) % NPB; ob = Q % 2
                for u in range(2):
                    kt = 2 * kp + u; n0 = col0(Q, kt)
                    P.op("pe", X.matmul(Ob[ob][:, n0:512], lhsT=va[:, kt, :], rhs=PTs[j][:, u * 512 + n0:(u + 1) * 512], start=(kt == 0), stop=(kt == 4 * Q + 3)),
                         reads=[bv, b_PT[j]], writes=[b_O[ob]])

            def norm(Q):
                ob = Q % 2
                P.op("dve", X.reciprocal(out=Rc[ob][0:64, :], in_=Ob[ob][64:128, :]), reads=[b_O[ob]], writes=[b_Rc[ob]])
                P.op("dve", X.tensor_tensor(out=On[ob][0:64, :], in0=Ob[ob][0:64, :], in1=Rc[ob][0:64, :], op=ALU.mult), reads=[b_O[ob], b_Rc[ob]], writes=[b_On[ob]])
                dma("sp", OT[orow0 + h * 64:orow0 + (h + 1) * 64, Q * 512:(Q + 1) * 512], On[ob][0:64, :], reads=[b_On[ob]])
            n = len(pairs)
            qk(0)
            if n > 1:
                qk(1)
            for i in range(n):
                if i + 2 < n:
                    qk(i + 2)
                ex(i)
                pv(i)
                Q, kp = pairs[i]
                if kp == 2 * Q + 1:
                    norm(Q)
            gi[0] += n
        A.release()
        P.barrier()

    def attention_band(QT_d, KT_d, V_d, nkv, omax, bhi, b_bias, orow0, sink=None, b_sink=None):
        A.mark()
        G = 8 // nkv
        sets = []
        for i in range(2):
            qt = A.alloc([S], BF16); bq = Buf()
            sets.append((qt, bq))
        ksets = []
        for i in range(2):
            kt_ = A.alloc([S], BF16); va = A.alloc([NT, 128], BF16); bk, bv = Buf(), Buf()
            P.op("pool", X.memset(va[:, :, 64:128], 1.0), writes=[bv])
            ksets.append((kt_, va, bk, bv))
        NSB = 6; LA = 4
        PTs = [A.alloc([512], BF16) for _ in range(NSB)]; b_PT = psbufs(NSB)
        Rc = [A.alloc([512], F32) for _ in range(2)]; b_Rc = psbufs(2)
        On = [A.alloc([512], BF16) for _ in range(2)]; b_On = psbufs(2)
        b_S = pbufs(NSB); b_O = pbufs(2)
        Sb = PS[0:NSB]; Ob = PS[6:8]

        def load_q(h):
            qt, bq = sets[h % 2]
            dma("sp", qt[0:64, :], QT_d[h], writes=[bq])

        def load_kv(g):
            kt_, va, bk, bv = ksets[g % 2]
            dma("sp", kt_[0:64, :], KT_d[g], writes=[bk])
            dma("sp", va[:, :, 0:64], V_d[:, g, :].rearrange("(k p) e -> p k e", p=128), writes=[bv])
        load_q(0); load_kv(0)
        gi = [0]
        for h in range(8):
            g = h // G
            if h + 1 < 8:
                load_q(h + 1)
                if (h + 1) // G != g:
                    load_kv((h + 1) // G)
            qt, bq = sets[h % 2]
            kt_, va, bk, bv = ksets[g % 2]
            steps = []
            for Q in range(NB):
                kts = list(range(max(0, 4 * Q - omax), 4 * Q + 4))
                for kt in kts:
                    m0 = max(kt, 4 * Q); m1 = min(kt + omax, 4 * Q + 3)
                    steps.append((Q, kt, m0, m1, kt == kts[0], kt == kts[-1]))
            base = gi[0]

            def qk(i):
                Q, kt, m0, m1, first, last = steps[i]; j = (base + i) % NSB
                c0 = (m0 - 4 * Q) * 128; c1 = (m1 - 4 * Q + 1) * 128
                o0 = m0 - kt; o1 = m1 - kt
                P.op("pe", X.matmul(Sb[j][:, c0:c1], lhsT=kt_[0:64, kt * 128:(kt + 1) * 128], rhs=qt[0:64, Q * 512 + c0:Q * 512 + c1], start=True, stop=False),
                     reads=[bq, bk], writes=[b_S[j]])
                P.op("pe", X.matmul(Sb[j][:, c0:c1].rearrange("p (o q) -> p o q", q=128), lhsT=ident_bf, rhs=bhi[:, h, o0:o1 + 1, :], start=False, stop=True),
                     reads=[b_ident, b_bias], writes=[b_S[j]])

            def ex(i):
                Q, kt, m0, m1, first, last = steps[i]; j = (base + i) % NSB
                c0 = (m0 - 4 * Q) * 128; c1 = (m1 - 4 * Q + 1) * 128
                P.op("act", X.activation(out=PTs[j][:, c0:c1], in_=Sb[j][:, c0:c1], func=AF.Exp), reads=[b_S[j]], writes=[b_PT[j]])

            def pv(i):
                Q, kt, m0, m1, first, last = steps[i]; j = (base + i) % NSB; ob = Q % 2
                c0 = (m0 - 4 * Q) * 128; c1 = (m1 - 4 * Q + 1) * 128
                P.op("pe", X.matmul(Ob[ob][:, c0:c1], lhsT=va[:, kt, :], rhs=PTs[j][:, c0:c1], start=first, stop=last, skip_group_check=True),
                     reads=[bv, b_PT[j]], writes=[b_O[ob]])

            def norm(Q):
                ob = Q % 2
                if sink is not None:
                    P.op("dve", X.tensor_scalar(out=Rc[ob][0:64, :], in0=Ob[ob][64:128, :], scalar1=sink[64:128, h:h + 1], scalar2=None, op0=ALU.add), reads=[b_O[ob], b_sink], writes=[b_Rc[ob]])
                    P.op("act", X.activation(out=Rc[ob][0:64, :], in_=Rc[ob][0:64, :], func=AF.Ln), reads=[b_Rc[ob]], writes=[b_Rc[ob]])
                    P.op("act", X.activation(out=Rc[ob][0:64, :], in_=Rc[ob][0:64, :], func=AF.Exp, scale=-1.0), reads=[b_Rc[ob]], writes=[b_Rc[ob]])
                else:
                    P.op("dve", X.reciprocal(out=Rc[ob][0:64, :], in_=Ob[ob][64:128, :]), reads=[b_O[ob]], writes=[b_Rc[ob]])
                P.op("dve", X.tensor_tensor(out=On[ob][0:64, :], in0=Ob[ob][0:64, :], in1=Rc[ob][0:64, :], op=ALU.mult), reads=[b_O[ob], b_Rc[ob]], writes=[b_On[ob]])
                dma("sp", OT[orow0 + h * 64:orow0 + (h + 1) * 64, Q * 512:(Q + 1) * 512], On[ob][0:64, :], reads=[b_On[ob]])
            for i in range(min(LA, len(steps))):
                qk(i)
            pending = []
            for i in range(len(steps)):
                if i + LA < len(steps):
                    qk(i + LA)
                ex(i)
                pv(i)
                while pending and pending[0][0] <= i:
                    norm(pending.pop(0)[1])
                if steps[i][5]:
                    pending.append((i + 3, steps[i][0]))
            for _, Qp in pending:
                norm(Qp)
            gi[0] += len(steps)
        A.release()
        P.barrier()

    def outproj_ln(Wo, b_Wo, Xsrc, Xdst, layer):
        A.mark()
        load_ln_params(ln1_g, ln1_b, layer, 0)
        NZ = 5
        tmps = [(A.alloc([2, 6], F32), A.alloc([2], F32), A.alloc([2], F32), Buf()) for _ in range(NZ)]
        xts = [A.alloc([4, D], F32) for _ in range(3)]; b_xt = psbufs(3)
        ots = [A.alloc([8, 512], BF16) for _ in range(3)]; b_ot = psbufs(3)
        zs = [A.alloc([D], F32) for _ in range(NZ)]; b_z = psbufs(NZ)
        b_ps = pbufs(4); pi = [0]

        def loads(blk):
            dma("sp", xts[blk % 3], Xsrc[blk * 512:(blk + 1) * 512, :].rearrange("(t p) d -> p t d", p=128), writes=[b_xt[blk % 3]])
            dma("sp", ots[blk % 3], OT[:, blk * 512:(blk + 1) * 512].rearrange("(c p) s -> p c s", p=128), writes=[b_ot[blk % 3]])

        def s1(i):
            blk, t = divmod(i, 4)
            if t == 0 and blk + 2 < NB:
                loads(blk + 2)
            xt, bx = xts[blk % 3], b_xt[blk % 3]; ot, bo = ots[blk % 3], b_ot[blk % 3]
            z, bz = zs[i % NZ], b_z[i % NZ]
            for hh in range(2):
                pj = pi[0] % 4; pi[0] += 1
                for c in range(8):
                    P.op("pe", X.matmul(PS[pj], lhsT=ot[:, c, t * 128:(t + 1) * 128], rhs=Wo[:, c, hh * 512:(hh + 1) * 512], start=(c == 0), stop=(c == 7)),
                         reads=[bo, b_Wo], writes=[b_ps[pj]])
                P.op("dve", X.scalar_tensor_tensor(out=z[:, hh * 512:(hh + 1) * 512], in0=xt[:, t, hh * 512:(hh + 1) * 512], scalar=ALPHA, in1=PS[pj], op0=ALU.mult, op1=ALU.add),
                     reads=[bx, b_ps[pj]], writes=[bz])
            ln_stage1(z, bz, 0, tmps[i % NZ])

        def s2(i):
            ln_stage2(zs[i % NZ], b_z[i % NZ], 0, tmps[i % NZ])

        def s3(i):
            blk, t = divmod(i, 4)
            ln_stage3(zs[i % NZ], b_z[i % NZ], 0, Xdst, slice(blk * 512 + t * 128, blk * 512 + (t + 1) * 128))
        loads(0); loads(1)
        s1(0); s1(1); s2(0)
        for i in range(NT):
            if i + 2 < NT:
                s1(i + 2)
            if i + 1 < NT:
                s2(i + 1)
            s3(i)
        A.release()
        P.barrier()

    def ffn(layer, Xsrc, Xdst):
        A.mark()
        NFA = 14
        Wd_a = A.alloc([NFA, D], BF16)
        A.mark()
        Wg = A.alloc([8, DFF], BF16); Wu = A.alloc([8, DFF], BF16)
        bl_Wg, bl_Wu = load_w_colblocks([(Wg, w_gate[layer]), (Wu, w_up[layer])], DFF, cb=256)
        b_Wd = []
        for f0 in range(0, NFA, 2):
            grp = [(Wd_a[:, f0:f0 + 2, :], w_down[layer][f0 * 128:(f0 + 2) * 128, :].rearrange("(c p) f -> p c f", p=128))]
            b = Buf()
            P.op("poolq", (lambda grp: (lambda e: [e.dma_start(out=o, in_=i) for (o, i) in grp]))(grp), writes=[b], ndma=1)
            b_Wd += [b, b]
        xts = [A.alloc([4, D], F32)] * 2; b_xt = [Buf()] * 2
        xTs = [A.alloc([8, 512], BF16) for _ in range(2)]; b_xT = psbufs(2)
        sgs = [A.alloc([512], F32) for _ in range(2)]; b_sg = psbufs(2)
        hts = [A.alloc([512], BF16) for _ in range(4)]; b_ht = psbufs(4)
        pst = PS[0:2]; b_pst = pbufs(2); cnt = [0]
        b_g = pbufs(2); b_u = pbufs(2); fi = 0
        load_x(0, Xsrc, xts[0], b_xt[0])
        for blk in range(NB):
            xt, bx = xts[blk % 2], b_xt[blk % 2]; xT, bxT = xTs[blk % 2], b_xT[blk % 2]
            make_xT(xt, bx, xT, bxT, pst, b_pst, cnt)
            if blk + 1 < NB:
                load_x(blk + 1, Xsrc, xts[(blk + 1) % 2], b_xt[(blk + 1) % 2])
            for f in range(NF):
                j = fi % 2; hj = fi % 4; fi += 1
                pg, pu = PS[2 + j], PS[4 + j]
                for c in range(8):
                    P.op("pe", X.matmul(pg, lhsT=Wg[:, c, f * 128:(f + 1) * 128], rhs=xT[:, c, :], start=(c == 0), stop=(c == 7)), reads=[colbuf(bl_Wg, (f + 1) * 128), bxT], writes=[b_g[j]])
                for c in range(8):
                    P.op("pe", X.matmul(pu, lhsT=Wu[:, c, f * 128:(f + 1) * 128], rhs=xT[:, c, :], start=(c == 0), stop=(c == 7)), reads=[colbuf(bl_Wu, (f + 1) * 128), bxT], writes=[b_u[j]])
                P.op("act", X.activation(out=sgs[j], in_=pg, func=AF.Silu), reads=[b_g[j]], writes=[b_sg[j]])
                P.op("dve", X.tensor_tensor(out=hts[hj], in0=pu, in1=sgs[j], op=ALU.mult), reads=[b_u[j], b_sg[j]], writes=[b_ht[hj]])
                dma("sp", HT[f * 128:(f + 1) * 128, blk * 512:(blk + 1) * 512], hts[hj], reads=[b_ht[hj]])
        A.release()
        P.barrier()
        A.mark()
        Wd_b = A.alloc([NF - NFA, D], BF16)
        for f0 in range(NFA, NF, 2):
            grp = [(Wd_b[:, f0 - NFA:f0 - NFA + 2, :], w_down[layer][f0 * 128:(f0 + 2) * 128, :].rearrange("(c p) f -> p c f", p=128))]
            b = Buf()
            P.op("poolq", (lambda grp: (lambda e: [e.dma_start(out=o, in_=i) for (o, i) in grp]))(grp), writes=[b], ndma=1)
            b_Wd += [b, b]

        def Wd_of(f):
            return Wd_a[:, f] if f < NFA else Wd_b[:, f - NFA]
        load_ln_params(ln2_g, ln2_b, layer, 2)
        NZ = 5
        tmps = [(A.alloc([2, 6], F32), A.alloc([2], F32), A.alloc([2], F32), Buf()) for _ in range(NZ)]
        xts = [A.alloc([4, D], F32) for _ in range(2)]; b_xt = psbufs(2)
        hbs = [A.alloc([NF, 512], BF16) for _ in range(2)]; b_hb = psbufs(2)
        zs = [A.alloc([D], F32) for _ in range(NZ)]; b_z = psbufs(NZ)
        b_ps = pbufs(4); pi = [0]

        def loads(blk):
            dma("sp", xts[blk % 2], Xsrc[blk * 512:(blk + 1) * 512, :].rearrange("(t p) d -> p t d", p=128), writes=[b_xt[blk % 2]])
            dma("sp", hbs[blk % 2], HT[:, blk * 512:(blk + 1) * 512].rearrange("(f p) s -> p f s", p=128), writes=[b_hb[blk % 2]])

        def s1(i):
            blk, t = divmod(i, 4)
            if t == 0 and blk + 1 < NB:
                loads(blk + 1)
            xt, bx = xts[blk % 2], b_xt[blk % 2]; hb, bh = hbs[blk % 2], b_hb[blk % 2]
            z, bz = zs[i % NZ], b_z[i % NZ]
            for hh in range(2):
                pj = pi[0] % 4; pi[0] += 1
                for f in range(NF):
                    P.op("pe", X.matmul(PS[pj], lhsT=hb[:, f, t * 128:(t + 1) * 128], rhs=Wd_of(f)[:, hh * 512:(hh + 1) * 512], start=(f == 0), stop=(f == NF - 1)),
                         reads=[bh, b_Wd[f]], writes=[b_ps[pj]])
                P.op("dve", X.scalar_tensor_tensor(out=z[:, hh * 512:(hh + 1) * 512], in0=xt[:, t, hh * 512:(hh + 1) * 512], scalar=ALPHA, in1=PS[pj], op0=ALU.mult, op1=ALU.add),
                     reads=[bx, b_ps[pj]], writes=[bz])
            ln_stage1(z, bz, 2, tmps[i % NZ])

        def s2(i):
            ln_stage2(zs[i % NZ], b_z[i % NZ], 2, tmps[i % NZ])

        def s3(i):
            blk, t = divmod(i, 4)
            ln_stage3(zs[i % NZ], b_z[i % NZ], 2, Xdst, slice(blk * 512 + t * 128, blk * 512 + (t + 1) * 128))
        loads(0)
        s1(0); s1(1); s2(0)
        for i in range(NT):
            if i + 2 < NT:
                s1(i + 2)
            if i + 1 < NT:
                s2(i + 1)
            s3(i)
        A.release()
        P.barrier()
        A.release()

    def inproj_ab(Xsrc):
        A.mark()
        Win = A.alloc([8, 1440], BF16); b_Win = Buf()
        load_w_cast(Win, ab_w_in, 1440, b_Win)
        Wuq = A.alloc([3, 768], BF16); b_Wuq = Buf()
        load_w_cast(Wuq, ab_w_uq, 768, b_Wuq)
        Wukv = A.alloc([2, 1024], BF16); b_Wukv = Buf()
        load_w_cast(Wukv, ab_w_ukv, 1024, b_Wukv)
        Wkr_rot = A.alloc([8, 32], BF16); Wuq_rot = A.alloc([3, 8, 32], BF16); b_rot = Buf()
        P.op("act", X.mul(out=Wkr_rot[:, :, 0:16], in_=Win[:, :, 656:672], mul=-1.0), reads=[b_Win], writes=[b_rot])
        P.op("act", X.copy(out=Wkr_rot[:, :, 16:32], in_=Win[:, :, 640:656]), reads=[b_Win], writes=[b_rot])
        Wuq4 = Wuq.rearrange("p c (h e) -> p c h e", e=96)
        for c in range(3):
            P.op("act", X.mul(out=Wuq_rot[:, c, :, 0:16], in_=Wuq4[:, c, :, 80:96], mul=-1.0), reads=[b_Wuq], writes=[b_rot])
            P.op("act", X.copy(out=Wuq_rot[:, c, :, 16:32], in_=Wuq4[:, c, :, 64:80]), reads=[b_Wuq], writes=[b_rot])
        Wq_nope = A.alloc([3, 512], BF16); Wq_rope = A.alloc([3, 256], BF16); Wk_nope = A.alloc([2, 512], BF16); b_wc = Buf()
        Wukv4 = Wukv.rearrange("p c (h e) -> p c h e", e=128)
        for c in range(3):
            P.op("dve", X.tensor_copy(out=Wq_nope[:, c, :].rearrange("p (h e) -> p h e", e=64), in_=Wuq4[:, c, :, 0:64]), reads=[b_Wuq], writes=[b_wc])
            P.op("pool", X.tensor_copy(out=Wq_rope[:, c, :].rearrange("p (h e) -> p h e", e=32), in_=Wuq4[:, c, :, 64:96]), reads=[b_Wuq], writes=[b_wc])
        for c in range(2):
            P.op("dve", X.tensor_copy(out=Wk_nope[:, c, :].rearrange("p (h e) -> p h e", e=64), in_=Wukv4[:, c, :, 0:64]), reads=[b_Wukv], writes=[b_wc])
        Wq_rot = Wuq_rot.rearrange("p c h e -> p c (h e)")
        gn = A.alloc([5], F32); b_gn = Buf()
        dma("sp", gn[:, 0:3], ab_q_norm.rearrange("(c p) -> p c", p=128), writes=[b_gn])
        dma("sp", gn[:, 3:5], ab_kv_norm.rearrange("(c p) -> p c", p=128), writes=[b_gn])
        xts = [A.alloc([4, D], F32) for _ in range(2)]; b_xt = psbufs(2)
        xTs = [A.alloc([8, 512], BF16) for _ in range(2)]; b_xT = psbufs(2)
        cs = [A.alloc([2, 512], F32) for _ in range(2)]; b_cs = psbufs(2)
        c32 = A.alloc([5, 512], F32); b_c32 = Buf()
        sq = A.alloc([5, 512], BF16); b_sq = Buf()
        rbc = A.alloc([2, 512], F32); b_rbc = Buf()
        cn = A.alloc([5, 512], BF16); b_cn = Buf()
        NST = 6
        stg = [A.alloc([512], BF16) for _ in range(NST)]; b_stg = psbufs(NST)
        t1s = [A.alloc([512], F32) for _ in range(2)]; t2s = [A.alloc([512], F32) for _ in range(2)]; b_t = psbufs(2)
        vst = [A.alloc([4, 512], BF16) for _ in range(2)]; b_vst = psbufs(2)
        vss = [A.alloc([4, 128], BF16) for _ in range(2)]; b_vss = psbufs(2)
        pst = PS[0:2]; b_pst = pbufs(2); cnt = [0]
        b_ps = pbufs(6); pi = [0]; si = [0]; ti = [0]
        SC_MLA = 96.0 ** -0.5
        SC_SWA = 64.0 ** -0.5

        def nps():
            j = pi[0] % 6; pi[0] += 1
            return PS[2 + j], b_ps[j]

        def nstg():
            j = si[0] % NST; si[0] += 1
            return stg[j], b_stg[j]

        def fm_proj(ps, bps, M, wsl, W_b, rhs_of_c, nchunks, rhs_b):
            for c in range(nchunks):
                P.op("pe", X.matmul(ps[0:M, :], lhsT=wsl(c), rhs=rhs_of_c(c), start=(c == 0), stop=(c == nchunks - 1)), reads=[W_b, rhs_b], writes=[bps])

        def rope_combine(psA, bA, psB, bB, cst, bcs, scale, dst_dram, np_=32):
            j = ti[0] % 2; ti[0] += 1
            t1, t2, bt = t1s[j], t2s[j], b_t[j]
            P.op("dve", X.scalar_tensor_tensor(out=t1[0:np_, :], in0=psA[0:np_, :], scalar=float(scale), in1=cst[0:np_, 0, :], op0=ALU.mult, op1=ALU.mult), reads=[bA, bcs], writes=[bt])
            P.op("dve", X.scalar_tensor_tensor(out=t2[0:np_, :], in0=psB[0:np_, :], scalar=float(scale), in1=cst[0:np_, 1, :], op0=ALU.mult, op1=ALU.mult), reads=[bB, bcs], writes=[bt])
            s, bs = nstg()
            P.op("pool", X.tensor_tensor(out=s[0:np_, :], in0=t1[0:np_, :], in1=t2[0:np_, :], op=ALU.add), reads=[bt], writes=[bs])
            return s, bs

        for blk in range(NB):
            cols = slice(blk * 512, (blk + 1) * 512)
            xt, bx = xts[blk % 2], b_xt[blk % 2]; xT, bxT = xTs[blk % 2], b_xT[blk % 2]
            cst, bcs = cs[blk % 2], b_cs[blk % 2]
            if blk == 0:
                load_x(0, Xsrc, xts[0], b_xt[0])
            if blk + 1 < NB:
                load_x(blk + 1, Xsrc, xts[(blk + 1) % 2], b_xt[(blk + 1) % 2])
            make_xT(xt, bx, xT, bxT, pst, b_pst, cnt)
            P.op("sp", (lambda cst, cols: (lambda e: [e.dma_start(out=cst[32 * r:32 * r + 32], in_=rope_cs[:, :, cols].rearrange("a p s -> p a s")) for r in range(4)]))(cst, cols), writes=[bcs], ndma=4)
            if dbg and blk == 0:
                dxt = dsc("DBG_xT", [128, 8, 512], BF16); dwin = dsc("DBG_Win", [128, 8, 1440], BF16)
                dma("sp", dxt, xT, reads=[bxT]); dma("sp", dwin, Win, reads=[b_Win])
            if DBGSTOP == 2: continue
            for i in range(5):
                ps, bps = nps()
                fm_proj(ps, bps, 128, lambda c, i=i: Win[:, c, i * 128:(i + 1) * 128], b_Win, lambda c: xT[:, c, :], 8, bxT)
                P.op("dve", X.tensor_copy(out=c32[:, i, :], in_=ps), reads=[bps], writes=[b_c32])
                P.op("act", X.activation(out=sq[:, i, :], in_=ps, func=AF.Square), reads=[bps], writes=[b_sq])
            for (k, i0, n, R_) in ((0, 0, 3, 384.0), (1, 3, 2, 256.0)):
                ps, bps = nps()
                for i in range(n):
                    P.op("pe", X.matmul(ps, lhsT=ones_bf, rhs=sq[:, i0 + i, :], start=(i == 0), stop=(i == n - 1)), reads=[b_ones, b_sq], writes=[bps])
                P.op("act", X.activation(out=rbc[:, k, :], in_=ps, func=AF.Ln, scale=1.0 / R_, bias=eps_rms[:, 0:1]), reads=[bps, b_eps], writes=[b_rbc])
                P.op("act", X.activation(out=rbc[:, k, :], in_=rbc[:, k, :], func=AF.Exp, scale=-0.5), reads=[b_rbc], writes=[b_rbc])
                for i in range(n):
                    P.op("dve", X.scalar_tensor_tensor(out=cn[:, i0 + i, :], in0=c32[:, i0 + i, :], scalar=gn[:, i0 + i:i0 + i + 1], in1=rbc[:, k, :], op0=ALU.mult, op1=ALU.mult),
                         reads=[b_c32, b_gn, b_rbc], writes=[b_cn])
            if DBGSTOP == 3: continue
            psA, bA = nps(); psB, bB = nps()
            fm_proj(psA, bA, 32, lambda c: Win[:, c, 640:672], b_Win, lambda c: xT[:, c, :], 8, bxT)
            fm_proj(psB, bB, 32, lambda c: Wkr_rot[:, c, :], b_rot, lambda c: xT[:, c, :], 8, bxT)
            s, bs = rope_combine(psA, bA, psB, bB, cst, bcs, 1.0, None)
            for h in range(8):
                dma("sp", KTa[h, 0:32, cols], s[0:32, :], reads=[bs])
            if DBGSTOP == 4: continue
            for i in range(4):
                ps, bps = nps()
                fm_proj(ps, bps, 128, lambda c, i=i: Win[:, c, 672 + i * 128:672 + (i + 1) * 128], b_Win, lambda c: xT[:, c, :], 8, bxT)
                s, bs = nstg()
                evac("act" if i % 2 == 0 else "dve", s, ps, [bps], [bs], scale=SC_SWA)
                dma("sp", QTs[2 * i:2 * i + 2, :, cols].rearrange("h p s -> (h p) s"), s, reads=[bs])
            ps, bps = nps()
            fm_proj(ps, bps, 128, lambda c: Win[:, c, 1184:1312], b_Win, lambda c: xT[:, c, :], 8, bxT)
            s, bs = nstg()
            evac("act", s, ps, [bps], [bs])
            dma("sp", KTs[:, :, cols].rearrange("h p s -> (h p) s"), s, reads=[bs])
            ps, bps = nps()
            for t in range(4):
                for c in range(8):
                    P.op("pe", X.matmul(ps[:, t * 128:(t + 1) * 128], lhsT=xT[:, c, t * 128:(t + 1) * 128], rhs=Win[:, c, 1312:1440], start=(c == 0), stop=(c == 7)), reads=[bxT, b_Win], writes=[bps])
            vs_, bvs = vss[blk % 2], b_vss[blk % 2]
            evac("dve", vs_, ps.rearrange("p (t e) -> p t e", e=128), [bps], [bvs])
            dma("sp", Vs[cols].rearrange("(t p) g e -> p t (g e)", p=128), vs_, reads=[bvs])
            if DBGSTOP == 5: continue
            Wuq4_ = Wuq.rearrange("p c (h e) -> p c h e", e=96)
            for g in range(4):
                ps, bps = nps()
                fm_proj(ps, bps, 128, lambda c, g=g: Wq_nope[:, c, 128 * g:128 * g + 128], b_wc, lambda c: cn[:, c, :], 3, b_cn)
                s, bs = nstg()
                evac("act", s, ps, [bps], [bs], scale=SC_MLA)
                dma("sp", QTa[2 * g, 32:96, cols], s[0:64, :], reads=[bs])
                dma("sp", QTa[2 * g + 1, 32:96, cols], s[64:128, :], reads=[bs])
            for g in range(2):
                psA, bA = nps(); psB, bB = nps()
                fm_proj(psA, bA, 128, lambda c, g=g: Wq_rope[:, c, 128 * g:128 * g + 128], b_wc, lambda c: cn[:, c, :], 3, b_cn)
                fm_proj(psB, bB, 128, lambda c, g=g: Wq_rot[:, c, 128 * g:128 * g + 128], b_rot, lambda c: cn[:, c, :], 3, b_cn)
                s, bs = rope_combine(psA, bA, psB, bB, cst, bcs, SC_MLA, None, np_=128)
                for r in range(4):
                    dma("sp", QTa[4 * g + r, 0:32, cols], s[32 * r:32 * r + 32, :], reads=[bs])
            if DBGSTOP == 6: continue
            Wukv4_ = Wukv.rearrange("p c (h e) -> p c h e", e=128)
            for g in range(4):
                ps, bps = nps()
                fm_proj(ps, bps, 128, lambda c, g=g: Wk_nope[:, c, 128 * g:128 * g + 128], b_wc, lambda c: cn[:, 3 + c, :], 2, b_cn)
                s, bs = nstg()
                evac("dve" if g % 2 == 0 else "act", s, ps, [bps], [bs])
                dma("sp", KTa[2 * g, 32:96, cols], s[0:64, :], reads=[bs])
                dma("sp", KTa[2 * g + 1, 32:96, cols], s[64:128, :], reads=[bs])
            if DBGSTOP == 7: continue
            vt, bvt = vst[blk % 2], b_vst[blk % 2]
            for t in range(4):
                ps, bps = nps()
                for c in range(2):
                    P.op("pe", X.matmul(ps.rearrange("p (h e) -> p h e", e=64), lhsT=cn[:, 3 + c, t * 128:(t + 1) * 128], rhs=Wukv[:, c, :].rearrange("p (h e) -> p h e", e=128)[:, :, 64:128], start=(c == 0), stop=(c == 1)),
                         reads=[b_cn, b_Wukv], writes=[bps])
                evac("act" if t % 2 == 0 else "dve", vt[:, t, :], ps, [bps], [bvt])
            dma("sp", Va[cols].rearrange("(t p) h e -> p t (h e)", p=128), vt, reads=[bvt])
        A.release()
        P.barrier()

    def inproj_cd(Xsrc):
        A.mark()
        Win = A.alloc([8, 3080], BF16)
        bl_Win = load_w_colblocks([(Win, cd_w_in)], 3080, cb=440)[0]

        def b_Win_of(c_lo, c_hi):
            bs = []
            for (c1, b) in bl_Win:
                if c1 > c_lo and c1 - 440 < c_hi:
                    bs.append(b)
            return bs
        nb = A.alloc([1], F32); b_nb = Buf()
        dma("sp", nb[0:8, :], cd_b_forget.rearrange("(h o) -> h o", o=1), writes=[b_nb])
        P.op("dve", X.tensor_scalar(out=nb[0:8, :], in0=nb[0:8, :], scalar1=-1.0, scalar2=None, op0=ALU.mult), reads=[b_nb], writes=[b_nb])
        ones32 = A.alloc([512], F32); b_o32 = Buf()
        P.op("pool", X.memset(ones32, 1.0), writes=[b_o32])
        onesS = A.alloc([3, 512], BF16); b_oS = Buf()
        P.op("pool", X.memset(onesS[0:8], 1.0), writes=[b_oS])
        cum = A.alloc([S], F32); b_cum = Buf()
        xts = [A.alloc([4, D], F32)] * 2; b_xt = [Buf()] * 2
        xTs = [A.alloc([8, 512], BF16) for _ in range(2)]; b_xT = psbufs(2)
        NST = 6
        stg = [A.alloc([512], BF16) for _ in range(NST)]; b_stg = psbufs(NST)
        vst = [A.alloc([4, 512], BF16) for _ in range(2)] * 2; b_vst = psbufs(2) * 2
        ls = A.alloc([512], F32); b_ls = Buf()
        r1 = A.alloc([512], F32); r2 = A.alloc([512], F32); b_r = Buf()
        augk = [A.alloc([3, 512], BF16) for _ in range(2)]; augq = [A.alloc([3, 512], BF16) for _ in range(2)]; b_aug = psbufs(2)
        pst = PS[0:2]; b_pst = pbufs(2); cnt = [0]
        b_ps = pbufs(6); pi = [0]; si = [0]; vi = [0]
        SC = 64.0 ** -0.5

        def nps():
            j = pi[0] % 6; pi[0] += 1
            return PS[2 + j], b_ps[j]

        def nstg():
            j = si[0] % NST; si[0] += 1
            return stg[j], b_stg[j]

        for blk in range(NB):
            cols = slice(blk * 512, (blk + 1) * 512)
            xt, bx = xts[blk % 2], b_xt[blk % 2]; xT, bxT = xTs[blk % 2], b_xT[blk % 2]
            if blk == 0:
                load_x(0, Xsrc, xt, bx)
            make_xT(xt, bx, xT, bxT, pst, b_pst, cnt)
            if blk + 1 < NB:
                load_x(blk + 1, Xsrc, xt, bx)
            dma("sp", QTf[:, 67:70, blk * 512:(blk + 1) * 512], onesS[0:8], reads=[b_oS])
            dma("sp", KTf[:, 64:67, blk * 512:(blk + 1) * 512], onesS[0:8], reads=[b_oS])
            for (c0, dst, scale) in ((0, QTf, SC), (512, KTf, None), (1544, QTc, SC), (2056, KTc, None)):
                for i in range(4):
                    ps, bps = nps()
                    for c in range(8):
                        P.op("pe", X.matmul(ps, lhsT=Win[:, c, c0 + i * 128:c0 + (i + 1) * 128], rhs=xT[:, c, :], start=(c == 0), stop=(c == 7)), reads=b_Win_of(c0 + i * 128, c0 + (i + 1) * 128) + [bxT], writes=[bps])
                    s, bs = nstg()
                    evac("act" if i % 2 == 0 else "dve", s, ps, [bps], [bs], scale=scale)
                    dma("sp", dst[2 * i, 0:64, cols], s[0:64, :], reads=[bs])
                    dma("sp", dst[2 * i + 1, 0:64, cols], s[64:128, :], reads=[bs])
            for (c0, dst) in ((1024, Vf), (2568, Vc)):
                vt, bvt = vst[vi[0] % 4], b_vst[vi[0] % 4]; vi[0] += 1
                for t in range(4):
                    ps, bps = nps()
                    for c in range(8):
                        P.op("pe", X.matmul(ps, lhsT=xT[:, c, t * 128:(t + 1) * 128], rhs=Win[:, c, c0:c0 + 512], start=(c == 0), stop=(c == 7)), reads=[bxT] + b_Win_of(c0, c0 + 512), writes=[bps])
                    evac("act" if t % 2 == 0 else "dve", vt[:, t, :], ps, [bps], [bvt])
                dma("sp", dst[cols].rearrange("(t p) h e -> p t (h e)", p=128), vt, reads=[bvt])
            ps, bps = nps()
            for c in range(8):
                P.op("pe", X.matmul(ps[0:8, :], lhsT=Win[:, c, 1536:1544], rhs=xT[:, c, :], start=(c == 0), stop=(c == 7)), reads=b_Win_of(1536, 1544) + [bxT], writes=[bps])
            P.op("act", X.activation(out=ls[0:8, :], in_=ps[0:8, :], func=AF.Exp, scale=-1.0, bias=nb[0:8, 0:1]), reads=[bps, b_nb], writes=[b_ls])
            P.op("act", X.activation(out=ls[0:8, :], in_=ls[0:8, :], func=AF.Ln, bias=1.0), reads=[b_ls], writes=[b_ls])
            init = 0.0 if blk == 0 else cum[0:8, blk * 512 - 1:blk * 512]
            P.op("dve", X.tensor_tensor_scan(out=cum[0:8, cols], data0=ones32[0:8, :], data1=ls[0:8, :], initial=init, op0=ALU.mult, op1=ALU.add), reads=[b_ls, b_o32, b_cum], writes=[b_cum])
            ak, aq, ba = augk[blk % 2], augq[blk % 2], b_aug[blk % 2]
            P.op("dve", X.tensor_copy(out=ak[0:8, 0, :], in_=cum[0:8, cols]), reads=[b_cum], writes=[ba])
            P.op("dve", X.tensor_tensor(out=r1[0:8, :], in0=cum[0:8, cols], in1=ak[0:8, 0, :], op=ALU.subtract), reads=[b_cum, ba], writes=[b_r])
            P.op("dve", X.tensor_copy(out=ak[0:8, 1, :], in_=r1[0:8, :]), reads=[b_r], writes=[ba])
            P.op("dve", X.tensor_tensor(out=r2[0:8, :], in0=r1[0:8, :], in1=ak[0:8, 1, :], op=ALU.subtract), reads=[b_r, ba], writes=[b_r])
            P.op("dve", X.tensor_copy(out=ak[0:8, 2, :], in_=r2[0:8, :]), reads=[b_r], writes=[ba])
            P.op("dve", X.tensor_scalar(out=aq[0:8], in0=ak[0:8], scalar1=-1.0, scalar2=None, op0=ALU.mult), reads=[ba], writes=[ba])
            dma("sp", KTf[:, 67:70, cols], ak[0:8], reads=[ba])
            dma("sp", QTf[:, 64:67, cols], aq[0:8], reads=[ba])
        A.release()
        P.barrier()

    def prep_bias(shape, load_fn):
        bf = A.alloc(shape, BF16); b = Buf()
        f = A.alloc(shape, F32)
        load_fn(f, b)
        P.op("dve", X.tensor_copy(out=bf, in_=f), reads=[b], writes=[b])
        return bf, b

    inproj_ab(x_in)
    A.mark()
    Wo = A.alloc([8, D], BF16); b_Wo = Buf()
    abf, b_al = prep_bias([8, 2, 128], lambda f, b: dma("poolq", f, alibi_d, writes=[b]))
    sk = A.alloc([8], F32); b_sk = Buf()
    dma("poolq", sk, ab_sinks.rearrange("(o h) -> o h", o=1).broadcast_to([128, 8]), writes=[b_sk])
    P.op("act", X.activation(out=sk, in_=sk, func=AF.Exp), reads=[b_sk], writes=[b_sk])
    load_w_cast(Wo, ab_w_out, D, b_Wo)
    attention_full(QTa, KTa, Va, 96, mla_mask_d, 0)
    attention_band(QTs, KTs, Vs, 2, 1, abf, b_al, 512, sink=sk, b_sink=b_sk)
    outproj_ln(Wo, b_Wo, x_in, X1, 0)
    A.release()
    ffn(0, X1, X2)
    inproj_cd(X2)
    A.mark()
    Wo = A.alloc([8, D], BF16); b_Wo = Buf()

    def load_ck(f, b):
        cm = A.alloc([5, 128], F32)
        dma("poolq", f, ck_bias_d, writes=[b])
        dma("poolq", cm, ck_mask_d, writes=[b])
        for h in range(8):
            P.op("pool", X.tensor_tensor(out=f[:, h], in0=f[:, h], in1=cm, op=ALU.add), reads=[b], writes=[b])
    cbf, b_cb = prep_bias([8, 5, 128], load_ck)
    load_w_cast(Wo, cd_w_out, D, b_Wo)
    attention_full(QTf, KTf, Vf, 70, causal_mask_d, 0)
    attention_band(QTc, KTc, Vc, 8, 4, cbf, b_cb, 512)
    outproj_ln(Wo, b_Wo, X2, X3, 1)
    A.release()
    ffn(1, X3, y_out)
    P.emit()
    es.close()
    return nc


_CACHE = {}


def kernel(**inputs):
    import ml_dtypes
    hc = host_constants()
    idx = hc.pop("_ck_idx")
    rel = np.asarray(inputs["cd_rel_bias"], np.float32)[0]
    hc["ck_bias"] = np.ascontiguousarray(np.transpose(rel[idx], (0, 3, 1, 2)))
    if "nc" not in _CACHE:
        _CACHE["nc"] = build_nc()
    nc = _CACHE["nc"]
    x = np.asarray(inputs["x"], np.float32)
    shared = {
        "ab_w_in": inputs["ab_w_in"][0], "ab_q_norm": inputs["ab_q_norm"][0], "ab_w_uq": inputs["ab_w_uq"][0],
        "ab_kv_norm": inputs["ab_kv_norm"][0], "ab_w_ukv": inputs["ab_w_ukv"][0], "ab_sinks": inputs["ab_sinks"][0],
        "ab_w_out": inputs["ab_w_out"][0], "cd_w_in": inputs["cd_w_in"][0], "cd_b_forget": inputs["cd_b_forget"][0],
        "cd_w_out": inputs["cd_w_out"][0],
        "ln1_g": inputs["ln1_g"], "ln1_b": inputs["ln1_b"], "ln2_g": inputs["ln2_g"], "ln2_b": inputs["ln2_b"],
        "ffn_w_gate": inputs["ffn_w_gate"], "ffn_w_up": inputs["ffn_w_up"], "ffn_w_down": inputs["ffn_w_down"],
    }
    shared = {k: np.ascontiguousarray(np.asarray(v, np.float32)) for k, v in shared.items()}
    shared.update(hc)
    in_maps = [dict(shared, x=np.ascontiguousarray(x[b])) for b in range(8)]
    res = run_bass_kernel_spmd(nc, in_maps, core_ids=list(range(8)))
    return np.stack([np.asarray(r["y"], np.float32) for r in res.results], axis=0)
```

```python
import numpy as np
import concourse.bass as bass
import concourse.mybir as mybir
from concourse.bass_utils import run_bass_kernel_spmd
from contextlib import ExitStack

F32 = mybir.dt.float32
BF16 = mybir.dt.bfloat16
AF = mybir.ActivationFunctionType
ALU = mybir.AluOpType
AX = mybir.AxisListType

COMPUTE = ("pe", "act", "dve", "pool")
QUEUES = ("sp", "poolq")
NDMASEM = 12


class _Rec:
    def __init__(self, name):
        self.name = name

    def __call__(self, *a, **k):
        name = self.name
        return lambda e: getattr(e, name)(*a, **k)


class _Recorder:
    def __getattr__(self, name):
        return _Rec(name)


X = _Recorder()


class Buf:
    __slots__ = ("name", "w", "r", "rd", "excl")

    def __init__(self, name="", excl=False):
        self.name = name
        self.excl = excl
        self.w = None
        self.r = {}
        self.rd = []


class Op:
    __slots__ = ("eng", "fn", "deps", "sig", "sem", "val", "ndma", "isdma", "prev")


class Prog:
    def __init__(self, nc):
        self.nc = nc
        self.streams = {"pe": [], "act": [], "dve": [], "pool": [], "sp": []}
        self.barrier_deps = []
        self.since_barrier_dma = []
        self.last = {}
        self.wgroups = {}

    def _stream_of(self, eng):
        return "pool" if eng == "poolq" else eng

    def op(self, eng, fn, reads=(), writes=(), ndma=0):
        o = Op()
        o.eng = eng
        o.fn = fn
        o.ndma = ndma
        o.isdma = ndma > 0
        o.sig = False
        o.sem = None
        o.val = 0
        st = self._stream_of(eng)
        deps = []
        xr = [b for b in reads if b.excl]
        if xr:
            reads = [b for b in reads if not b.excl]
            writes = list(writes) + xr
        for b in reads:
            if b.w is not None:
                deps.append(b.w)
            deps.extend(self.wgroups.get(id(b), ()))
        for b in writes:
            if b.w is not None:
                deps.append(b.w)
            if id(b) in self.wgroups:
                deps.extend(self.wgroups.pop(id(b)))
            deps.extend(b.r.values())
            deps.extend(b.rd)
        deps.extend(self.barrier_deps)
        if eng == "pe":
            deps = [d for d in deps if d.isdma or d.eng != "pe"]
        seen = set()
        dd = []
        for d in deps:
            if id(d) not in seen and d is not o:
                seen.add(id(d))
                dd.append(d)
                d.sig = True
        o.deps = dd
        for b in reads:
            if o.isdma:
                b.rd.append(o)
            else:
                b.r[eng] = o
        for b in writes:
            b.w = o
            b.r = {}
            b.rd = []
        self.streams[st].append(o)
        if o.isdma:
            self.since_barrier_dma.append(o)
            o.sig = True
        else:
            self.last[eng] = o
        return o

    def barrier(self):
        deps = list(self.last.values()) + list(self.since_barrier_dma)
        for d in deps:
            d.sig = True
        self.barrier_deps = deps
        self.since_barrier_dma = []

    def emit(self, final_wait_eng="sp"):
        nc = self.nc
        for d in list(self.last.values()):
            d.sig = True
        with ExitStack() as es:
            sems = {}
            for e in COMPUTE:
                sems[e] = es.enter_context(nc.semaphore("s_" + e))
            dsems = {}
            for q in QUEUES:
                dsems[q] = [es.enter_context(nc.semaphore("d_%s%d" % (q, i))) for i in range(NDMASEM)]
            for st, ops in self.streams.items():
                cnt = 0
                qcnt = {q: 0 for q in QUEUES}
                qval = {q: [0] * NDMASEM for q in QUEUES}
                for o in ops:
                    if o.isdma:
                        k = qcnt[o.eng] % NDMASEM
                        qcnt[o.eng] += 1
                        o.sem = dsems[o.eng][k]
                        o.prev = qval[o.eng][k]
                        qval[o.eng][k] += 16 * o.ndma
                        o.val = qval[o.eng][k]
                    elif o.sig:
                        cnt += 1
                        o.sem = sems[o.eng]
                        o.val = cnt
            tail = list(self.barrier_deps) + list(self.last.values()) + list(self.since_barrier_dma)
            for d in tail:
                assert d.sig or d.isdma
            self.maxval = 0
            es.enter_context(nc.allow_non_contiguous_dma(reason='small strided param loads'))
            block = es.enter_context(nc.Block())
            handles = {"pe": "tensor", "act": "scalar", "dve": "vector", "pool": "gpsimd", "sp": "sync"}

            def run_stream(st, eng):
                waited = {}

                def wait(sem, val):
                    if val <= 0:
                        return
                    key = id(sem)
                    if waited.get(key, 0) >= val:
                        return
                    waited[key] = val
                    self.maxval = max(self.maxval, val)
                    eng.wait_ge(sem, val)

                for o in self.streams[st]:
                    for d in o.deps:
                        wait(d.sem, d.val)
                    if o.isdma:
                        wait(o.sem, o.prev)
                        ins = o.fn(eng)
                        assert len(ins) == o.ndma, (len(ins), o.ndma)
                        for i in ins:
                            i.then_inc(o.sem, 16)
                    else:
                        i = o.fn(eng)
                        if o.sig:
                            i.then_inc(o.sem, 1)
                if st == final_wait_eng:
                    for d in tail:
                        wait(d.sem, d.val)

            for st in self.streams:
                getattr(block, handles[st])(lambda eng, st=st: run_stream(st, eng))


class Arena:
    def __init__(self, nc, es, nbytes, name="arena"):
        self.t = es.enter_context(nc.sbuf_tensor(name, [128, nbytes // 4], F32))
        self.nbytes = nbytes
        self.off = 0
        self.marks = []

    def alloc(self, free_shape, dtype, parts=128, pbase=0):
        esz = 2 if dtype == BF16 else 4
        n = int(np.prod(free_shape))
        nb = (n * esz + 31) // 32 * 32
        assert self.off + nb <= self.nbytes, "SBUF arena overflow: %d + %d > %d" % (self.off, nb, self.nbytes)
        w0 = self.off // 4
        ap = self.t[:, w0:w0 + nb // 4]
        if dtype != F32:
            ap = ap.bitcast(dtype)
        ap = ap[:, 0:n]
        self.off += nb
        if len(free_shape) == 2:
            ap = ap.rearrange("p (a b) -> p a b", b=free_shape[1])
        elif len(free_shape) == 3:
            ap = ap.rearrange("p (a b c) -> p a b c", b=free_shape[1], c=free_shape[2])
        return ap

    def mark(self):
        self.marks.append(self.off)

    def release(self):
        self.off = self.marks.pop()


S = 4096
D = 1024
NT = 32
NB = 8
DFF = 2816
NF = 22
ALPHA = 4.0 ** 0.25
LN_EPS = 1e-5
RMS_EPS = 1e-6
NEG = -30000.0
import os as _os
DBGSTOP = int(_os.environ.get('DBGSTOP', '0'))


def host_constants():
    c = {}
    inv = (10000.0 ** (-np.arange(0, 32, 2, dtype=np.float32) / 32)).astype(np.float32)
    ang = np.arange(S, dtype=np.float32)[:, None] * inv[None, :]
    cos, sin = np.cos(ang).astype(np.float32), np.sin(ang).astype(np.float32)
    cs = np.zeros((2, 32, S), np.float32)
    cs[0, 0:16] = cos.T
    cs[0, 16:32] = cos.T
    cs[1, 0:16] = sin.T
    cs[1, 16:32] = sin.T
    c["rope_cs"] = cs
    k = np.arange(128)[:, None]
    q = np.arange(128)[None, :]
    c["mla_mask"] = np.where((k >= 64) & (q < 64), NEG, 0.0).astype(np.float32)
    c["causal_mask"] = np.where(k > q, NEG, 0.0).astype(np.float32)

    def band_mask(nof, L):
        m = np.zeros((128, nof, 128), np.float32)
        for o in range(nof):
            dc = 2 * o + (q >= 64).astype(np.int32) - (k >= 64).astype(np.int32)
            m[:, o, :] = np.where((dc >= 0) & (dc <= L), 0.0, NEG)
        return m
    slopes = np.exp2(-8.0 * np.arange(1, 9, dtype=np.float32) / 8).astype(np.float32)
    al = np.zeros((128, 8, 2, 128), np.float32)
    bm = band_mask(2, 2)
    for h in range(8):
        for o in range(2):
            dist = 128 * o + q - k
            al[:, h, o, :] = -slopes[h] * np.abs(dist).astype(np.float32) + bm[:, o, :]
    c["alibi"] = al
    c["ck_mask"] = band_mask(5, 8)
    idx = np.zeros((128, 5, 128), np.int64)
    for o in range(5):
        idx[:, o, :] = np.clip(128 * o + q - k, -63, 256) + 63
    c["_ck_idx"] = idx
    return c


def build_nc(dbg=False, nphase=99):
    nc = bass.Bass("TRN2", target_bir_lowering=False)

    def din(name, shape, dt=F32):
        return nc.dram_tensor(name, list(shape), dt, kind="ExternalInput").ap()
    skind = "ExternalOutput" if dbg else "Internal"

    def dsc(name, shape, dt):
        return nc.dram_tensor(name, list(shape), dt, kind=skind).ap()

    x_in = din("x", [S, D])
    ab_w_in = din("ab_w_in", [D, 1440]); ab_q_norm = din("ab_q_norm", [384]); ab_w_uq = din("ab_w_uq", [384, 768])
    ab_kv_norm = din("ab_kv_norm", [256]); ab_w_ukv = din("ab_w_ukv", [256, 1024]); ab_sinks = din("ab_sinks", [8])
    ab_w_out = din("ab_w_out", [D, D]); cd_w_in = din("cd_w_in", [D, 3080]); cd_b_forget = din("cd_b_forget", [8])
    cd_w_out = din("cd_w_out", [D, D])
    ln1_g = din("ln1_g", [2, D]); ln1_b = din("ln1_b", [2, D]); ln2_g = din("ln2_g", [2, D]); ln2_b = din("ln2_b", [2, D])
    w_gate = din("ffn_w_gate", [2, D, DFF]); w_up = din("ffn_w_up", [2, D, DFF]); w_down = din("ffn_w_down", [2, DFF, D])
    rope_cs = din("rope_cs", [2, 32, S]); mla_mask_d = din("mla_mask", [128, 128]); causal_mask_d = din("causal_mask", [128, 128])
    alibi_d = din("alibi", [128, 8, 2, 128]); ck_mask_d = din("ck_mask", [128, 5, 128]); ck_bias_d = din("ck_bias", [128, 8, 5, 128])
    y_out = nc.dram_tensor("y", [S, D], F32, kind="ExternalOutput").ap()

    QTa = dsc("QTa", [8, 96, S], BF16); KTa = dsc("KTa", [8, 96, S], BF16); Va = dsc("Va", [S, 8, 64], BF16)
    QTs = dsc("QTs", [8, 64, S], BF16); KTs = dsc("KTs", [2, 64, S], BF16); Vs = dsc("Vs", [S, 2, 64], BF16)
    QTf = dsc("QTf", [8, 70, S], BF16); KTf = dsc("KTf", [8, 70, S], BF16); Vf = dsc("Vf", [S, 8, 64], BF16)
    QTc = dsc("QTc", [8, 64, S], BF16); KTc = dsc("KTc", [8, 64, S], BF16); Vc = dsc("Vc", [S, 8, 64], BF16)
    OT = dsc("OT", [D, S], BF16)
    HT = dsc("HT", [DFF, S], BF16)
    X1 = dsc("X1", [S, D], F32); X2 = dsc("X2", [S, D], F32); X3 = dsc("X3", [S, D], F32)

    P = Prog(nc)
    es = ExitStack()
    A = Arena(nc, es, 176 * 1024)
    PP = [es.enter_context(nc.psum_tensor("pp%d" % i, [128, 1024], F32))[:, :] for i in range(4)]
    PS = []
    for i in range(4):
        PS += [PP[i][:, 0:512], PP[i][:, 512:1024]]

    def dma(q, out, in_, reads=(), writes=()):
        return P.op(q, lambda e: [e.dma_start(out=out, in_=in_)], reads, writes, ndma=1)

    def evac(eng, out, in_, reads, writes, scale=None):
        if eng == "act":
            if scale is None:
                return P.op("act", X.copy(out=out, in_=in_), reads, writes)
            return P.op("act", X.mul(out=out, in_=in_, mul=float(scale)), reads, writes)
        if scale is None:
            return P.op(eng, X.tensor_copy(out=out, in_=in_), reads, writes)
        return P.op(eng, X.tensor_scalar(out=out, in0=in_, scalar1=float(scale), scalar2=None, op0=ALU.mult), reads, writes)

    ident = A.alloc([128], F32); b_ident = Buf()
    P.op("pool", X.memset(ident, 1.0), writes=[b_ident])
    P.op("pool", X.affine_select(out=ident, in_=ident, pattern=[[-1, 128]], compare_op=ALU.is_equal, fill=0.0, base=0, channel_multiplier=1), reads=[b_ident], writes=[b_ident])
    ident_bf = A.alloc([128], BF16)
    P.op("dve", X.tensor_copy(out=ident_bf, in_=ident), reads=[b_ident], writes=[b_ident])
    ones_bf = A.alloc([128], BF16); b_ones = Buf()
    P.op("pool", X.memset(ones_bf, 1.0), writes=[b_ones])
    lng = A.alloc([4, D], F32); b_lng = Buf()
    eps_ln = A.alloc([1], F32); eps_rms = A.alloc([1], F32); b_eps = Buf()
    P.op("pool", X.memset(eps_ln, LN_EPS), writes=[b_eps])
    P.op("pool", X.memset(eps_rms, RMS_EPS), writes=[b_eps])

    def psbufs(n):
        return [Buf() for _ in range(n)]

    def pbufs(n):
        return [Buf(excl=True) for _ in range(n)]

    def load_x(blk, Xsrc, xt, b_xt):
        dma("sp", xt, Xsrc[blk * 512:(blk + 1) * 512, :].rearrange("(t p) d -> p t d", p=128), writes=[b_xt])

    def make_xT(xt, b_xt, xT, b_xT, pst, b_pst, cnt):
        for c in range(8):
            j = cnt[0] % len(pst); cnt[0] += 1
            for t in range(4):
                P.op("pe", X.transpose(out=pst[j][:, t * 128:(t + 1) * 128], in_=xt[:, t, c * 128:(c + 1) * 128], identity=ident),
                     reads=[b_xt, b_ident], writes=[b_pst[j]])
            evac("act" if c % 2 == 0 else "dve", xT[:, c, :], pst[j], [b_pst[j]], [b_xT])

    def ln_stage1(z, b_z, gi, tmp):
        st, mv, sc, b_st = tmp
        for hh in range(2):
            P.op("dve", X.bn_stats(out=st[:, hh, :], in_=z[:, hh * 512:(hh + 1) * 512]), reads=[b_z], writes=[b_st])
        P.op("dve", X.bn_aggr(out=mv, in_=st), reads=[b_st], writes=[b_st])
        P.op("act", X.activation(out=sc[:, 0:1], in_=mv[:, 1:2], func=AF.Ln, bias=eps_ln[:, 0:1]), reads=[b_st, b_eps], writes=[b_st])
        P.op("act", X.activation(out=sc[:, 0:1], in_=sc[:, 0:1], func=AF.Exp, scale=-0.5), reads=[b_st], writes=[b_st])

    def ln_stage2(z, b_z, gi, tmp):
        st, mv, sc, b_st = tmp
        P.op("dve", X.scalar_tensor_tensor(out=sc[:, 1:2], in0=mv[:, 0:1], scalar=-1.0, in1=sc[:, 0:1], op0=ALU.mult, op1=ALU.mult), reads=[b_st], writes=[b_st])
        P.op("act", X.activation(out=z, in_=z, func=AF.Identity, scale=sc[:, 0:1], bias=sc[:, 1:2]), reads=[b_z, b_st], writes=[b_z])

    def ln_stage3(z, b_z, gi, Xdst, rows):
        P.op("dve", X.tensor_tensor(out=z[:, 0:512], in0=z[:, 0:512], in1=lng[:, gi, 0:512], op=ALU.mult), reads=[b_z, b_lng], writes=[b_z])
        P.op("pool", X.tensor_tensor(out=z[:, 512:1024], in0=z[:, 512:1024], in1=lng[:, gi, 512:1024], op=ALU.mult), reads=[b_z, b_lng], writes=[b_z])
        P.op("pool", X.tensor_tensor(out=z, in0=z, in1=lng[:, gi + 1, :], op=ALU.add), reads=[b_z, b_lng], writes=[b_z])
        dma("sp", Xdst[rows, :], z, reads=[b_z])

    def load_ln_params(g_d, b_d, layer, gi):
        dma("sp", lng[:, gi, :], g_d[layer:layer + 1, :].partition_broadcast(128) if False else g_d[layer:layer + 1, :].broadcast_to([128, D]), writes=[b_lng])
        dma("sp", lng[:, gi + 1, :], b_d[layer:layer + 1, :].broadcast_to([128, D]), writes=[b_lng])

    def load_w_cast(dst, src_rows_by_cols, ncols, b):
        C = dst.shape[1]
        step = 1024
        pairs = []
        for c in range(C):
            for c0 in range(0, ncols, step):
                c1 = min(ncols, c0 + step)
                pairs.append((dst[:, c, c0:c1], src_rows_by_cols[c * 128:(c + 1) * 128, c0:c1]))
        GR = 8
        for g0 in range(0, len(pairs), GR):
            grp = pairs[g0:g0 + GR]
            P.op("poolq", (lambda grp: (lambda e: [e.dma_start(out=o, in_=i) for (o, i) in grp]))(grp), writes=[Buf()], ndma=len(grp))
        P.wgroups.setdefault(id(b), []).extend(P.streams["pool"][-((len(pairs) + GR - 1) // GR):])
        b.w = None

    def load_w_colblocks(dsts_srcs, ncols, cb=512):
        out = [[] for _ in dsts_srcs]
        for c0 in range(0, ncols, cb):
            c1 = min(ncols, c0 + cb)
            for wi, (dst, src) in enumerate(dsts_srcs):
                grp = [(dst[:, :, c0:c1], src.rearrange("(c p) f -> p c f", p=128)[:, :, c0:c1])]
                b = Buf()
                P.op("poolq", (lambda grp: (lambda e: [e.dma_start(out=o, in_=i) for (o, i) in grp]))(grp), writes=[b], ndma=len(grp))
                out[wi].append((c1, b))
        return out

    def colbuf(blist, c_hi):
        for (c1, b) in blist:
            if c_hi <= c1:
                return b
        raise AssertionError

    def attention_full(QT_d, KT_d, V_d, R, mask_d, orow0):
        A.mark()
        mask32 = A.alloc([128], F32); mask = A.alloc([128], BF16); b_mask = Buf()
        dma("sp", mask32, mask_d, writes=[b_mask])
        P.op("dve", X.tensor_copy(out=mask, in_=mask32), reads=[b_mask], writes=[b_mask])
        sets = []
        for i in range(2):
            qt = A.alloc([S], BF16); kt_ = A.alloc([S], BF16); va = A.alloc([NT, 128], BF16)
            bq, bk, bv = Buf(), Buf(), Buf()
            P.op("pool", X.memset(va[:, :, 64:128], 1.0), writes=[bv])
            sets.append((qt, kt_, va, bq, bk, bv))
        NPB = 3
        PTs = [A.alloc([1024], BF16) for _ in range(NPB)]; b_PT = psbufs(NPB)
        Rc = [A.alloc([512], F32) for _ in range(2)]; b_Rc = psbufs(2)
        On = [A.alloc([512], BF16) for _ in range(2)]; b_On = psbufs(2)
        b_S = pbufs(NPB); b_O = pbufs(2)
        Sp = PP[0:3]; Ob = [PP[3][:, 0:512], PP[3][:, 512:1024]]

        def load_head(h):
            qt, kt_, va, bq, bk, bv = sets[h % 2]
            dma("sp", qt[0:R, :], QT_d[h], writes=[bq])
            dma("sp", kt_[0:R, :], KT_d[h], writes=[bk])
            dma("sp", va[:, :, 0:64], V_d[:, h, :].rearrange("(k p) e -> p k e", p=128), writes=[bv])
        load_head(0)
        gi = [0]
        for h in range(8):
            if h + 1 < 8:
                load_head(h + 1)
            qt, kt_, va, bq, bk, bv = sets[h % 2]
            pairs = [(Q, kp) for Q in range(NB) for kp in range(2 * Q + 2)]
            base = gi[0]

            def col0(Q, kt):
                return 0 if kt < 4 * Q else 128 * (kt - 4 * Q)

            def qk(i):
                Q, kp = pairs[i]; j = (base + i) % NPB
                for u in range(2):
                    kt = 2 * kp + u; n0 = col0(Q, kt); diag = kt >= 4 * Q
                    Sv = Sp[j][:, u * 512:(u + 1) * 512]
                    if not diag:
                        P.op("pe", X.matmul(Sv[:, 0:512], lhsT=kt_[0:R, kt * 128:(kt + 1) * 128], rhs=qt[0:R, Q * 512:(Q + 1) * 512], start=True, stop=True),
                             reads=[bq, bk], writes=[b_S[j]])
                    else:
                        if n0 + 128 < 512:
                            P.op("pe", X.matmul(Sv[:, n0 + 128:512], lhsT=kt_[0:R, kt * 128:(kt + 1) * 128], rhs=qt[0:R, Q * 512 + n0 + 128:(Q + 1) * 512], start=True, stop=True),
                                 reads=[bq, bk], writes=[b_S[j]])
                        P.op("pe", X.matmul(Sv[:, n0:n0 + 128], lhsT=kt_[0:R, kt * 128:(kt + 1) * 128], rhs=qt[0:R, Q * 512 + n0:Q * 512 + n0 + 128], start=True, stop=False),
                             reads=[bq, bk], writes=[b_S[j]])
                        P.op("pe", X.matmul(Sv[:, n0:n0 + 128], lhsT=ident_bf, rhs=mask, start=False, stop=True),
                             reads=[b_ident, b_mask], writes=[b_S[j]])

            def ex(i):
                Q, kp = pairs[i]; j = (base + i) % NPB
                if 2 * kp + 1 < 4 * Q:
                    P.op("act", X.activation(out=PTs[j], in_=Sp[j], func=AF.Exp), reads=[b_S[j]], writes=[b_PT[j]])
                else:
                    for u in range(2):
                        n0 = col0(Q, 2 * kp + u)
                        P.op("act", X.activation(out=PTs[j][:, u * 512 + n0:(u + 1) * 512], in_=Sp[j][:, u * 512 + n0:(u + 1) * 512], func=AF.Exp), reads=[b_S[j]], writes=[b_PT[j]])

            def pv(i):
                Q, kp = pairs[i]; j = (base + i) % NPB; ob = Q % 2
                for u in range(2):
                    kt = 2 * kp + u; n0 = col0(Q, kt)
                    P.op("pe", X.matmul(Ob[ob][:, n0:512], lhsT=va[:, kt, :], rhs=PTs[j][:, u * 512 + n0:(u + 1) * 512], start=(kt == 0), stop=(kt == 4 * Q + 3)),
                         reads=[bv, b_PT[j]], writes=[b_O[ob]])

            def norm(Q):
                ob = Q % 2
                P.op("dve", X.reciprocal(out=Rc[ob][0:64, :], in_=Ob[ob][64:128, :]), reads=[b_O[ob]], writes=[b_Rc[ob]])
                P.op("dve", X.tensor_tensor(out=On[ob][0:64, :], in0=Ob[ob][0:64, :], in1=Rc[ob][0:64, :], op=ALU.mult), reads=[b_O[ob], b_Rc[ob]], writes=[b_On[ob]])
                dma("sp", OT[orow0 + h * 64:orow0 + (h + 1) * 64, Q * 512:(Q + 1) * 512], On[ob][0:64, :], reads=[b_On[ob]])
            n = len(pairs)
            qk(0)
            if n > 1:
                qk(1)
            for i in range(n):
                if i + 2 < n:
                    qk(i + 2)
                ex(i)
                pv(i)
                Q, kp = pairs[i]
                if kp == 2 * Q + 1:
                    norm(Q)
            gi[0] += n
        A.release()
        P.barrier()

    def attention_band(QT_d, KT_d, V_d, nkv, omax, bhi, b_bias, orow0, sink=None, b_sink=None):
        A.mark()
        G = 8 // nkv
        sets = []
        for i in range(2):
            qt = A.alloc([S], BF16); bq = Buf()
            sets.append((qt, bq))
        ksets = []
        for i in range(2):
            kt_ = A.alloc([S], BF16); va = A.alloc([NT, 128], BF16); bk, bv = Buf(), Buf()
            P.op("pool", X.memset(va[:, :, 64:128], 1.0), writes=[bv])
            ksets.append((kt_, va, bk, bv))
        NSB = 6; LA = 4
        PTs = [A.alloc([512], BF16) for _ in range(NSB)]; b_PT = psbufs(NSB)
        Rc = [A.alloc([512], F32) for _ in range(2)]; b_Rc = psbufs(2)
        On = [A.alloc([512], BF16) for _ in range(2)]; b_On = psbufs(2)
        b_S = pbufs(NSB); b_O = pbufs(2)
        Sb = PS[0:NSB]; Ob = PS[6:8]

        def load_q(h):
            qt, bq = sets[h % 2]
            dma("sp", qt[0:64, :], QT_d[h], writes=[bq])

        def load_kv(g):
            kt_, va, bk, bv = ksets[g % 2]
            dma("sp", kt_[0:64, :], KT_d[g], writes=[bk])
            dma("sp", va[:, :, 0:64], V_d[:, g, :].rearrange("(k p) e -> p k e", p=128), writes=[bv])
        load_q(0); load_kv(0)
        gi = [0]
        for h in range(8):
            g = h // G
            if h + 1 < 8:
                load_q(h + 1)
                if (h + 1) // G != g:
                    load_kv((h + 1) // G)
            qt, bq = sets[h % 2]
            kt_, va, bk, bv = ksets[g % 2]
            steps = []
            for Q in range(NB):
                kts = list(range(max(0, 4 * Q - omax), 4 * Q + 4))
                for kt in kts:
                    m0 = max(kt, 4 * Q); m1 = min(kt + omax, 4 * Q + 3)
                    steps.append((Q, kt, m0, m1, kt == kts[0], kt == kts[-1]))
            base = gi[0]

            def qk(i):
                Q, kt, m0, m1, first, last = steps[i]; j = (base + i) % NSB
                c0 = (m0 - 4 * Q) * 128; c1 = (m1 - 4 * Q + 1) * 128
                o0 = m0 - kt; o1 = m1 - kt
                P.op("pe", X.matmul(Sb[j][:, c0:c1], lhsT=kt_[0:64, kt * 128:(kt + 1) * 128], rhs=qt[0:64, Q * 512 + c0:Q * 512 + c1], start=True, stop=False),
                     reads=[bq, bk], writes=[b_S[j]])
                P.op("pe", X.matmul(Sb[j][:, c0:c1].rearrange("p (o q) -> p o q", q=128), lhsT=ident_bf, rhs=bhi[:, h, o0:o1 + 1, :], start=False, stop=True),
                     reads=[b_ident, b_bias], writes=[b_S[j]])

            def ex(i):
                Q, kt, m0, m1, first, last = steps[i]; j = (base + i) % NSB
                c0 = (m0 - 4 * Q) * 128; c1 = (m1 - 4 * Q + 1) * 128
                P.op("act", X.activation(out=PTs[j][:, c0:c1], in_=Sb[j][:, c0:c1], func=AF.Exp), reads=[b_S[j]], writes=[b_PT[j]])

            def pv(i):
                Q, kt, m0, m1, first, last = steps[i]; j = (base + i) % NSB; ob = Q % 2
                c0 = (m0 - 4 * Q) * 128; c1 = (m1 - 4 * Q + 1) * 128
                P.op("pe", X.matmul(Ob[ob][:, c0:c1], lhsT=va[:, kt, :], rhs=PTs[j][:, c0:c1], start=first, stop=last, skip_group_check=True),
                     reads=[bv, b_PT[j]], writes=[b_O[ob]])

            def norm(Q):
                ob = Q % 2
                if sink is not None:
                    P.op("dve", X.tensor_scalar(out=Rc[ob][0:64, :], in0=Ob[ob][64:128, :], scalar1=sink[64:128, h:h + 1], scalar2=None, op0=ALU.add), reads=[b_O[ob], b_sink], writes=[b_Rc[ob]])
                    P.op("act", X.activation(out=Rc[ob][0:64, :], in_=Rc[ob][0:64, :], func=AF.Ln), reads=[b_Rc[ob]], writes=[b_Rc[ob]])
                    P.op("act", X.activation(out=Rc[ob][0:64, :], in_=Rc[ob][0:64, :], func=AF.Exp, scale=-1.0), reads=[b_Rc[ob]], writes=[b_Rc[ob]])
                else:
                    P.op("dve", X.reciprocal(out=Rc[ob][0:64, :], in_=Ob[ob][64:128, :]), reads=[b_O[ob]], writes=[b_Rc[ob]])
                P.op("dve", X.tensor_tensor(out=On[ob][0:64, :], in0=Ob[ob][0:64, :], in1=Rc[ob][0:64, :], op=ALU.mult), reads=[b_O[ob], b_Rc[ob]], writes=[b_On[ob]])
                dma("sp", OT[orow0 + h * 64:orow0 + (h + 1) * 64, Q * 512:(Q + 1) * 512], On[ob][0:64, :], reads=[b_On[ob]])
            for i in range(min(LA, len(steps))):
                qk(i)
            pending = []
            for i in range(len(steps)):
                if i + LA < len(steps):
                    qk(i + LA)
                ex(i)
                pv(i)
                while pending and pending[0][0] <= i:
                    norm(pending.pop(0)[1])
                if steps[i][5]:
                    pending.append((i + 3, steps[i][0]))
            for _, Qp in pending:
                norm(Qp)
            gi[0] += len(steps)
        A.release()
        P.barrier()

    def outproj_ln(Wo, b_Wo, Xsrc, Xdst, layer):
        A.mark()
        load_ln_params(ln1_g, ln1_b, layer, 0)
        NZ = 5
        tmps = [(A.alloc([2, 6], F32), A.alloc([2], F32), A.alloc([2], F32), Buf()) for _ in range(NZ)]
        xts = [A.alloc([4, D], F32) for _ in range(3)]; b_xt = psbufs(3)
        ots = [A.alloc([8, 512], BF16) for _ in range(3)]; b_ot = psbufs(3)
        zs = [A.alloc([D], F32) for _ in range(NZ)]; b_z = psbufs(NZ)
        b_ps = pbufs(4); pi = [0]

        def loads(blk):
            dma("sp", ots[blk % 3], OT[:, blk * 512:(blk + 1) * 512].rearrange("(c p) s -> p c s", p=128), writes=[b_ot[blk % 3]])
            dma("sp", xts[blk % 3], Xsrc[blk * 512:(blk + 1) * 512, :].rearrange("(t p) d -> p t d", p=128), writes=[b_xt[blk % 3]])

        def s1(i):
            blk, t = divmod(i, 4)
            if t == 0 and blk + 2 < NB:
                loads(blk + 2)
            xt, bx = xts[blk % 3], b_xt[blk % 3]; ot, bo = ots[blk % 3], b_ot[blk % 3]
            z, bz = zs[i % NZ], b_z[i % NZ]
            for hh in range(2):
                pj = pi[0] % 4; pi[0] += 1
                for c in range(8):
                    P.op("pe", X.matmul(PS[pj], lhsT=ot[:, c, t * 128:(t + 1) * 128], rhs=Wo[:, c, hh * 512:(hh + 1) * 512], start=(c == 0), stop=(c == 7)),
                         reads=[bo, b_Wo], writes=[b_ps[pj]])
                P.op("dve", X.scalar_tensor_tensor(out=z[:, hh * 512:(hh + 1) * 512], in0=xt[:, t, hh * 512:(hh + 1) * 512], scalar=ALPHA, in1=PS[pj], op0=ALU.mult, op1=ALU.add),
                     reads=[bx, b_ps[pj]], writes=[bz])
            ln_stage1(z, bz, 0, tmps[i % NZ])

        def s2(i):
            ln_stage2(zs[i % NZ], b_z[i % NZ], 0, tmps[i % NZ])

        def s3(i):
            blk, t = divmod(i, 4)
            ln_stage3(zs[i % NZ], b_z[i % NZ], 0, Xdst, slice(blk * 512 + t * 128, blk * 512 + (t + 1) * 128))
        loads(0)
        s1(0); loads(1); s1(1); s2(0)
        for i in range(NT):
            if i + 2 < NT:
                s1(i + 2)
            if i + 1 < NT:
                s2(i + 1)
            s3(i)
        A.release()
        P.barrier()

    def ffn(layer, Xsrc, Xdst):
        A.mark()
        NFA = 14
        Wd_a = A.alloc([NFA, D], BF16)
        A.mark()
        Wg = A.alloc([8, DFF], BF16); Wu = A.alloc([8, DFF], BF16)
        bl_Wg, bl_Wu = load_w_colblocks([(Wg, w_gate[layer]), (Wu, w_up[layer])], DFF, cb=256)
        b_Wd = []
        for f0 in range(0, NFA, 2):
            grp = [(Wd_a[:, f, :], w_down[layer][f * 128:(f + 1) * 128, :]) for f in (f0, f0 + 1)]
            b = Buf()
            P.op("poolq", (lambda grp: (lambda e: [e.dma_start(out=o, in_=i) for (o, i) in grp]))(grp), writes=[b], ndma=2)
            b_Wd += [b, b]
        xts = [A.alloc([4, D], F32)] * 2; b_xt = [Buf()] * 2
        xTs = [A.alloc([8, 512], BF16) for _ in range(2)]; b_xT = psbufs(2)
        sgs = [A.alloc([512], F32) for _ in range(2)]; b_sg = psbufs(2)
        hts = [A.alloc([512], BF16) for _ in range(4)]; b_ht = psbufs(4)
        pst = PS[0:2]; b_pst = pbufs(2); cnt = [0]
        b_g = pbufs(2); b_u = pbufs(2); fi = 0
        load_x(0, Xsrc, xts[0], b_xt[0])
        for blk in range(NB):
            xt, bx = xts[blk % 2], b_xt[blk % 2]; xT, bxT = xTs[blk % 2], b_xT[blk % 2]
            make_xT(xt, bx, xT, bxT, pst, b_pst, cnt)
            if blk + 1 < NB:
                load_x(blk + 1, Xsrc, xts[(blk + 1) % 2], b_xt[(blk + 1) % 2])
            for f in range(NF):
                j = fi % 2; hj = fi % 4; fi += 1
                pg, pu = PS[2 + j], PS[4 + j]
                for c in range(8):
                    P.op("pe", X.matmul(pg, lhsT=Wg[:, c, f * 128:(f + 1) * 128], rhs=xT[:, c, :], start=(c == 0), stop=(c == 7)), reads=[colbuf(bl_Wg, (f + 1) * 128), bxT], writes=[b_g[j]])
                for c in range(8):
                    P.op("pe", X.matmul(pu, lhsT=Wu[:, c, f * 128:(f + 1) * 128], rhs=xT[:, c, :], start=(c == 0), stop=(c == 7)), reads=[colbuf(bl_Wu, (f + 1) * 128), bxT], writes=[b_u[j]])
                P.op("act", X.activation(out=sgs[j], in_=pg, func=AF.Silu), reads=[b_g[j]], writes=[b_sg[j]])
                P.op("dve", X.tensor_tensor(out=hts[hj], in0=pu, in1=sgs[j], op=ALU.mult), reads=[b_u[j], b_sg[j]], writes=[b_ht[hj]])
                dma("sp", HT[f * 128:(f + 1) * 128, blk * 512:(blk + 1) * 512], hts[hj], reads=[b_ht[hj]])
        A.release()
        P.barrier()
        A.mark()
        Wd_b = A.alloc([NF - NFA, D], BF16)
        for f0 in range(NFA, NF, 2):
            grp = [(Wd_b[:, f - NFA, :], w_down[layer][f * 128:(f + 1) * 128, :]) for f in (f0, f0 + 1)]
            b = Buf()
            P.op("poolq", (lambda grp: (lambda e: [e.dma_start(out=o, in_=i) for (o, i) in grp]))(grp), writes=[b], ndma=2)
            b_Wd += [b, b]

        def Wd_of(f):
            return Wd_a[:, f] if f < NFA else Wd_b[:, f - NFA]
        load_ln_params(ln2_g, ln2_b, layer, 2)
        NZ = 5
        tmps = [(A.alloc([2, 6], F32), A.alloc([2], F32), A.alloc([2], F32), Buf()) for _ in range(NZ)]
        xts = [A.alloc([4, D], F32) for _ in range(2)]; b_xt = psbufs(2)
        hbs = [A.alloc([NF, 512], BF16) for _ in range(2)]; b_hb = psbufs(2)
        zs = [A.alloc([D], F32) for _ in range(NZ)]; b_z = psbufs(NZ)
        b_ps = pbufs(4); pi = [0]

        def loads(blk):
            dma("sp", hbs[blk % 2], HT[:, blk * 512:(blk + 1) * 512].rearrange("(f p) s -> p f s", p=128), writes=[b_hb[blk % 2]])
            dma("sp", xts[blk % 2], Xsrc[blk * 512:(blk + 1) * 512, :].rearrange("(t p) d -> p t d", p=128), writes=[b_xt[blk % 2]])

        def s1(i):
            blk, t = divmod(i, 4)
            if t == 1 and blk + 1 < NB:
                loads(blk + 1)
            xt, bx = xts[blk % 2], b_xt[blk % 2]; hb, bh = hbs[blk % 2], b_hb[blk % 2]
            z, bz = zs[i % NZ], b_z[i % NZ]
            for hh in range(2):
                pj = pi[0] % 4; pi[0] += 1
                for f in range(NF):
                    P.op("pe", X.matmul(PS[pj], lhsT=hb[:, f, t * 128:(t + 1) * 128], rhs=Wd_of(f)[:, hh * 512:(hh + 1) * 512], start=(f == 0), stop=(f == NF - 1)),
                         reads=[bh, b_Wd[f]], writes=[b_ps[pj]])
                P.op("dve", X.scalar_tensor_tensor(out=z[:, hh * 512:(hh + 1) * 512], in0=xt[:, t, hh * 512:(hh + 1) * 512], scalar=ALPHA, in1=PS[pj], op0=ALU.mult, op1=ALU.add),
                     reads=[bx, b_ps[pj]], writes=[bz])
            ln_stage1(z, bz, 2, tmps[i % NZ])

        def s2(i):
            ln_stage2(zs[i % NZ], b_z[i % NZ], 2, tmps[i % NZ])

        def s3(i):
            blk, t = divmod(i, 4)
            ln_stage3(zs[i % NZ], b_z[i % NZ], 2, Xdst, slice(blk * 512 + t * 128, blk * 512 + (t + 1) * 128))
        loads(0)
        s1(0); s1(1); s2(0)
        for i in range(NT):
            if i + 2 < NT:
                s1(i + 2)
            if i + 1 < NT:
                s2(i + 1)
            s3(i)
        A.release()
        P.barrier()
        A.release()

    def inproj_ab(Xsrc):
        A.mark()
        Win = A.alloc([8, 1440], BF16); b_Win = Buf()
        load_w_cast(Win, ab_w_in, 1440, b_Win)
        Wuq = A.alloc([3, 768], BF16); b_Wuq = Buf()
        load_w_cast(Wuq, ab_w_uq, 768, b_Wuq)
        Wukv = A.alloc([2, 1024], BF16); b_Wukv = Buf()
        load_w_cast(Wukv, ab_w_ukv, 1024, b_Wukv)
        Wkr_rot = A.alloc([8, 32], BF16); Wuq_rot = A.alloc([3, 8, 32], BF16); b_rot = Buf()
        P.op("act", X.mul(out=Wkr_rot[:, :, 0:16], in_=Win[:, :, 656:672], mul=-1.0), reads=[b_Win], writes=[b_rot])
        P.op("act", X.copy(out=Wkr_rot[:, :, 16:32], in_=Win[:, :, 640:656]), reads=[b_Win], writes=[b_rot])
        Wuq4 = Wuq.rearrange("p c (h e) -> p c h e", e=96)
        for c in range(3):
            P.op("act", X.mul(out=Wuq_rot[:, c, :, 0:16], in_=Wuq4[:, c, :, 80:96], mul=-1.0), reads=[b_Wuq], writes=[b_rot])
            P.op("act", X.copy(out=Wuq_rot[:, c, :, 16:32], in_=Wuq4[:, c, :, 64:80]), reads=[b_Wuq], writes=[b_rot])
        Wq_nope = A.alloc([3, 512], BF16); Wq_rope = A.alloc([3, 256], BF16); Wk_nope = A.alloc([2, 512], BF16); b_wc = Buf()
        Wukv4 = Wukv.rearrange("p c (h e) -> p c h e", e=128)
        for c in range(3):
            P.op("dve", X.tensor_copy(out=Wq_nope[:, c, :].rearrange("p (h e) -> p h e", e=64), in_=Wuq4[:, c, :, 0:64]), reads=[b_Wuq], writes=[b_wc])
            P.op("pool", X.tensor_copy(out=Wq_rope[:, c, :].rearrange("p (h e) -> p h e", e=32), in_=Wuq4[:, c, :, 64:96]), reads=[b_Wuq], writes=[b_wc])
        for c in range(2):
            P.op("dve", X.tensor_copy(out=Wk_nope[:, c, :].rearrange("p (h e) -> p h e", e=64), in_=Wukv4[:, c, :, 0:64]), reads=[b_Wukv], writes=[b_wc])
        Wq_rot = Wuq_rot.rearrange("p c h e -> p c (h e)")
        gn = A.alloc([5], F32); b_gn = Buf()
        dma("sp", gn[:, 0:3], ab_q_norm.rearrange("(c p) -> p c", p=128), writes=[b_gn])
        dma("sp", gn[:, 3:5], ab_kv_norm.rearrange("(c p) -> p c", p=128), writes=[b_gn])
        xts = [A.alloc([4, D], F32) for _ in range(2)]; b_xt = psbufs(2)
        xTs = [A.alloc([8, 512], BF16) for _ in range(2)]; b_xT = psbufs(2)
        cs = [A.alloc([2, 512], F32) for _ in range(2)]; b_cs = psbufs(2)
        c32 = A.alloc([5, 512], F32); b_c32 = Buf()
        sq = A.alloc([5, 512], BF16); b_sq = Buf()
        rbc = A.alloc([2, 512], F32); b_rbc = Buf()
        cn = A.alloc([5, 512], BF16); b_cn = Buf()
        NST = 6
        stg = [A.alloc([512], BF16) for _ in range(NST)]; b_stg = psbufs(NST)
        t1s = [A.alloc([512], F32) for _ in range(2)]; t2s = [A.alloc([512], F32) for _ in range(2)]; b_t = psbufs(2)
        vst = [A.alloc([4, 512], BF16) for _ in range(2)]; b_vst = psbufs(2)
        vss = [A.alloc([4, 128], BF16) for _ in range(2)]; b_vss = psbufs(2)
        pst = PS[0:2]; b_pst = pbufs(2); cnt = [0]
        b_ps = pbufs(6); pi = [0]; si = [0]; ti = [0]
        SC_MLA = 96.0 ** -0.5
        SC_SWA = 64.0 ** -0.5

        def nps():
            j = pi[0] % 6; pi[0] += 1
            return PS[2 + j], b_ps[j]

        def nstg():
            j = si[0] % NST; si[0] += 1
            return stg[j], b_stg[j]

        def fm_proj(ps, bps, M, wsl, W_b, rhs_of_c, nchunks, rhs_b):
            for c in range(nchunks):
                P.op("pe", X.matmul(ps[0:M, :], lhsT=wsl(c), rhs=rhs_of_c(c), start=(c == 0), stop=(c == nchunks - 1)), reads=[W_b, rhs_b], writes=[bps])

        def rope_combine(psA, bA, psB, bB, cst, bcs, scale, dst_dram, np_=32):
            j = ti[0] % 2; ti[0] += 1
            t1, t2, bt = t1s[j], t2s[j], b_t[j]
            P.op("dve", X.scalar_tensor_tensor(out=t1[0:np_, :], in0=psA[0:np_, :], scalar=float(scale), in1=cst[0:np_, 0, :], op0=ALU.mult, op1=ALU.mult), reads=[bA, bcs], writes=[bt])
            P.op("dve", X.scalar_tensor_tensor(out=t2[0:np_, :], in0=psB[0:np_, :], scalar=float(scale), in1=cst[0:np_, 1, :], op0=ALU.mult, op1=ALU.mult), reads=[bB, bcs], writes=[bt])
            s, bs = nstg()
            P.op("pool", X.tensor_tensor(out=s[0:np_, :], in0=t1[0:np_, :], in1=t2[0:np_, :], op=ALU.add), reads=[bt], writes=[bs])
            return s, bs

        for blk in range(NB):
            cols = slice(blk * 512, (blk + 1) * 512)
            xt, bx = xts[blk % 2], b_xt[blk % 2]; xT, bxT = xTs[blk % 2], b_xT[blk % 2]
            cst, bcs = cs[blk % 2], b_cs[blk % 2]
            if blk == 0:
                load_x(0, Xsrc, xts[0], b_xt[0])
            if blk + 1 < NB:
                load_x(blk + 1, Xsrc, xts[(blk + 1) % 2], b_xt[(blk + 1) % 2])
            make_xT(xt, bx, xT, bxT, pst, b_pst, cnt)
            P.op("sp", (lambda cst, cols: (lambda e: [e.dma_start(out=cst[32 * r:32 * r + 32], in_=rope_cs[:, :, cols].rearrange("a p s -> p a s")) for r in range(4)]))(cst, cols), writes=[bcs], ndma=4)
            if dbg and blk == 0:
                dxt = dsc("DBG_xT", [128, 8, 512], BF16); dwin = dsc("DBG_Win", [128, 8, 1440], BF16)
                dma("sp", dxt, xT, reads=[bxT]); dma("sp", dwin, Win, reads=[b_Win])
            if DBGSTOP == 2: continue
            for i in range(5):
                ps, bps = nps()
                fm_proj(ps, bps, 128, lambda c, i=i: Win[:, c, i * 128:(i + 1) * 128], b_Win, lambda c: xT[:, c, :], 8, bxT)
                P.op("dve", X.tensor_copy(out=c32[:, i, :], in_=ps), reads=[bps], writes=[b_c32])
                P.op("act", X.activation(out=sq[:, i, :], in_=ps, func=AF.Square), reads=[bps], writes=[b_sq])
            for (k, i0, n, R_) in ((0, 0, 3, 384.0), (1, 3, 2, 256.0)):
                ps, bps = nps()
                for i in range(n):
                    P.op("pe", X.matmul(ps, lhsT=ones_bf, rhs=sq[:, i0 + i, :], start=(i == 0), stop=(i == n - 1)), reads=[b_ones, b_sq], writes=[bps])
                P.op("act", X.activation(out=rbc[:, k, :], in_=ps, func=AF.Ln, scale=1.0 / R_, bias=eps_rms[:, 0:1]), reads=[bps, b_eps], writes=[b_rbc])
                P.op("act", X.activation(out=rbc[:, k, :], in_=rbc[:, k, :], func=AF.Exp, scale=-0.5), reads=[b_rbc], writes=[b_rbc])
                for i in range(n):
                    P.op("dve", X.scalar_tensor_tensor(out=cn[:, i0 + i, :], in0=c32[:, i0 + i, :], scalar=gn[:, i0 + i:i0 + i + 1], in1=rbc[:, k, :], op0=ALU.mult, op1=ALU.mult),
                         reads=[b_c32, b_gn, b_rbc], writes=[b_cn])
            if DBGSTOP == 3: continue
            psA, bA = nps(); psB, bB = nps()
            fm_proj(psA, bA, 32, lambda c: Win[:, c, 640:672], b_Win, lambda c: xT[:, c, :], 8, bxT)
            fm_proj(psB, bB, 32, lambda c: Wkr_rot[:, c, :], b_rot, lambda c: xT[:, c, :], 8, bxT)
            s, bs = rope_combine(psA, bA, psB, bB, cst, bcs, 1.0, None)
            for h in range(8):
                dma("sp", KTa[h, 0:32, cols], s[0:32, :], reads=[bs])
            if DBGSTOP == 4: continue
            for i in range(4):
                ps, bps = nps()
                fm_proj(ps, bps, 128, lambda c, i=i: Win[:, c, 672 + i * 128:672 + (i + 1) * 128], b_Win, lambda c: xT[:, c, :], 8, bxT)
                s, bs = nstg()
                evac("act" if i % 2 == 0 else "dve", s, ps, [bps], [bs], scale=SC_SWA)
                dma("sp", QTs[2 * i:2 * i + 2, :, cols].rearrange("h p s -> (h p) s"), s, reads=[bs])
            ps, bps = nps()
            fm_proj(ps, bps, 128, lambda c: Win[:, c, 1184:1312], b_Win, lambda c: xT[:, c, :], 8, bxT)
            s, bs = nstg()
            evac("act", s, ps, [bps], [bs])
            dma("sp", KTs[:, :, cols].rearrange("h p s -> (h p) s"), s, reads=[bs])
            ps, bps = nps()
            for t in range(4):
                for c in range(8):
                    P.op("pe", X.matmul(ps[:, t * 128:(t + 1) * 128], lhsT=xT[:, c, t * 128:(t + 1) * 128], rhs=Win[:, c, 1312:1440], start=(c == 0), stop=(c == 7)), reads=[bxT, b_Win], writes=[bps])
            vs_, bvs = vss[blk % 2], b_vss[blk % 2]
            evac("dve", vs_, ps.rearrange("p (t e) -> p t e", e=128), [bps], [bvs])
            dma("sp", Vs[cols].rearrange("(t p) g e -> p t (g e)", p=128), vs_, reads=[bvs])
            if DBGSTOP == 5: continue
            Wuq4_ = Wuq.rearrange("p c (h e) -> p c h e", e=96)
            for g in range(4):
                ps, bps = nps()
                fm_proj(ps, bps, 128, lambda c, g=g: Wq_nope[:, c, 128 * g:128 * g + 128], b_wc, lambda c: cn[:, c, :], 3, b_cn)
                s, bs = nstg()
                evac("act", s, ps, [bps], [bs], scale=SC_MLA)
                dma("sp", QTa[2 * g, 32:96, cols], s[0:64, :], reads=[bs])
                dma("sp", QTa[2 * g + 1, 32:96, cols], s[64:128, :], reads=[bs])
            for g in range(2):
                psA, bA = nps(); psB, bB = nps()
                fm_proj(psA, bA, 128, lambda c, g=g: Wq_rope[:, c, 128 * g:128 * g + 128], b_wc, lambda c: cn[:, c, :], 3, b_cn)
                fm_proj(psB, bB, 128, lambda c, g=g: Wq_rot[:, c, 128 * g:128 * g + 128], b_rot, lambda c: cn[:, c, :], 3, b_cn)
                s, bs = rope_combine(psA, bA, psB, bB, cst, bcs, SC_MLA, None, np_=128)
                for r in range(4):
                    dma("sp", QTa[4 * g + r, 0:32, cols], s[32 * r:32 * r + 32, :], reads=[bs])
            if DBGSTOP == 6: continue
            Wukv4_ = Wukv.rearrange("p c (h e) -> p c h e", e=128)
            for g in range(4):
                ps, bps = nps()
                fm_proj(ps, bps, 128, lambda c, g=g: Wk_nope[:, c, 128 * g:128 * g + 128], b_wc, lambda c: cn[:, 3 + c, :], 2, b_cn)
                s, bs = nstg()
                evac("dve" if g % 2 == 0 else "act", s, ps, [bps], [bs])
                dma("sp", KTa[2 * g, 32:96, cols], s[0:64, :], reads=[bs])
                dma("sp", KTa[2 * g + 1, 32:96, cols], s[64:128, :], reads=[bs])
            if DBGSTOP == 7: continue
            vt, bvt = vst[blk % 2], b_vst[blk % 2]
            for t in range(4):
                ps, bps = nps()
                for c in range(2):
                    P.op("pe", X.matmul(ps.rearrange("p (h e) -> p h e", e=64), lhsT=cn[:, 3 + c, t * 128:(t + 1) * 128], rhs=Wukv[:, c, :].rearrange("p (h e) -> p h e", e=128)[:, :, 64:128], start=(c == 0), stop=(c == 1)),
                         reads=[b_cn, b_Wukv], writes=[bps])
                evac("act" if t % 2 == 0 else "dve", vt[:, t, :], ps, [bps], [bvt])
            dma("sp", Va[cols].rearrange("(t p) h e -> p t (h e)", p=128), vt, reads=[bvt])
        A.release()
        P.barrier()

    def inproj_cd(Xsrc):
        A.mark()
        Win = A.alloc([8, 3080], BF16)
        bl_Win = load_w_colblocks([(Win, cd_w_in)], 3080, cb=440)[0]

        def b_Win_of(c_lo, c_hi):
            bs = []
            for (c1, b) in bl_Win:
                if c1 > c_lo and c1 - 440 < c_hi:
                    bs.append(b)
            return bs
        nb = A.alloc([1], F32); b_nb = Buf()
        dma("sp", nb[0:8, :], cd_b_forget.rearrange("(h o) -> h o", o=1), writes=[b_nb])
        P.op("dve", X.tensor_scalar(out=nb[0:8, :], in0=nb[0:8, :], scalar1=-1.0, scalar2=None, op0=ALU.mult), reads=[b_nb], writes=[b_nb])
        ones32 = A.alloc([512], F32); b_o32 = Buf()
        P.op("pool", X.memset(ones32, 1.0), writes=[b_o32])
        onesS = A.alloc([3, 512], BF16); b_oS = Buf()
        P.op("pool", X.memset(onesS[0:8], 1.0), writes=[b_oS])
        cum = A.alloc([S], F32); b_cum = Buf()
        xts = [A.alloc([4, D], F32)] * 2; b_xt = [Buf()] * 2
        xTs = [A.alloc([8, 512], BF16) for _ in range(2)]; b_xT = psbufs(2)
        NST = 6
        stg = [A.alloc([512], BF16) for _ in range(NST)]; b_stg = psbufs(NST)
        vst = [A.alloc([4, 512], BF16) for _ in range(2)] * 2; b_vst = psbufs(2) * 2
        ls = A.alloc([512], F32); b_ls = Buf()
        r1 = A.alloc([512], F32); r2 = A.alloc([512], F32); b_r = Buf()
        augk = [A.alloc([3, 512], BF16) for _ in range(2)]; augq = [A.alloc([3, 512], BF16) for _ in range(2)]; b_aug = psbufs(2)
        pst = PS[0:2]; b_pst = pbufs(2); cnt = [0]
        b_ps = pbufs(6); pi = [0]; si = [0]; vi = [0]
        SC = 64.0 ** -0.5

        def nps():
            j = pi[0] % 6; pi[0] += 1
            return PS[2 + j], b_ps[j]

        def nstg():
            j = si[0] % NST; si[0] += 1
            return stg[j], b_stg[j]

        for blk in range(NB):
            cols = slice(blk * 512, (blk + 1) * 512)
            xt, bx = xts[blk % 2], b_xt[blk % 2]; xT, bxT = xTs[blk % 2], b_xT[blk % 2]
            if blk == 0:
                load_x(0, Xsrc, xt, bx)
            make_xT(xt, bx, xT, bxT, pst, b_pst, cnt)
            if blk + 1 < NB:
                load_x(blk + 1, Xsrc, xt, bx)
            dma("sp", QTf[:, 67:70, blk * 512:(blk + 1) * 512], onesS[0:8], reads=[b_oS])
            dma("sp", KTf[:, 64:67, blk * 512:(blk + 1) * 512], onesS[0:8], reads=[b_oS])
            for (c0, dst, scale) in ((0, QTf, SC), (512, KTf, None), (1544, QTc, SC), (2056, KTc, None)):
                for i in range(4):
                    ps, bps = nps()
                    for c in range(8):
                        P.op("pe", X.matmul(ps, lhsT=Win[:, c, c0 + i * 128:c0 + (i + 1) * 128], rhs=xT[:, c, :], start=(c == 0), stop=(c == 7)), reads=b_Win_of(c0 + i * 128, c0 + (i + 1) * 128) + [bxT], writes=[bps])
                    s, bs = nstg()
                    evac("act" if i % 2 == 0 else "dve", s, ps, [bps], [bs], scale=scale)
                    dma("sp", dst[2 * i, 0:64, cols], s[0:64, :], reads=[bs])
                    dma("sp", dst[2 * i + 1, 0:64, cols], s[64:128, :], reads=[bs])
            for (c0, dst) in ((1024, Vf), (2568, Vc)):
                vt, bvt = vst[vi[0] % 4], b_vst[vi[0] % 4]; vi[0] += 1
                for t in range(4):
                    ps, bps = nps()
                    for c in range(8):
                        P.op("pe", X.matmul(ps, lhsT=xT[:, c, t * 128:(t + 1) * 128], rhs=Win[:, c, c0:c0 + 512], start=(c == 0), stop=(c == 7)), reads=[bxT] + b_Win_of(c0, c0 + 512), writes=[bps])
                    evac("act" if t % 2 == 0 else "dve", vt[:, t, :], ps, [bps], [bvt])
                dma("sp", dst[cols].rearrange("(t p) h e -> p t (h e)", p=128), vt, reads=[bvt])
            ps, bps = nps()
            for c in range(8):
                P.op("pe", X.matmul(ps[0:8, :], lhsT=Win[:, c, 1536:1544], rhs=xT[:, c, :], start=(c == 0), stop=(c == 7)), reads=b_Win_of(1536, 1544) + [bxT], writes=[bps])
            P.op("act", X.activation(out=ls[0:8, :], in_=ps[0:8, :], func=AF.Exp, scale=-1.0, bias=nb[0:8, 0:1]), reads=[bps, b_nb], writes=[b_ls])
            P.op("act", X.activation(out=ls[0:8, :], in_=ls[0:8, :], func=AF.Ln, bias=1.0), reads=[b_ls], writes=[b_ls])
            init = 0.0 if blk == 0 else cum[0:8, blk * 512 - 1:blk * 512]
            P.op("dve", X.tensor_tensor_scan(out=cum[0:8, cols], data0=ones32[0:8, :], data1=ls[0:8, :], initial=init, op0=ALU.mult, op1=ALU.add), reads=[b_ls, b_o32, b_cum], writes=[b_cum])
            ak, aq, ba = augk[blk % 2], augq[blk % 2], b_aug[blk % 2]
            P.op("dve", X.tensor_copy(out=ak[0:8, 0, :], in_=cum[0:8, cols]), reads=[b_cum], writes=[ba])
            P.op("dve", X.tensor_tensor(out=r1[0:8, :], in0=cum[0:8, cols], in1=ak[0:8, 0, :], op=ALU.subtract), reads=[b_cum, ba], writes=[b_r])
            P.op("dve", X.tensor_copy(out=ak[0:8, 1, :], in_=r1[0:8, :]), reads=[b_r], writes=[ba])
            P.op("dve", X.tensor_tensor(out=r2[0:8, :], in0=r1[0:8, :], in1=ak[0:8, 1, :], op=ALU.subtract), reads=[b_r, ba], writes=[b_r])
            P.op("dve", X.tensor_copy(out=ak[0:8, 2, :], in_=r2[0:8, :]), reads=[b_r], writes=[ba])
            P.op("dve", X.tensor_scalar(out=aq[0:8], in0=ak[0:8], scalar1=-1.0, scalar2=None, op0=ALU.mult), reads=[ba], writes=[ba])
            dma("sp", KTf[:, 67:70, cols], ak[0:8], reads=[ba])
            dma("sp", QTf[:, 64:67, cols], aq[0:8], reads=[ba])
        A.release()
        P.barrier()

    def prep_bias(shape, load_fn):
        bf = A.alloc(shape, BF16); b = Buf()
        f = A.alloc(shape, F32)
        load_fn(f, b)
        P.op("dve", X.tensor_copy(out=bf, in_=f), reads=[b], writes=[b])
        return bf, b

    inproj_ab(x_in)
    A.mark()
    Wo = A.alloc([8, D], BF16); b_Wo = Buf()
    abf, b_al = prep_bias([8, 2, 128], lambda f, b: dma("poolq", f, alibi_d, writes=[b]))
    sk = A.alloc([8], F32); b_sk = Buf()
    dma("poolq", sk, ab_sinks.rearrange("(o h) -> o h", o=1).broadcast_to([128, 8]), writes=[b_sk])
    P.op("act", X.activation(out=sk, in_=sk, func=AF.Exp), reads=[b_sk], writes=[b_sk])
    load_w_cast(Wo, ab_w_out, D, b_Wo)
    attention_full(QTa, KTa, Va, 96, mla_mask_d, 0)
    attention_band(QTs, KTs, Vs, 2, 1, abf, b_al, 512, sink=sk, b_sink=b_sk)
    outproj_ln(Wo, b_Wo, x_in, X1, 0)
    A.release()
    ffn(0, X1, X2)
    inproj_cd(X2)
    A.mark()
    Wo = A.alloc([8, D], BF16); b_Wo = Buf()

    def load_ck(f, b):
        cm = A.alloc([5, 128], F32)
        dma("poolq", f, ck_bias_d, writes=[b])
        dma("poolq", cm, ck_mask_d, writes=[b])
        for h in range(8):
            P.op("pool", X.tensor_tensor(out=f[:, h], in0=f[:, h], in1=cm, op=ALU.add), reads=[b], writes=[b])
    cbf, b_cb = prep_bias([8, 5, 128], load_ck)
    load_w_cast(Wo, cd_w_out, D, b_Wo)
    attention_full(QTf, KTf, Vf, 70, causal_mask_d, 0)
    attention_band(QTc, KTc, Vc, 8, 4, cbf, b_cb, 512)
    outproj_ln(Wo, b_Wo, X2, X3, 1)
    A.release()
    ffn(1, X3, y_out)
    P.emit()
    es.close()
    return nc


_CACHE = {}


def kernel(**inputs):
    import ml_dtypes
    hc = host_constants()
    idx = hc.pop("_ck_idx")
    rel = np.asarray(inputs["cd_rel_bias"], np.float32)[0]
    hc["ck_bias"] = np.ascontiguousarray(np.transpose(rel[idx], (0, 3, 1, 2)))
    if "nc" not in _CACHE:
        _CACHE["nc"] = build_nc()
    nc = _CACHE["nc"]
    x = np.asarray(inputs["x"], np.float32)
    shared = {
        "ab_w_in": inputs["ab_w_in"][0], "ab_q_norm": inputs["ab_q_norm"][0], "ab_w_uq": inputs["ab_w_uq"][0],
        "ab_kv_norm": inputs["ab_kv_norm"][0], "ab_w_ukv": inputs["ab_w_ukv"][0], "ab_sinks": inputs["ab_sinks"][0],
        "ab_w_out": inputs["ab_w_out"][0], "cd_w_in": inputs["cd_w_in"][0], "cd_b_forget": inputs["cd_b_forget"][0],
        "cd_w_out": inputs["cd_w_out"][0],
        "ln1_g": inputs["ln1_g"], "ln1_b": inputs["ln1_b"], "ln2_g": inputs["ln2_g"], "ln2_b": inputs["ln2_b"],
        "ffn_w_gate": inputs["ffn_w_gate"], "ffn_w_up": inputs["ffn_w_up"], "ffn_w_down": inputs["ffn_w_down"],
    }
    shared = {k: np.ascontiguousarray(np.asarray(v, np.float32)) for k, v in shared.items()}
    shared.update(hc)
    in_maps = [dict(shared, x=np.ascontiguousarray(x[b])) for b in range(8)]
    res = run_bass_kernel_spmd(nc, in_maps, core_ids=list(range(8)))
    return np.stack([np.asarray(r["y"], np.float32) for r in res.results], axis=0)
```
